# Optimizing a Trainium2 kernel written in Bass

```python
import math
import jax, jax.numpy as jnp
from jax import lax
import numpy as np

D_MODEL = 1024
BATCH = 8
SEQ = 2048
DEPTH = 4

HEAD_DIM = 64
W_BR = D_MODEL // 2
N_BRANCH = 3
H_A = W_BR // HEAD_DIM
MOBA_BLOCK = 256
MOBA_TOPK = 3
MOBA_QCHUNK = 32
DIL_PAIRS = ((128, 1), (512, 4), (2048, 16))
N_DIL = len(DIL_PAIRS)
H_B = W_BR // HEAD_DIM
DIL_QCHUNK = 64
H_C = W_BR // HEAD_DIM
KV_C = 2
SWA_WINDOW = 128
N_BUCKETS = 32
REL_MAX_DIST = 2048
H_TOT = H_A + N_DIL * H_B + H_C
OFF_A = 0
OFF_B = H_A
OFF_C = H_A + N_DIL * H_B

EPS = 1e-6
NEG_INF = -1e30

SPLIT_SIZES = ([W_BR] * 4
               + [N_DIL * W_BR] * 3 + [W_BR]
               + [W_BR, KV_C * HEAD_DIM, KV_C * HEAD_DIM, W_BR]
               + [N_BRANCH * D_MODEL])
C_IN = int(sum(SPLIT_SIZES))
SPLIT_POINTS = [int(v) for v in np.cumsum(SPLIT_SIZES)[:-1]]

kernel_name = "hybrid_moba_dilated_swa_gated_trunk"


def rms_norm(x, g):
    x32 = x.astype(jnp.float32)
    y = x32 * lax.rsqrt(jnp.mean(x32 * x32, axis=-1, keepdims=True) + EPS)
    return (y * g.astype(jnp.float32)).astype(x.dtype)


def rel_bucket(dist):
    dist = jnp.maximum(dist, 0)
    max_exact = N_BUCKETS // 2
    log_ratio = jnp.log(jnp.maximum(dist, 1).astype(jnp.float32) / max_exact) / math.log(REL_MAX_DIST / max_exact)
    large = max_exact + (log_ratio * (N_BUCKETS - max_exact)).astype(jnp.int32)
    large = jnp.minimum(large, N_BUCKETS - 1)
    return jnp.where(dist < max_exact, dist, large)


def moba_attention(q, k, v, bias_tab):
    b, s, h, dh = q.shape
    s_pad = -(-s // MOBA_BLOCK) * MOBA_BLOCK
    pad = ((0, 0), (0, s_pad - s), (0, 0), (0, 0))
    qh, kh, vh = [jnp.pad(t, pad).transpose(0, 2, 1, 3) for t in (q, k, v)]
    nb = s_pad // MOBA_BLOCK
    kb = kh.reshape(b, h, nb, MOBA_BLOCK, dh)
    vb = vh.reshape(b, h, nb, MOBA_BLOCK, dh)
    k_mean = jnp.mean(kb.astype(jnp.float32), axis=3)
    gate = jnp.einsum('bhsd,bhnd->bhsn', qh.astype(jnp.float32), k_mean)
    q_blk = jnp.arange(s_pad) // MOBA_BLOCK
    past = jnp.arange(nb)[None, :] < q_blk[:, None]
    gate = jnp.where(past, gate, NEG_INF)
    n_sel = min(MOBA_TOPK, max(nb - 1, 1))
    _, top_idx = lax.top_k(gate, n_sel)
    sel_valid = jnp.arange(n_sel)[None, :] < jnp.minimum(q_blk, MOBA_TOPK)[:, None]
    scale = dh ** -0.5
    bias_tab = bias_tab.astype(jnp.float32)
    b_idx = jnp.arange(b)[:, None, None, None]
    h_idx = jnp.arange(h)[None, :, None, None]
    h_idx5 = jnp.arange(h)[None, :, None, None, None]
    blk_off = jnp.arange(MOBA_BLOCK)

    def chunk(c):
        t0 = c * MOBA_QCHUNK
        t = t0 + jnp.arange(MOBA_QCHUNK)
        q_c = lax.dynamic_slice_in_dim(qh, t0, MOBA_QCHUNK, axis=2)
        idx_c = lax.dynamic_slice_in_dim(top_idx, t0, MOBA_QCHUNK, axis=2)
        valid_c = lax.dynamic_slice_in_dim(sel_valid, t0, MOBA_QCHUNK, axis=0)
        k_sel = kb[b_idx, h_idx, idx_c]
        v_sel = vb[b_idx, h_idx, idx_c]
        s_sel = jnp.einsum('bhqd,bhqjkd->bhqjk', q_c, k_sel,
                           preferred_element_type=jnp.float32) * scale
        key_pos = idx_c[..., None] * MOBA_BLOCK + blk_off
        dist = t[:, None, None] - key_pos
        bias_sel = bias_tab[h_idx5, rel_bucket(dist)]
        s_sel = jnp.where(valid_c[:, :, None], s_sel + bias_sel, NEG_INF)
        n_own = t0 // MOBA_BLOCK
        k_own = lax.dynamic_index_in_dim(kb, n_own, axis=2, keepdims=False)
        v_own = lax.dynamic_index_in_dim(vb, n_own, axis=2, keepdims=False)
        s_own = jnp.einsum('bhqd,bhkd->bhqk', q_c, k_own,
                           preferred_element_type=jnp.float32) * scale
        dist_own = t[:, None] - (n_own * MOBA_BLOCK + blk_off)[None, :]
        s_own = jnp.where(dist_own >= 0, s_own + bias_tab[:, rel_bucket(dist_own)], NEG_INF)
        n_s = n_sel * MOBA_BLOCK
        logits = jnp.concatenate([s_sel.reshape(b, h, MOBA_QCHUNK, n_s), s_own], axis=-1)
        p = jax.nn.softmax(logits, axis=-1)
        p_sel = p[..., :n_s].reshape(b, h, MOBA_QCHUNK, n_sel, MOBA_BLOCK).astype(v.dtype)
        p_own = p[..., n_s:].astype(v.dtype)
        out = (jnp.einsum('bhqjk,bhqjkd->bhqd', p_sel, v_sel, preferred_element_type=jnp.float32)
               + jnp.einsum('bhqk,bhkd->bhqd', p_own, v_own, preferred_element_type=jnp.float32))
        return out.astype(v.dtype)

    out = lax.map(chunk, jnp.arange(s_pad // MOBA_QCHUNK))
    out = out.transpose(1, 0, 3, 2, 4).reshape(b, s_pad, h, dh)
    return out[:, :s]


def dilated_attention(q, k, v, bias_tab):
    b, s, _, h, dh = q.shape
    scale = dh ** -0.5
    bias_tab = bias_tab.astype(jnp.float32)
    n_keys = [w // d + 1 for (w, d) in DIL_PAIRS]
    k_pads = [jnp.pad(k[:, :, gi], ((0, 0), (w, 0), (0, 0), (0, 0))) for gi, (w, d) in enumerate(DIL_PAIRS)]
    v_pads = [jnp.pad(v[:, :, gi], ((0, 0), (w, 0), (0, 0), (0, 0))) for gi, (w, d) in enumerate(DIL_PAIRS)]
    bias_g = [bias_tab[gi * h:(gi + 1) * h][:, rel_bucket(d * jnp.arange(n_keys[gi]))]
              for gi, (w, d) in enumerate(DIL_PAIRS)]

    def chunk(c):
        t0 = c * DIL_QCHUNK
        t = t0 + jnp.arange(DIL_QCHUNK)
        q_c = lax.dynamic_slice_in_dim(q, t0, DIL_QCHUNK, axis=1)
        outs, lses = [], []
        for gi, (w, d) in enumerate(DIL_PAIRS):
            steps = jnp.arange(n_keys[gi]) * d
            idx = t[:, None] + w - steps[None, :]
            k_g = k_pads[gi][:, idx]
            v_g = v_pads[gi][:, idx]
            logits = jnp.einsum('bqhd,bqjhd->bhqj', q_c[:, :, gi], k_g,
                                preferred_element_type=jnp.float32) * scale + bias_g[gi][:, None, :]
            logits = jnp.where((t[:, None] - steps[None, :]) >= 0, logits, NEG_INF)
            lse = jax.nn.logsumexp(logits, axis=-1)
            p = jnp.exp(logits - lse[..., None]).astype(v.dtype)
            outs.append(jnp.einsum('bhqj,bqjhd->bqhd', p, v_g, preferred_element_type=jnp.float32))
            lses.append(lse)
        wts = jax.nn.softmax(jnp.stack(lses, axis=0), axis=0)
        out = sum(wts[gi].transpose(0, 2, 1)[..., None] * outs[gi] for gi in range(N_DIL))
        return out.astype(v.dtype)

    out = lax.map(chunk, jnp.arange(s // DIL_QCHUNK))
    return out.transpose(1, 0, 2, 3, 4).reshape(b, s, h, dh)


def swa_sink_attention(q, k, v, sinks, bias_tab):
    b, s, h, dh = q.shape
    kvh = k.shape[2]
    rep = h // kvh
    W = SWA_WINDOW
    nb = s // W
    scale = dh ** -0.5
    qb = q.reshape(b, nb, W, kvh, rep, dh)
    kb = k.reshape(b, nb, W, kvh, dh)
    vb = v.reshape(b, nb, W, kvh, dh)

    def with_prev(t):
        prev = jnp.pad(t, ((0, 0), (1, 0), (0, 0), (0, 0), (0, 0)))[:, :-1]
        return jnp.concatenate([prev, t], axis=2)

    kk, vv = with_prev(kb), with_prev(vb)
    logits = jnp.einsum('bnqgrd,bnkgd->bngrqk', qb, kk,
                        preferred_element_type=jnp.float32) * scale
    i = jnp.arange(W)
    j = jnp.arange(2 * W)
    dist = i[:, None] + W - j[None, :]
    key_pos = jnp.arange(nb)[:, None] * W - W + j[None, :]
    valid = ((dist >= 0) & (dist < W))[None] & (key_pos >= 0)[:, None, :]
    bias = bias_tab.astype(jnp.float32)[:, rel_bucket(dist)].reshape(kvh, rep, W, 2 * W)
    logits = jnp.where(valid[None, :, None, None], logits + bias, NEG_INF)
    sink = sinks.astype(jnp.float32).reshape(kvh, rep)[None, None, :, :, None, None]
    m = jnp.maximum(jnp.max(logits, axis=-1, keepdims=True), sink)
    e = jnp.exp(logits - m)
    p = e / (jnp.sum(e, axis=-1, keepdims=True) + jnp.exp(sink - m))
    out = jnp.einsum('bngrqk,bnkgd->bnqgrd', p.astype(v.dtype), vv, preferred_element_type=jnp.float32)
    return out.reshape(b, s, h, dh).astype(v.dtype)


def setup_inputs(seed: int = 0) -> dict:
    key = jax.random.key(seed)
    ks = jax.random.split(key, 8)
    x = jax.random.normal(ks[0], (BATCH, SEQ, D_MODEL), jnp.float32)
    ln_g = 1.0 + 0.02 * jax.random.normal(ks[1], (DEPTH, D_MODEL), jnp.float32)
    w_in = jax.random.normal(ks[2], (DEPTH, D_MODEL, C_IN), jnp.float32) * D_MODEL ** -0.5
    qk_g = 1.0 + 0.02 * jax.random.normal(ks[3], (DEPTH, 6, HEAD_DIM), jnp.float32)
    sinks = 0.5 * jax.random.normal(ks[4], (DEPTH, H_C), jnp.float32)
    w_branch = jax.random.normal(ks[5], (DEPTH, N_BRANCH, W_BR, D_MODEL), jnp.float32) * W_BR ** -0.5
    w_out = jax.random.normal(ks[6], (DEPTH, D_MODEL, D_MODEL), jnp.float32) * D_MODEL ** -0.5
    rel_bias = 0.2 * jax.random.normal(ks[7], (H_TOT, N_BUCKETS), jnp.float32)
    return {"x": x, "ln_g": ln_g, "w_in": w_in, "qk_g": qk_g, "sinks": sinks,
            "w_branch": w_branch, "w_out": w_out, "rel_bias": rel_bias}


def reference(x, ln_g, w_in, qk_g, sinks, w_branch, w_out, rel_bias):
    b, s, _ = x.shape
    for l in range(DEPTH):
        hn = rms_norm(x, ln_g[l])
        proj = jnp.einsum('bsd,dc->bsc', hn, w_in[l])
        (qa, ka, va, ga, qb, kb, vb, gb, qc, kc, vc, gc, gate_logits) = jnp.split(proj, SPLIT_POINTS, axis=-1)
        g = qk_g[l]
        qa = rms_norm(qa.reshape(b, s, H_A, HEAD_DIM), g[0])
        ka = rms_norm(ka.reshape(b, s, H_A, HEAD_DIM), g[1])
        o_a = moba_attention(qa, ka, va.reshape(b, s, H_A, HEAD_DIM), rel_bias[OFF_A:OFF_A + H_A])
        qb = rms_norm(qb.reshape(b, s, N_DIL, H_B, HEAD_DIM), g[2])
        kb = rms_norm(kb.reshape(b, s, N_DIL, H_B, HEAD_DIM), g[3])
        o_b = dilated_attention(qb, kb, vb.reshape(b, s, N_DIL, H_B, HEAD_DIM),
                                rel_bias[OFF_B:OFF_B + N_DIL * H_B])
        qc = rms_norm(qc.reshape(b, s, H_C, HEAD_DIM), g[4])
        kc = rms_norm(kc.reshape(b, s, KV_C, HEAD_DIM), g[5])
        o_c = swa_sink_attention(qc, kc, vc.reshape(b, s, KV_C, HEAD_DIM), sinks[l],
                                 rel_bias[OFF_C:OFF_C + H_C])
        branches = jnp.stack([o_a.reshape(b, s, W_BR) * jax.nn.silu(ga),
                              o_b.reshape(b, s, W_BR) * jax.nn.silu(gb),
                              o_c.reshape(b, s, W_BR) * jax.nn.silu(gc)], axis=2)
        br = jnp.einsum('bsiw,iwd->bsid', branches, w_branch[l])
        gates = jax.nn.sigmoid(gate_logits.reshape(b, s, N_BRANCH, D_MODEL))
        merged = jnp.sum(gates * br, axis=2)
        x = x + jnp.einsum('bsd,de->bse', merged, w_out[l])
    return x
```

```python
import math
from contextlib import ExitStack

import numpy as np
import ml_dtypes
import concourse.bass as bass
import concourse.mybir as mybir
from concourse.bass_utils import run_bass_kernel_spmd

F32 = mybir.dt.float32
BF16 = mybir.dt.bfloat16
AF = mybir.ActivationFunctionType
ALU = mybir.AluOpType
AX = mybir.AxisListType

S_TOK = 2048
D = 1024
NT = 16
C_IN = 11520
NEG = -30000.0
EPS = 1e-6
A_Q, A_K, A_V, A_G = 0, 512, 1024, 1536
B_Q, B_K, B_V, B_G = 2048, 3584, 5120, 6656
C_Q, C_K, C_V, C_G = 7168, 7680, 7808, 7936
GATE0 = 8448
DIL = (1, 4, 16)
TA_W = 2176


class Res:
    __slots__ = ("name", "w", "r")

    def __init__(self, name=""):
        self.name = name
        self.w = None
        self.r = {}


class DmaSlot:
    __slots__ = ("key", "total")

    def __init__(self, key):
        self.key = key
        self.total = 0


class Sched:
    ENGS = ("pe", "act", "dve", "pool", "sp")

    def __init__(self):
        self.prog = {e: [] for e in self.ENGS}
        self.count = {e: 0 for e in self.ENGS}
        self.seen = {e: {} for e in self.ENGS}
        self.slots = []

    def new_slot(self):
        s = DmaSlot("d%03d" % len(self.slots))
        self.slots.append(s)
        return s

    def _deps(self, eng, reads, writes):
        waits = {}
        seen = self.seen[eng]

        def add(tok):
            key, val = tok
            if key == eng and eng in ("pe", "sp"):
                return
            if seen.get(key, 0) >= val:
                return
            if waits.get(key, 0) < val:
                waits[key] = val

        for r in reads:
            if r.w is not None:
                add(r.w)
        for w in writes:
            if w.w is not None:
                add(w.w)
            for k, v in w.r.items():
                add((k, v))
        for k, v in waits.items():
            seen[k] = v
        return list(waits.items())

    def _mark(self, tok, reads, writes):
        for r in reads:
            if r.r.get(tok[0], 0) < tok[1]:
                r.r[tok[0]] = tok[1]
        for w in writes:
            w.w = tok
            w.r = {}

    def op(self, eng, fn, reads=(), writes=()):
        waits = self._deps(eng, reads, writes)
        self.count[eng] += 1
        tok = (eng, self.count[eng])
        self.prog[eng].append((fn, waits, (eng, 1)))
        self._mark(tok, reads, writes)
        return tok

    def dma(self, q, fn, slot, reads=(), writes=()):
        waits = self._deps(q, reads, writes)
        slot.total += 16
        tok = (slot.key, slot.total)
        self.prog[q].append((fn, waits, (slot.key, 16)))
        self._mark(tok, reads, writes)
        return tok

    def barrier(self):
        toks = [(e, self.count[e]) for e in self.ENGS if self.count[e] > 0]
        toks += [(s.key, s.total) for s in self.slots if s.total > 0]
        for e in self.ENGS:
            waits = {}
            for key, val in toks:
                if key == e and e in ("pe", "sp"):
                    continue
                if self.seen[e].get(key, 0) >= val:
                    continue
                waits[key] = val
                self.seen[e][key] = val
            if waits:
                self.prog[e].append((None, list(waits.items()), None))

    def emit(self, nc, stack):
        keys = set()
        for e in self.ENGS:
            for (_, waits, inc) in self.prog[e]:
                for k, _v in waits:
                    keys.add(k)
                if inc is not None:
                    keys.add(inc[0])
        sems = {}
        for k in sorted(keys):
            sems[k] = stack.enter_context(nc.semaphore("s_" + k))
        block = stack.enter_context(nc.Block())
        prog = self.prog

        def run(name, eng):
            for (fn, waits, inc) in prog[name]:
                for (k, v) in waits:
                    eng.wait_ge(sems[k], v)
                if fn is not None:
                    ins = fn(eng)
                    if inc is not None:
                        ins.then_inc(sems[inc[0]], inc[1])

        @block.tensor
        def _(e):
            run("pe", e)

        @block.scalar
        def _(e):
            run("act", e)

        @block.vector
        def _(e):
            run("dve", e)

        @block.gpsimd
        def _(e):
            run("pool", e)

        @block.sync
        def _(e):
            run("sp", e)


def _bucket(dist):
    dist = np.maximum(dist, 0)
    me = 16
    lr = np.log(np.maximum(dist, 1).astype(np.float32) / np.float32(me)) / np.float32(math.log(2048 / me))
    large = me + (lr.astype(np.float32) * np.float32(16)).astype(np.int32)
    large = np.minimum(large, 31)
    return np.where(dist < me, dist, large)


def _tables(rel_bias):
    rel_bias = np.asarray(rel_bias, np.float32)
    p = np.arange(128)[:, None]
    c = np.arange(TA_W)[None, :]
    dist = c - 128 - p
    bk = _bucket(dist)
    tabA = np.where(dist[None] >= 0, rel_bias[0:8][:, bk], np.float32(NEG)).astype(np.float32)
    c = np.arange(256)[None, :]
    dm = c - p
    tabB = np.empty((24, 128, 256), np.float32)
    for g, d in enumerate(DIL):
        bk = _bucket(d * dm)
        ok = (dm >= 0) & (dm <= 128)
        tabB[g * 8:(g + 1) * 8] = np.where(ok[None], rel_bias[8 + g * 8: 16 + g * 8][:, bk], np.float32(NEG))
    bk = _bucket(dm)
    ok = (dm >= 0) & (dm <= 127)
    tabC = np.where(ok[None], rel_bias[32:40][:, bk], np.float32(NEG)).astype(np.float32)
    return tabA, tabB, tabC


def _consts():
    cb = np.zeros((128, 384), np.float32)
    cb[64, 256:320] = 0.5
    cb[0, 320:384] = 0.5
    cb[:, 0:128] = np.eye(128, dtype=np.float32)
    cb[0:64, 128:192] = 1.0 / 64
    cb[64:128, 192:256] = 1.0 / 64
    cf = np.zeros((128, 256), np.float32)
    cf[64, 0:64] = 0.5
    cf[0, 64:128] = 0.5
    cm = np.zeros((16, 8), np.float32)
    for t in range(16):
        for j in range(8):
            cm[t, j] = 0.0 if j < (t // 2) else -1e30
    cf[:, 128:256] = cm.reshape(1, 128)
    return cb.astype(ml_dtypes.bfloat16), cf


def _smallf(ln_g, qk_g, sinks):
    nl = ln_g.shape[0]
    out = np.zeros((128, nl * 22 + 1), np.float32)
    out[:, nl * 22] = EPS
    for l in range(nl):
        out[:, l * 22: l * 22 + 8] = np.asarray(ln_g[l], np.float32).reshape(8, 128).T
        for j in range(6):
            out[:, l * 22 + 8 + j] = np.tile(np.asarray(qk_g[l, j], np.float32), 2)
        out[:, l * 22 + 14: l * 22 + 22] = np.asarray(sinks[l], np.float32)[None, :]
    return out


class _Stop(Exception):
    pass


class Prog:
    def __init__(self, NL, debug=None):
        self.NL = NL
        self.debug = debug
        nc = self.nc = bass.Bass("TRN2", target_bir_lowering=False)
        self.S = Sched()
        dt = nc.dram_tensor
        self.x_d = dt("x", [S_TOK, D], F32, kind="ExternalInput").ap()
        self.w_in_d = dt("w_in", [NL, D, C_IN], F32, kind="ExternalInput").ap()
        self.w_br_d = dt("w_branch", [NL, 3, 512, D], F32, kind="ExternalInput").ap()
        self.w_out_d = dt("w_out", [NL, D, D], F32, kind="ExternalInput").ap()
        self.tabA_d = dt("tabA", [8, 128, TA_W], F32, kind="ExternalInput").ap()
        self.tabB_d = dt("tabB", [24, 128, 256], F32, kind="ExternalInput").ap()
        self.tabC_d = dt("tabC", [8, 128, 256], F32, kind="ExternalInput").ap()
        self.cb_d = dt("constb", [128, 384], BF16, kind="ExternalInput").ap()
        self.cf_d = dt("constf", [128, 256], F32, kind="ExternalInput").ap()
        self.sm_d = dt("smallf", [128, NL * 22 + 1], F32, kind="ExternalInput").ap()
        self.y_d = dt("y", [S_TOK, D], F32, kind="ExternalOutput").ap()
        if debug:
            self.dbg_d = dt("dbg", [128, 12 * 2048], BF16, kind="ExternalOutput").ap()

    def stop(self, name):
        if self.debug == name:
            raise _Stop()

    def mm(self, out, lhsT, rhs, start, stop, reads, writes):
        self.S.op("pe", lambda e: e.matmul(out, lhsT=lhsT, rhs=rhs, start=start, stop=stop),
                  reads=reads, writes=writes)

    def carve(self, off, nbytes, dtype):
        assert off % 4 == 0 and nbytes % 4 == 0 and off + nbytes <= self.WORK, (off, nbytes)
        ap = self.work[:, off // 2:(off + nbytes) // 2]
        if dtype == F32:
            ap = ap.bitcast(F32)
        return ap

    def sbank(self):
        i = self.sb_i
        self.sb_i = (i + 1) % 4
        return self.banks[i], self.r_bank[i]

    def build(self):
        nc, S, NL = self.nc, self.S, self.NL
        with ExitStack() as st:
            T = lambda name, shape, dtype: st.enter_context(nc.sbuf_tensor(name, shape, dtype))
            self.xres = T("xres", [128, NT, D], F32)
            self.hnT = T("hnT", [128, 8, S_TOK], BF16)
            self.yT = T("yT", [128, 12, S_TOK], BF16)
            self.cb = T("cb", [128, 384], BF16)
            self.cf = T("cf", [128, 256], F32)
            self.sm = T("sm", [128, NL * 22 + 1], F32)
            self.epsc = self.sm[:, NL * 22:NL * 22 + 1]
            self.WORK = 61 * 1024
            self.work = T("work", [128, self.WORK // 2], BF16)
            self.banks = [st.enter_context(nc.psum_tensor("bank%d" % i, [128, 512], F32)) for i in range(8)]
            self.r_bank = [Res("bank%d" % i) for i in range(8)]
            self.sb_i = 0
            self.ident = self.cb[:, 0:128]
            self.bdiag = self.cb[:, 128:256]
            self.r_x = [Res("x%d" % i) for i in range(NT)]
            self.r_hnT = [[Res("hnT%d_%d" % (tg, kc)) for kc in range(8)] for tg in range(4)]
            self.r_yT = [[Res("yT%d_%d" % (ch, c)) for c in range(4)] for ch in range(12)]
            self.r_const = Res("const")
            self.NSTG, self.NSLOT = 2, 5
            self.stg = [self.carve(i * 4096, 4096, F32).rearrange("p (k c) -> p k c", k=8) for i in range(self.NSTG)]
            self.r_stg = [Res("stg%d" % i) for i in range(self.NSTG)]
            self.stg_slot = [S.new_slot() for _ in range(self.NSTG)]
            base = self.NSTG * 4096
            self.wsl = [self.carve(base + i * 2048, 2048, BF16).rearrange("p (k c) -> p k c", k=8)
                        for i in range(self.NSLOT)]
            self.r_wsl = [Res("wsl%d" % i) for i in range(self.NSLOT)]
            self.w_n = 0
            self.W0 = base + self.NSLOT * 2048
            self.misc_slot = S.new_slot()
            self.tab_slots = [S.new_slot(), S.new_slot()]
            self.out_slot = S.new_slot()

            S.dma("sp", lambda e: e.dma_start(out=self.cb[:], in_=self.cb_d[:, :]), self.misc_slot, writes=[self.r_const])
            S.dma("sp", lambda e: e.dma_start(out=self.cf[:], in_=self.cf_d[:, :]), self.misc_slot, writes=[self.r_const])
            S.dma("sp", lambda e: e.dma_start(out=self.sm[:], in_=self.sm_d[:, :]), self.misc_slot, writes=[self.r_const])
            xs = S.new_slot()
            xv = self.x_d.rearrange("(t p) d -> p t d", p=128)
            for i in range(NT):
                S.dma("sp", lambda e, i=i: e.dma_start(out=self.xres[:, i, :], in_=xv[:, i, :]), xs, writes=[self.r_x[i]])
            for i in range(NT):
                self.r_x[i].w = (xs.key, xs.total)
            for l in range(NL):
                sl = self.sm[:, l * 22 + 14: l * 22 + 22]
                S.op("act", lambda e, sl=sl: e.activation(out=sl, in_=sl, func=AF.Exp),
                     reads=[self.r_const], writes=[self.r_const])

            for l in range(NL):
                self.phase0(l)
                S.barrier()
                if self.debug == "p0":
                    break
                try:
                    self.phase1(l)
                except _Stop:
                    pass
                S.barrier()
                if self.debug and l == 0:
                    break
                self.phase2(l)
                S.barrier()

            if self.debug == "p0":
                ds = S.new_slot()
                for ch in range(8):
                    S.dma("sp", lambda e, ch=ch: e.dma_start(out=self.dbg_d[:, ch * 2048:(ch + 1) * 2048],
                                                             in_=self.hnT[:, ch, :]), ds,
                          reads=[self.r_hnT[tg][ch] for tg in range(4)])
            elif self.debug:
                ds = S.new_slot()
                nd = {"A1": 1, "A": 4, "B": 8, "yT": 12}.get(self.debug, 0)
                for ch in range(nd):
                    S.dma("sp", lambda e, ch=ch: e.dma_start(out=self.dbg_d[:, ch * 2048:(ch + 1) * 2048],
                                                             in_=self.yT[:, ch, :]), ds, reads=self.r_yT[ch])
            yv = self.y_d.rearrange("(t p) d -> p t d", p=128)
            for i in range(NT):
                S.dma("sp", lambda e, i=i: e.dma_start(out=yv[:, i, :], in_=self.xres[:, i, :]), self.out_slot,
                      reads=[self.r_x[i]])
            S.barrier()
            S.emit(nc, st)
        return nc

    def wunit(self, srcs, nk=8):
        S = self.S
        i = self.w_n
        self.w_n += 1
        sg, sl = i % self.NSTG, i % self.NSLOT
        stg, slot = self.stg[sg], self.wsl[sl]
        for (src, c0) in srcs:
            nc_ = src.shape[-1]
            S.dma("sp", lambda e, src=src, c0=c0, nc_=nc_: e.dma_start(out=stg[:, 0:nk, c0:c0 + nc_], in_=src),
                  self.stg_slot[sg], writes=[self.r_stg[sg]])
        S.op("pool", lambda e: e.tensor_copy(out=slot[:, 0:nk, :], in_=stg[:, 0:nk, :]),
             reads=[self.r_stg[sg]], writes=[self.r_wsl[sl]])
        return slot, self.r_wsl[sl]

    def win_unit(self, l, col0, ncols=128, dup=False):
        src = self.w_in_d[l, :, col0:col0 + ncols].rearrange("(kc p) c -> p kc c", p=128)
        if dup:
            return self.wunit([(src, 0), (src, ncols)])
        return self.wunit([(src, 0)])

    def phase0(self, l):
        S = self.S
        W0 = self.W0
        junk = self.carve(W0, 2048, BF16)
        hnb = self.carve(W0 + 2048, 8192, BF16).rearrange("p (t d) -> p t d", t=4)
        ss = self.carve(W0 + 10240, 64, F32)
        rstd = self.carve(W0 + 10304, 64, F32)
        r_junk, r_ss, r_rstd = Res(), [Res() for _ in range(NT)], [Res() for _ in range(NT)]
        r_hnb = [Res() for _ in range(4)]
        lng = self.sm[:, l * 22: l * 22 + 8]
        for tg in range(4):
            for tt in range(4):
                i = tg * 4 + tt
                S.op("act", lambda e, i=i: e.activation(out=junk, in_=self.xres[:, i, :], func=AF.Square,
                                                        accum_out=ss[:, i:i + 1]),
                     reads=[self.r_x[i]], writes=[r_junk, r_ss[i]])
                S.op("act", lambda e, i=i: e.activation(out=rstd[:, i:i + 1], in_=ss[:, i:i + 1], func=AF.Sqrt,
                                                        scale=1.0 / D, bias=self.epsc),
                     reads=[r_ss[i], self.r_const], writes=[r_rstd[i]])
                S.op("dve", lambda e, i=i: e.reciprocal(out=rstd[:, i:i + 1], in_=rstd[:, i:i + 1]),
                     reads=[r_rstd[i]], writes=[r_rstd[i]])
                S.op("dve", lambda e, i=i, tt=tt: e.tensor_scalar(out=hnb[:, tt, :], in0=self.xres[:, i, :],
                                                                  scalar1=rstd[:, i:i + 1], scalar2=None,
                                                                  op0=ALU.mult),
                     reads=[self.r_x[i], r_rstd[i]], writes=[r_hnb[tt]])
            for kc in range(8):
                bank, rb = self.sbank()
                pb = bank[:].bitcast(BF16)
                for tt in range(4):
                    S.op("pe", lambda e, pb=pb, tt=tt, kc=kc: e.transpose(pb[:, tt * 128:(tt + 1) * 128],
                                                                         hnb[:, tt, kc * 128:(kc + 1) * 128],
                                                                         self.ident),
                         reads=[r_hnb[tt], self.r_const], writes=[rb])
                dst = self.hnT[:, kc, tg * 512:(tg + 1) * 512]
                if kc % 2 == 0:
                    S.op("act", lambda e, pb=pb, dst=dst, kc=kc: e.activation(out=dst, in_=pb[:, 0:512], func=AF.Copy,
                                                                             scale=lng[:, kc:kc + 1]),
                         reads=[rb, self.r_const], writes=[self.r_hnT[tg][kc]])
                else:
                    S.op("dve", lambda e, pb=pb, dst=dst, kc=kc: e.tensor_scalar(out=dst, in0=pb[:, 0:512],
                                                                                scalar1=lng[:, kc:kc + 1], scalar2=None,
                                                                                op0=ALU.mult),
                         reads=[rb, self.r_const], writes=[self.r_hnT[tg][kc]])

    def p1_layout(self):
        o = self.W0
        L = {}

        def take(name, nbytes):
            nonlocal o
            L[name] = o
            o += nbytes
        take("QT", 4096)
        take("KT", 4096)
        take("G", 4096)
        take("Vp", 6144)
        take("P", 3 * 1024)
        take("sq", 2 * 1024)
        take("rs", 2 * 2048)
        take("tstage", 2 * 2176)
        take("tabBC", 2 * 1024)
        take("rr", 2048)
        take("tt", 2048)
        take("rb16", 2048)
        take("th", 2048)
        take("km", 64)
        take("rank", 1024)
        take("pen", 512)
        assert o <= self.WORK, o
        return L

    def phase1(self, l):
        S = self.S
        L = self.p1_layout()
        c = self.carve
        self.QT = c(L["QT"], 4096, BF16)
        self.KT = c(L["KT"], 4096, BF16)
        self.G = c(L["G"], 4096, BF16)
        self.Vp = c(L["Vp"], 6144, BF16).rearrange("p (t c) -> p t c", t=16)
        self.Pb = [c(L["P"] + i * 1024, 1024, BF16) for i in range(3)]
        self.sqb = [c(L["sq"] + i * 1024, 1024, BF16) for i in range(2)]
        self.rsb = [c(L["rs"] + i * 2048, 2048, F32) for i in range(2)]
        self.tstage = [c(L["tstage"] + i * 2176, 2176, F32) for i in range(2)]
        self.tabBC = [c(L["tabBC"] + i * 1024, 1024, BF16).rearrange("p (h c) -> p h c", h=2) for i in range(2)]
        self.rr = c(L["rr"], 2048, F32)
        self.tt = c(L["tt"], 2048, F32)
        self.rb16 = c(L["rb16"], 2048, BF16).rearrange("p (a n) -> p a n", a=2)
        self.th = c(L["th"], 2048, F32)
        self.km = c(L["km"], 64, BF16)[:, 0:16]
        self.rank = c(L["rank"], 1024, F32)
        self.pen = c(L["pen"], 512, BF16)
        yc = self.yT[:, 8:12, :].rearrange("p a t -> p (a t)")
        self.TA = [yc[:, i * TA_W:(i + 1) * TA_W] for i in range(2)]
        o = 2 * TA_W
        self.penT = yc[:, o:o + 2048]
        o += 2048
        self.gm = yc[:, o:o + 512].bitcast(F32)
        o += 512
        self.cmpb = yc[:, o:o + 1024]
        o += 1024
        self.acc0 = yc[:, 0:4096].bitcast(F32)
        self.acc1 = yc[:, 4096:8192].bitcast(F32)
        self.r_QT = [Res() for _ in range(4)]
        self.r_KT = [Res() for _ in range(4)]
        self.r_G = [Res() for _ in range(4)]
        self.r_Vp = [Res() for _ in range(4)]
        self.r_P = [Res() for _ in range(3)]
        self.r_sq = [Res() for _ in range(2)]
        self.r_rs = [Res() for _ in range(2)]
        self.r_tstage = [Res() for _ in range(2)]
        self.r_tabBC = [Res() for _ in range(2)]
        self.r_TA = [Res() for _ in range(2)]
        self.r_rr, self.r_tt, self.r_th = Res(), Res(), Res()
        self.r_rb16 = Res()
        self.r_misc = Res()
        self.r_acc = Res()
        self.p_i = 0
        self.q_i = 0
        self.ts_i = 0
        self.tb_i = 0
        self.o_i = 0
        S.op("pool", lambda e: e.memset(self.rb16, 0.0), writes=[self.r_rb16])
        S.op("pool", lambda e: e.memset(self.Vp[:, :, 64:128], 0.0), writes=self.r_Vp)
        S.op("pool", lambda e: e.memset(self.Vp[:, :, 64:65], 1.0), writes=self.r_Vp)
        gq = lambda j: self.sm[:, l * 22 + 8 + j: l * 22 + 9 + j]
        for hp in range(4):
            self.load_tabA(hp)
            self.proj_qk(l, A_Q + hp * 128, self.QT, self.r_QT, gq(0), 1)
            self.proj_qk(l, A_K + hp * 128, self.KT, self.r_KT, gq(1), 1)
            self.proj_v(l, A_V + hp * 128, 1)
            self.proj_g(l, A_G + hp * 128)
            if self.debug == "A1a":
                raise _Stop()
            self.moba_pair(l, hp)
            if self.debug == "A1":
                raise _Stop()
        S.barrier()
        if self.debug == "A":
            return
        for hp in range(4):
            for g, d in enumerate(DIL):
                self.load_tabBC(self.tabB_d, g * 8 + 2 * hp)
                self.stop("t1")
                self.proj_qk(l, B_Q + g * 512 + hp * 128, self.QT, self.r_QT, gq(2), d)
                self.proj_qk(l, B_K + g * 512 + hp * 128, self.KT, self.r_KT, gq(3), d)
                self.proj_v(l, B_V + g * 512 + hp * 128, d)
                if g == 0:
                    self.proj_g(l, B_G + hp * 128)
                self.stop("t2")
                self.window_pair(d, first=(g == 0), mode="B")
                self.stop("B%d" % (g + 1))
            self.finish_B(hp)
            self.stop("B4")
        S.barrier()
        if self.debug == "B":
            return
        for kv in range(2):
            self.proj_qk(l, C_K + kv * 64, self.KT, self.r_KT, gq(5), 1, ncols=64, dup=True)
            self.proj_v(l, C_V + kv * 64, 1, ncols=64, dup=True)
            for hq in range(2):
                hp = kv * 2 + hq
                self.load_tabBC(self.tabC_d, 2 * hp)
                self.proj_qk(l, C_Q + hp * 128, self.QT, self.r_QT, gq(4), 1)
                self.proj_g(l, C_G + hp * 128)
                self.window_pair(1, first=True, mode="C", hp=hp, l=l)

    def load_tabA(self, hp):
        S = self.S
        for hh in range(2):
            for pc in range(4):
                i = self.ts_i
                self.ts_i = (i + 1) % 2
                stg = self.tstage[i]
                src = self.tabA_d[2 * hp + hh, :, pc * 544:(pc + 1) * 544]
                S.dma("sp", lambda e, stg=stg, src=src: e.dma_start(out=stg, in_=src), self.tab_slots[i],
                      writes=[self.r_tstage[i]])
                dst = self.TA[hh][:, pc * 544:(pc + 1) * 544]
                S.op("act", lambda e, stg=stg, dst=dst: e.activation(out=dst, in_=stg, func=AF.Exp),
                     reads=[self.r_tstage[i]], writes=[self.r_TA[hh]])

    def load_tabBC(self, tab_d, idx0):
        S = self.S
        b = self.tb_i
        self.tb_i = (b + 1) % 2
        i = self.ts_i
        self.ts_i = (i + 1) % 2
        stg = self.tstage[i][:, 0:512].rearrange("p (h c) -> p h c", h=2)
        src = tab_d[idx0:idx0 + 2, :, :].rearrange("h p c -> p h c")
        S.dma("sp", lambda e: e.dma_start(out=stg, in_=src), self.tab_slots[i], writes=[self.r_tstage[i]])
        S.op("act", lambda e: e.activation(out=self.tabBC[b], in_=stg, func=AF.Exp),
             reads=[self.r_tstage[i]], writes=[self.r_tabBC[b]])
        self.cur_tab = (self.tabBC[b], self.r_tabBC[b])

    def proj_qk(self, l, col0, dst, r_dst, gcol, d, ncols=128, dup=False):
        S = self.S
        W, rW = self.win_unit(l, col0, ncols, dup)
        jobs = []
        for cch in range(4):
            def s0(cch=cch):
                bank, rb = self.sbank()
                for kc in range(8):
                    self.mm(bank[:], W[:, kc, :], self.hnT[:, kc, cch * 512:(cch + 1) * 512], kc == 0, kc == 7,
                            [rW, self.r_hnT[cch][kc]], [rb])
                i = self.q_i
                self.q_i = (i + 1) % 2
                S.op("act", lambda e: e.activation(out=self.sqb[i], in_=bank[:], func=AF.Square),
                     reads=[rb], writes=[self.r_sq[i]])
                return bank, rb, i

            def s1(state, cch=cch):
                bank, rb, i = state
                bank2, rb2 = self.sbank()
                self.mm(bank2[:], self.bdiag, self.sqb[i], True, True, [self.r_sq[i], self.r_const], [rb2])
                S.op("act", lambda e: e.activation(out=self.rsb[i], in_=bank2[:], func=AF.Sqrt, bias=self.epsc),
                     reads=[rb2, self.r_const], writes=[self.r_rs[i]])
                S.op("dve", lambda e: e.reciprocal(out=self.rsb[i], in_=self.rsb[i]),
                     reads=[self.r_rs[i]], writes=[self.r_rs[i]])
                if d == 1:
                    o_ap = dst[:, cch * 512:(cch + 1) * 512]
                    i0, i1 = bank[:], self.rsb[i]
                    wr = [r_dst[cch]]
                else:
                    na = 512 // d
                    o_ap = dst.rearrange("p (r m) -> p r m", r=d)[:, :, cch * na:(cch + 1) * na]
                    i0 = bank[:].rearrange("p (a r) -> p r a", r=d)
                    i1 = self.rsb[i].rearrange("p (a r) -> p r a", r=d)
                    wr = r_dst
                S.op("dve", lambda e: e.scalar_tensor_tensor(out=o_ap, in0=i0, scalar=gcol, in1=i1,
                                                             op0=ALU.mult, op1=ALU.mult),
                     reads=[rb, self.r_rs[i], self.r_const], writes=wr)
            jobs.append((s0, s1))
        self.pipeline(jobs, 1)

    def pipeline(self, jobs, lag):
        st = {}
        n = len(jobs)
        for t in range(n + lag):
            if t < n:
                st[t] = jobs[t][0]()
            if t - lag >= 0:
                jobs[t - lag][1](st.pop(t - lag))

    def proj_g(self, l, col0):
        S = self.S
        W, rW = self.win_unit(l, col0)
        for cch in range(4):
            bank, rb = self.sbank()
            for kc in range(8):
                self.mm(bank[:], W[:, kc, :], self.hnT[:, kc, cch * 512:(cch + 1) * 512], kc == 0, kc == 7,
                        [rW, self.r_hnT[cch][kc]], [rb])
            S.op("act", lambda e, bank=bank: e.activation(out=self.th, in_=bank[:], func=AF.Tanh, scale=0.5),
                 reads=[rb], writes=[self.r_th])
            dst = self.G[:, cch * 512:(cch + 1) * 512]
            S.op("dve", lambda e, bank=bank, dst=dst: e.scalar_tensor_tensor(out=dst, in0=self.th, scalar=1.0, in1=bank[:],
                                                                             op0=ALU.add, op1=ALU.mult),
                 reads=[rb, self.r_th], writes=[self.r_G[cch]])

    def proj_v(self, l, col0, d, ncols=128, dup=False):
        S = self.S
        W, rW = self.win_unit(l, col0, ncols, dup)
        Ls = S_TOK // d
        for tq in range(4):
            bank, rb = self.sbank()
            for tt in range(4):
                j = tq * 4 + tt
                u0 = j * 128
                r, m0 = u0 // Ls, u0 % Ls
                start = m0 * d + r
                for kc in range(8):
                    lhsT = self.hnT[:, kc, start: start + 127 * d + 1: d]
                    reads = [rW] + [self.r_hnT[cc][kc] for cc in range(start // 512, (start + 127 * d) // 512 + 1)]
                    self.mm(bank[:, tt * 128:(tt + 1) * 128], lhsT, W[:, kc, :], kc == 0, kc == 7, reads, [rb])
            bv = bank[:].rearrange("p (t c) -> p t c", t=4)
            S.op("act", lambda e, bv=bv, tq=tq: e.activation(out=self.Vp[:, tq * 4:(tq + 1) * 4, 0:64], in_=bv[:, :, 0:64],
                                                             func=AF.Copy),
                 reads=[rb], writes=[self.r_Vp[tq]])
            S.op("dve", lambda e, bv=bv, tq=tq: e.tensor_copy(out=self.Vp[:, tq * 4:(tq + 1) * 4, 128:192],
                                                              in_=bv[:, :, 64:128]),
                 reads=[rb], writes=[self.r_Vp[tq]])

    def normalize(self, o0, o1, r_o0, r_o1, N, gcols, ych, ycols, r_y, sink=None, from_sbuf=False):
        S = self.S
        rr, tt = self.rr, self.tt
        if sink is not None:
            e0, e1 = sink
            S.op("dve", lambda e: e.tensor_scalar(out=rr[64:65, 0:N], in0=o0[64:65, 0:N], scalar1=e0, scalar2=None,
                                                  op0=ALU.add), reads=[r_o0, self.r_const], writes=[self.r_rr])
            S.op("dve", lambda e: e.reciprocal(out=rr[64:65, 0:N], in_=rr[64:65, 0:N]), reads=[self.r_rr], writes=[self.r_rr])
            S.op("dve", lambda e: e.tensor_scalar(out=rr[0:1, 0:N], in0=o1[0:1, 0:N], scalar1=e1, scalar2=None,
                                                  op0=ALU.add), reads=[r_o1, self.r_const], writes=[self.r_rr])
            S.op("dve", lambda e: e.reciprocal(out=rr[0:1, 0:N], in_=rr[0:1, 0:N]), reads=[self.r_rr], writes=[self.r_rr])
        else:
            S.op("dve", lambda e: e.reciprocal(out=rr[64:65, 0:N], in_=o0[64:65, 0:N]), reads=[r_o0], writes=[self.r_rr])
            S.op("dve", lambda e: e.reciprocal(out=rr[0:1, 0:N], in_=o1[0:1, 0:N]), reads=[r_o1], writes=[self.r_rr])
        self.stop("h4")
        rb16 = self.rb16
        for row in (64, 0):
            S.op("dve", lambda e, row=row: e.tensor_copy(out=rb16[row:row + 1, 0, 0:N], in_=rr[row:row + 1, 0:N]),
                 reads=[self.r_rr], writes=[self.r_rb16])
            S.op("dve", lambda e, row=row: e.tensor_tensor(out=rb16[row:row + 1, 1, 0:N], in0=rr[row:row + 1, 0:N],
                                                           in1=rb16[row:row + 1, 0, 0:N], op=ALU.subtract),
                 reads=[self.r_rr, self.r_rb16], writes=[self.r_rb16])
        self.stop("h5")
        bank, rb = self.sbank()
        for a in range(2):
            self.mm(bank[:, 0:N], self.cb[:, 256:384], rb16[:, a, 0:N], a == 0, a == 1,
                    [self.r_rb16, self.r_const], [rb])
        self.stop("h6")
        S.op("dve", lambda e: e.tensor_tensor(out=tt[:, 0:N], in0=gcols, in1=bank[:, 0:N], op=ALU.mult),
             reads=[rb] + self.r_G, writes=[self.r_tt])
        self.stop("h7")
        S.op("dve", lambda e: e.tensor_tensor(out=self.yT[0:64, ych, ycols], in0=tt[0:64, 0:N], in1=o0[0:64, 0:N],
                                              op=ALU.mult), reads=[self.r_tt, r_o0], writes=r_y)
        S.op("dve", lambda e: e.tensor_tensor(out=self.yT[64:128, ych, ycols], in0=tt[64:128, 0:N], in1=o1[64:128, 0:N],
                                              op=ALU.mult), reads=[self.r_tt, r_o1], writes=r_y)

    def obanks(self):
        i = self.o_i
        self.o_i = (i + 1) % 2
        return (self.banks[4 + 2 * i], self.r_bank[4 + 2 * i], self.banks[5 + 2 * i], self.r_bank[5 + 2 * i])

    def moba_pair(self, l, hp):
        S = self.S
        QT, KT, Vp = self.QT, self.KT, self.Vp
        def km_fn(e):
            with self.nc.allow_low_precision("block key sums only rank MoBA blocks; bf16 matmul operand"):
                return e.tensor_reduce(out=self.km[:, 0:8], in_=KT.rearrange("p (j k) -> p j k", j=8),
                                       axis=AX.X, op=ALU.add)
        S.op("dve", km_fn, reads=self.r_KT, writes=[self.r_misc])
        self.stop("g1")
        gm = self.gm.rearrange("p (t h j) -> p t h j", t=16, h=2)
        cm = self.cf[:, 128:256].rearrange("p (t j) -> p t j", t=16)
        for hh in range(2):
            gbank, rgb = self.sbank()
            gv = gbank[:, 0:128].rearrange("p (t j) -> p t j", t=16)
            for t in range(16):
                self.mm(gv[:, t, :], QT[hh * 64:(hh + 1) * 64, t * 128:(t + 1) * 128],
                        self.km[hh * 64:(hh + 1) * 64, 0:8], True, True,
                        [self.r_QT[t // 4], self.r_misc], [rgb])
            S.op("dve", lambda e, gv=gv, hh=hh: e.tensor_tensor(out=gm[:, :, hh, :], in0=gv, in1=cm, op=ALU.add),
                 reads=[rgb, self.r_const], writes=[self.r_misc])
        self.stop("g3")
        g3 = self.gm.rearrange("p (a j) -> p a j", j=8)
        rank = self.rank.rearrange("p (a j) -> p a j", j=8)
        pen = self.pen.rearrange("p (a j) -> p a j", j=8)
        for half in range(2):
            a0 = half * 16
            cmpv = self.cmpb.rearrange("p (a j k) -> p a j k", a=16, j=8)
            in0 = g3[:, a0:a0 + 16, :].unsqueeze(2).broadcast_to([128, 16, 8, 8])
            in1 = g3[:, a0:a0 + 16, :].unsqueeze(3).broadcast_to([128, 16, 8, 8])
            S.op("dve", lambda e, in0=in0, in1=in1, cmpv=cmpv: e.tensor_tensor(out=cmpv, in0=in0, in1=in1, op=ALU.is_gt),
                 reads=[self.r_misc], writes=[self.r_misc])
            S.op("dve", lambda e, cmpv=cmpv, a0=a0: e.tensor_reduce(out=rank[:, a0:a0 + 16, :], in_=cmpv, axis=AX.X, op=ALU.add),
                 reads=[self.r_misc], writes=[self.r_misc])
        self.stop("g4")
        S.op("dve", lambda e: e.tensor_scalar(out=self.pen, in0=self.rank, scalar1=2.5, scalar2=-240000.0,
                                              op0=ALU.is_gt, op1=ALU.mult), reads=[self.r_misc], writes=[self.r_misc])
        self.stop("g5")
        pent = self.pen.rearrange("p (t c) -> p t c", t=16)
        r_penT = Res()
        S.op("pool", lambda e: e.memset(self.penT, 0.0), writes=[r_penT])
        for half in range(2):
            bank, rb = self.sbank()
            pb = bank[:].bitcast(BF16)
            for tt_ in range(8):
                t = half * 8 + tt_
                S.op("pe", lambda e, pb=pb, tt_=tt_, t=t: e.transpose(pb[0:16, tt_ * 128:(tt_ + 1) * 128], pent[:, t, :],
                                                                       self.ident),
                     reads=[self.r_misc, self.r_const], writes=[rb])
            if self.debug == "g6":
                continue
            S.op("dve", lambda e, pb=pb, half=half: e.tensor_copy(out=self.penT[0:16, half * 1024:(half + 1) * 1024],
                                                                  in_=pb[0:16, 0:1024]),
                 reads=[rb], writes=[r_penT])
        if self.debug in ("A1b", "g6"):
            raise _Stop()
        for n in range(8):
            if self.debug == "A1c" and n == 1:
                raise _Stop()
            O0b, rO0, O1b, rO1 = self.obanks()
            q0 = 256 * n
            for hh in range(2):
                ps = slice(hh * 64, (hh + 1) * 64)
                Ob, rO = (O0b, rO0) if hh == 0 else (O1b, rO1)
                Oview = Ob[0:65, 0:256] if hh == 0 else Ob[0:128, 0:256]
                lc = (lambda kt: Vp[:, kt, 0:65]) if hh == 0 else (lambda kt: Vp[:, kt, 64:192])
                npairs = n + 1
                jobs = []
                for jp in range(npairs):
                    a = 2 * jp
                    past = jp < n

                    def s0(a=a, past=past, jp=jp, ps=ps, hh=hh):
                        bank, rb = self.sbank()
                        rhs = QT[ps, q0:q0 + 256]
                        rq = [self.r_QT[q0 // 512]]
                        for hf, kt in enumerate((a + 1, a)):
                            o = bank[:, hf * 256:(hf + 1) * 256]
                            if past:
                                cidx = hh * 8 + jp
                                lp = self.cb[:, cidx:cidx + 1].broadcast_to([128, 128])
                                self.mm(o, lp, self.penT[:, q0:q0 + 256], True, False, [r_penT, self.r_const], [rb])
                            self.mm(o, KT[ps, kt * 128:(kt + 1) * 128], rhs, not past, True,
                                    rq + [self.r_KT[kt // 4]], [rb])
                        i = self.p_i
                        self.p_i = (i + 1) % 3
                        P = self.Pb[i]
                        S.op("act", lambda e: e.activation(out=P, in_=bank[:], func=AF.Exp, scale=0.125),
                             reads=[rb], writes=[self.r_P[i]])
                        self.stop("h1")
                        c0 = 256 * n - 128 * a + 128
                        base = self.TA[hh][:, c0 - 128:c0 - 127]
                        tv = bass.AP(base.tensor, base.offset, [list(base.ap[0]), [128, 2], [1, 256]])
                        Pv = P.rearrange("p (h c) -> p h c", h=2)
                        S.op("dve", lambda e: e.tensor_tensor(out=Pv, in0=Pv, in1=tv, op=ALU.mult),
                             reads=[self.r_TA[hh], self.r_P[i]], writes=[self.r_P[i]])
                        self.stop("h2")
                        return i

                    def s1(i, a=a, jp=jp, npairs=npairs, Oview=Oview, rO=rO, lc=lc):
                        P = self.Pb[i]
                        for hf, kt in enumerate((a + 1, a)):
                            self.mm(Oview, lc(kt), P[:, hf * 256:(hf + 1) * 256], jp == 0 and hf == 0,
                                    jp == npairs - 1 and hf == 1, [self.r_P[i], self.r_Vp[kt // 4]], [rO])
                    jobs.append((s0, s1))
                self.pipeline(jobs, 2)
                self.stop("h3")
            self.normalize(O0b, O1b, rO0, rO1, 256, self.G[:, q0:q0 + 256], hp, slice(q0, q0 + 256),
                           [self.r_yT[hp][q0 // 512]])

    def window_pair(self, d, first, mode, hp=None, l=None):
        S = self.S
        QT, KT, Vp = self.QT, self.KT, self.Vp
        tab, r_tab = self.cur_tab
        tps = NT // d
        osets = [(self.banks[4], self.r_bank[4], self.banks[5], self.r_bank[5]),
                 (self.banks[6], self.r_bank[6], self.banks[7], self.r_bank[7])]

        def pv(i, kt, qi, start, stop):
            qt = kt + qi
            O0b, rO0, O1b, rO1 = osets[(qt // 4) % 2]
            cs = slice((qt % 4) * 128, (qt % 4) * 128 + 128)
            P0 = self.Pb[i][:, qi * 128: qi * 128 + 128]
            P1 = self.Pb[i][:, 256 + qi * 128: 256 + qi * 128 + 128]
            self.mm(O0b[0:65, cs], Vp[:, kt, 0:65], P0, start, stop, [self.r_P[i], self.r_Vp[kt // 4]], [rO0])
            self.mm(O1b[0:128, cs], Vp[:, kt, 64:192], P1, start, stop, [self.r_P[i], self.r_Vp[kt // 4]], [rO1])

        jobs = []
        for kt in range(NT):
            first_in_sub = (kt % tps) == 0
            last_in_sub = (kt % tps) == tps - 1
            nq = 1 if last_in_sub else 2
            N = 128 * nq

            def s0(kt=kt, N=N):
                rq = [self.r_QT[kt // 4], self.r_QT[(kt * 128 + N - 1) // 512]]
                i = self.p_i
                self.p_i = (i + 1) % 3
                Pv = self.Pb[i].rearrange("p (h c) -> p h c", h=2)[:, :, 0:N]
                for hh in range(2):
                    bank, rb = self.sbank()
                    ps = slice(hh * 64, (hh + 1) * 64)
                    self.mm(bank[:, 0:N], KT[ps, kt * 128:(kt + 1) * 128],
                            QT[ps, kt * 128: kt * 128 + N], True, True, [self.r_KT[kt // 4]] + rq, [rb])
                    S.op("act", lambda e, bank=bank, hh=hh: e.activation(out=self.Pb[i][:, hh * 256: hh * 256 + N],
                                                                         in_=bank[:, 0:N], func=AF.Exp, scale=0.125),
                         reads=[rb], writes=[self.r_P[i]])
                tv = tab[:, :, 0:N]
                S.op("dve", lambda e: e.tensor_tensor(out=Pv, in0=Pv, in1=tv, op=ALU.mult),
                     reads=[r_tab, self.r_P[i]], writes=[self.r_P[i]])
                self.stop("w1")
                return i

            def s1(i, kt=kt, nq=nq, first_in_sub=first_in_sub):
                pv(i, kt, 0, first_in_sub, True)
                if nq == 2:
                    pv(i, kt, 1, True, False)
                self.stop("w2")
                if kt % 4 == 3:
                    if self.debug == "w3":
                        raise _Stop()
                    self.consume_group(kt // 4, d, first, mode, osets[(kt // 4) % 2], hp, l)
                    self.stop("w4")
            jobs.append((s0, s1))
        self.pipeline(jobs, 1)

    def consume_group(self, tq, d, first, mode, ob, hp, l):
        S = self.S
        O0b, rO0, O1b, rO1 = ob
        if mode == "C":
            q0 = tq * 512
            e0 = self.sm[64:65, l * 22 + 14 + 2 * hp: l * 22 + 15 + 2 * hp]
            e1 = self.sm[0:1, l * 22 + 15 + 2 * hp: l * 22 + 16 + 2 * hp]
            self.normalize(O0b, O1b, rO0, rO1, 512, self.G[:, q0:q0 + 512], 8 + hp, slice(q0, q0 + 512),
                           [self.r_yT[8 + hp][tq]], sink=(e0, e1))
            return
        Ls = S_TOK // d
        if d == 1:
            dst0 = self.acc0[0:65, tq * 512:(tq + 1) * 512]
            dst1 = self.acc1[:, tq * 512:(tq + 1) * 512]
            src0, src1 = O0b[0:65, :], O1b[:, :]
        elif d == 4:
            dst0 = self.acc0[0:65, tq:tq + 4 * 511 + 1:4]
            dst1 = self.acc1[:, tq:tq + 4 * 511 + 1:4]
            src0, src1 = O0b[0:65, :], O1b[:, :]
        else:
            dst0 = self.acc0[0:65, :].rearrange("p (m r) -> p r m", r=16)[:, 4 * tq:4 * tq + 4, :]
            dst1 = self.acc1[:, :].rearrange("p (m r) -> p r m", r=16)[:, 4 * tq:4 * tq + 4, :]
            src0 = O0b[0:65, :].rearrange("p (r m) -> p r m", r=4)
            src1 = O1b[:, :].rearrange("p (r m) -> p r m", r=4)
        if first:
            S.op("act", lambda e: e.activation(out=dst0, in_=src0, func=AF.Copy), reads=[rO0], writes=[self.r_acc])
            S.op("dve", lambda e: e.tensor_copy(out=dst1, in_=src1), reads=[rO1], writes=[self.r_acc])
        else:
            S.op("dve", lambda e: e.tensor_tensor(out=dst0, in0=dst0, in1=src0, op=ALU.add), reads=[rO0, self.r_acc],
                 writes=[self.r_acc])
            S.op("dve", lambda e: e.tensor_tensor(out=dst1, in0=dst1, in1=src1, op=ALU.add), reads=[rO1, self.r_acc],
                 writes=[self.r_acc])

    def finish_B(self, hp):
        for cch in range(4):
            q0 = cch * 512
            self.normalize(self.acc0[:, q0:q0 + 512], self.acc1[:, q0:q0 + 512], self.r_acc, self.r_acc, 512,
                           self.G[:, q0:q0 + 512], 4 + hp, slice(q0, q0 + 512), [self.r_yT[4 + hp][cch]])

    def phase2(self, l):
        S = self.S
        o = self.W0
        mT = self.carve(o, 16384, BF16).rearrange("p (k t) -> p k t", k=8)
        o += 16384
        acc = self.carve(o, 4096, F32)
        o += 4096
        th = [self.carve(o + i * 2048, 2048, F32) for i in range(2)]
        o += 4096
        wo = self.carve(o, 8192, BF16).rearrange("p (k e) -> p k e", k=8)
        o += 8192
        assert o <= self.WORK
        r_mT = [[Res() for _ in range(2)] for _ in range(8)]
        r_acc = [Res() for _ in range(2)]
        r_th = [Res() for _ in range(2)]
        r_wo = Res()
        th_i = 0
        for half in range(2):
            t0 = half * 1024
            for dc in range(8):
                for i in range(3):
                    Wg, rWg = self.win_unit(l, GATE0 + i * 1024 + dc * 128)
                    srcb = self.w_br_d[l, i, :, dc * 128:(dc + 1) * 128].rearrange("(wc p) c -> p wc c", p=128)
                    Wb, rWb = self.wunit([(srcb, 0)], nk=4)
                    for cc in range(2):
                        tok = slice(t0 + cc * 512, t0 + (cc + 1) * 512)
                        cg = (t0 + cc * 512) // 512
                        bg, rbg = self.sbank()
                        for kc in range(8):
                            self.mm(bg[:], Wg[:, kc, :], self.hnT[:, kc, tok], kc == 0, kc == 7,
                                    [rWg, self.r_hnT[cg][kc]], [rbg])
                        bb, rbb = self.sbank()
                        for wc in range(4):
                            self.mm(bb[:], Wb[:, wc, :], self.yT[:, i * 4 + wc, tok], wc == 0, wc == 3,
                                    [rWb, self.r_yT[i * 4 + wc][cg]], [rbb])
                        k = th_i
                        th_i = (k + 1) % 2
                        S.op("act", lambda e, bg=bg, k=k: e.activation(out=th[k], in_=bg[:], func=AF.Tanh, scale=0.5),
                             reads=[rbg], writes=[r_th[k]])
                        av = acc[:, cc * 512:(cc + 1) * 512]
                        if i == 0:
                            S.op("dve", lambda e, bb=bb, k=k, av=av: e.scalar_tensor_tensor(
                                out=av, in0=th[k], scalar=1.0, in1=bb[:], op0=ALU.add, op1=ALU.mult),
                                reads=[rbb, r_th[k]], writes=[r_acc[cc]])
                        else:
                            S.op("dve", lambda e, bb=bb, k=k: e.scalar_tensor_tensor(
                                out=th[k], in0=th[k], scalar=1.0, in1=bb[:], op0=ALU.add, op1=ALU.mult),
                                reads=[rbb, r_th[k]], writes=[r_th[k]])
                            if i == 1:
                                S.op("dve", lambda e, k=k, av=av: e.tensor_tensor(out=av, in0=av, in1=th[k], op=ALU.add),
                                     reads=[r_th[k], r_acc[cc]], writes=[r_acc[cc]])
                            else:
                                mv = mT[:, dc, cc * 512:(cc + 1) * 512]
                                S.op("dve", lambda e, k=k, av=av, mv=mv: e.tensor_tensor(out=mv, in0=av, in1=th[k], op=ALU.add),
                                     reads=[r_th[k], r_acc[cc]], writes=[r_mT[dc][cc]])
            for eh in range(2):
                for ec in range(4):
                    src = self.w_out_d[l, :, eh * 512 + ec * 128: eh * 512 + (ec + 1) * 128].rearrange("(kc p) c -> p kc c", p=128)
                    Wu, rWu = self.wunit([(src, 0)])
                    S.op("pool", lambda e, Wu=Wu, ec=ec: e.tensor_copy(out=wo[:, :, ec * 128:(ec + 1) * 128], in_=Wu[:, :, :]),
                         reads=[rWu], writes=[r_wo])
                for tt in range(8):
                    ti = half * 8 + tt
                    bank, rb = self.sbank()
                    for dc in range(8):
                        self.mm(bank[:], mT[:, dc, tt * 128:(tt + 1) * 128], wo[:, dc, :], dc == 0, dc == 7,
                                [r_mT[dc][tt // 4], r_wo], [rb])
                    xv = self.xres[:, ti, eh * 512:(eh + 1) * 512]
                    S.op("dve", lambda e, bank=bank, xv=xv: e.scalar_tensor_tensor(out=xv, in0=bank[:], scalar=0.5, in1=xv,
                                                                                  op0=ALU.mult, op1=ALU.add),
                         reads=[rb, self.r_x[ti]], writes=[self.r_x[ti]])


_PROG_CACHE = {}


def _get_prog(NL, debug=None):
    key = (NL, debug)
    if key not in _PROG_CACHE:
        p = Prog(NL, debug)
        p.build()
        _PROG_CACHE[key] = p
    return _PROG_CACHE[key]


FUSED = False


def kernel(x, ln_g, w_in, qk_g, sinks, w_branch, w_out, rel_bias):
    x = np.ascontiguousarray(np.asarray(x, np.float32))
    ln_g = np.asarray(ln_g, np.float32)
    w_in = np.ascontiguousarray(np.asarray(w_in, np.float32))
    qk_g = np.asarray(qk_g, np.float32)
    sinks = np.asarray(sinks, np.float32)
    w_branch = np.ascontiguousarray(np.asarray(w_branch, np.float32))
    w_out = np.ascontiguousarray(np.asarray(w_out, np.float32))
    tabA, tabB, tabC = _tables(rel_bias)
    cb, cf = _consts()
    depth = w_in.shape[0]
    ncores = x.shape[0]
    groups = [list(range(depth))] if FUSED else [[l] for l in range(depth)]
    cur = [x[b] for b in range(ncores)]
    for ls in groups:
        prog = _get_prog(len(ls))
        sl = slice(ls[0], ls[-1] + 1)
        common = {"w_in": w_in[sl], "w_branch": w_branch[sl], "w_out": w_out[sl], "tabA": tabA, "tabB": tabB,
                  "tabC": tabC, "constb": cb, "constf": cf, "smallf": _smallf(ln_g[sl], qk_g[sl], sinks[sl])}
        in_maps = [dict(common, x=cur[b]) for b in range(ncores)]
        res = run_bass_kernel_spmd(prog.nc, in_maps, core_ids=list(range(ncores)))
        cur = [np.asarray(res.results[b]["y"], np.float32) for b in range(ncores)]
    return np.stack(cur, axis=0)
```

```python
import math
from contextlib import ExitStack

import numpy as np
import ml_dtypes
import concourse.bass as bass
import concourse.mybir as mybir
from concourse.bass_utils import run_bass_kernel_spmd

F32 = mybir.dt.float32
BF16 = mybir.dt.bfloat16
AF = mybir.ActivationFunctionType
ALU = mybir.AluOpType
AX = mybir.AxisListType

S_TOK = 2048
D = 1024
NT = 16
C_IN = 11520
NEG = -30000.0
EPS = 1e-6
A_Q, A_K, A_V, A_G = 0, 512, 1024, 1536
B_Q, B_K, B_V, B_G = 2048, 3584, 5120, 6656
C_Q, C_K, C_V, C_G = 7168, 7680, 7808, 7936
GATE0 = 8448
DIL = (1, 4, 16)
TA_W = 2176


class Res:
    __slots__ = ("name", "w", "r")

    def __init__(self, name=""):
        self.name = name
        self.w = None
        self.r = {}


class DmaSlot:
    __slots__ = ("key", "total")

    def __init__(self, key):
        self.key = key
        self.total = 0


class Sched:
    ENGS = ("pe", "act", "dve", "pool", "sp")

    def __init__(self):
        self.prog = {e: [] for e in self.ENGS}
        self.count = {e: 0 for e in self.ENGS}
        self.seen = {e: {} for e in self.ENGS}
        self.slots = []

    def new_slot(self):
        s = DmaSlot("d%03d" % len(self.slots))
        self.slots.append(s)
        return s

    def _deps(self, eng, reads, writes):
        waits = {}
        seen = self.seen[eng]

        def add(tok):
            key, val = tok
            if key == eng and eng in ("pe", "sp"):
                return
            if seen.get(key, 0) >= val:
                return
            if waits.get(key, 0) < val:
                waits[key] = val

        for r in reads:
            if r.w is not None:
                add(r.w)
        for w in writes:
            if w.w is not None:
                add(w.w)
            for k, v in w.r.items():
                add((k, v))
        for k, v in waits.items():
            seen[k] = v
        return list(waits.items())

    def _mark(self, tok, reads, writes):
        for r in reads:
            if r.r.get(tok[0], 0) < tok[1]:
                r.r[tok[0]] = tok[1]
        for w in writes:
            w.w = tok
            w.r = {}

    def op(self, eng, fn, reads=(), writes=()):
        waits = self._deps(eng, reads, writes)
        self.count[eng] += 1
        tok = (eng, self.count[eng])
        self.prog[eng].append((fn, waits, (eng, 1)))
        self._mark(tok, reads, writes)
        return tok

    def dma(self, q, fn, slot, reads=(), writes=()):
        waits = self._deps(q, reads, writes)
        slot.total += 16
        tok = (slot.key, slot.total)
        self.prog[q].append((fn, waits, (slot.key, 16)))
        self._mark(tok, reads, writes)
        return tok

    def barrier(self):
        toks = [(e, self.count[e]) for e in self.ENGS if self.count[e] > 0]
        toks += [(s.key, s.total) for s in self.slots if s.total > 0]
        for e in self.ENGS:
            waits = {}
            for key, val in toks:
                if key == e and e in ("pe", "sp"):
                    continue
                if self.seen[e].get(key, 0) >= val:
                    continue
                waits[key] = val
                self.seen[e][key] = val
            if waits:
                self.prog[e].append((None, list(waits.items()), None))

    def emit(self, nc, stack):
        keys = set()
        for e in self.ENGS:
            for (_, waits, inc) in self.prog[e]:
                for k, _v in waits:
                    keys.add(k)
                if inc is not None:
                    keys.add(inc[0])
        sems = {}
        for k in sorted(keys):
            sems[k] = stack.enter_context(nc.semaphore("s_" + k))
        block = stack.enter_context(nc.Block())
        prog = self.prog

        def run(name, eng):
            for (fn, waits, inc) in prog[name]:
                for (k, v) in waits:
                    eng.wait_ge(sems[k], v)
                if fn is not None:
                    ins = fn(eng)
                    if inc is not None:
                        ins.then_inc(sems[inc[0]], inc[1])

        @block.tensor
        def _(e):
            run("pe", e)

        @block.scalar
        def _(e):
            run("act", e)

        @block.vector
        def _(e):
            run("dve", e)

        @block.gpsimd
        def _(e):
            run("pool", e)

        @block.sync
        def _(e):
            run("sp", e)


def _bucket(dist):
    dist = np.maximum(dist, 0)
    me = 16
    lr = np.log(np.maximum(dist, 1).astype(np.float32) / np.float32(me)) / np.float32(math.log(2048 / me))
    large = me + (lr.astype(np.float32) * np.float32(16)).astype(np.int32)
    large = np.minimum(large, 31)
    return np.where(dist < me, dist, large)


def _tables(rel_bias):
    rel_bias = np.asarray(rel_bias, np.float32)
    p = np.arange(128)[:, None]
    c = np.arange(TA_W)[None, :]
    dist = c - 128 - p
    bk = _bucket(dist)
    tabA = np.where(dist[None] >= 0, rel_bias[0:8][:, bk], np.float32(NEG)).astype(np.float32)
    c = np.arange(256)[None, :]
    dm = c - p
    tabB = np.empty((24, 128, 256), np.float32)
    for g, d in enumerate(DIL):
        bk = _bucket(d * dm)
        ok = (dm >= 0) & (dm <= 128)
        tabB[g * 8:(g + 1) * 8] = np.where(ok[None], rel_bias[8 + g * 8: 16 + g * 8][:, bk], np.float32(NEG))
    bk = _bucket(dm)
    ok = (dm >= 0) & (dm <= 127)
    tabC = np.where(ok[None], rel_bias[32:40][:, bk], np.float32(NEG)).astype(np.float32)
    return tabA, tabB, tabC


def _consts():
    cb = np.zeros((128, 384), np.float32)
    cb[64, 256:320] = 0.5
    cb[0, 320:384] = 0.5
    cb[:, 0:128] = np.eye(128, dtype=np.float32)
    cb[0:64, 128:192] = 1.0 / 64
    cb[64:128, 192:256] = 1.0 / 64
    cf = np.zeros((128, 256), np.float32)
    cf[64, 0:64] = 0.5
    cf[0, 64:128] = 0.5
    cm = np.zeros((16, 8), np.float32)
    for t in range(16):
        for j in range(8):
            cm[t, j] = 0.0 if j < (t // 2) else -1e30
    cf[:, 128:256] = cm.reshape(1, 128)
    return cb.astype(ml_dtypes.bfloat16), cf


def _smallf(ln_g, qk_g, sinks):
    nl = ln_g.shape[0]
    out = np.zeros((128, nl * 22 + 1), np.float32)
    out[:, nl * 22] = EPS
    for l in range(nl):
        out[:, l * 22: l * 22 + 8] = np.asarray(ln_g[l], np.float32).reshape(8, 128).T
        for j in range(6):
            out[:, l * 22 + 8 + j] = np.tile(np.asarray(qk_g[l, j], np.float32), 2)
        out[:, l * 22 + 14: l * 22 + 22] = np.asarray(sinks[l], np.float32)[None, :]
    return out


class _Stop(Exception):
    pass


class Prog:
    def __init__(self, NL, debug=None):
        self.NL = NL
        self.debug = debug
        nc = self.nc = bass.Bass("TRN2", target_bir_lowering=False)
        self.S = Sched()
        dt = nc.dram_tensor
        self.x_d = dt("x", [S_TOK, D], F32, kind="ExternalInput").ap()
        self.w_in_d = dt("w_in", [NL, D, C_IN], F32, kind="ExternalInput").ap()
        self.w_br_d = dt("w_branch", [NL, 3, 512, D], F32, kind="ExternalInput").ap()
        self.w_out_d = dt("w_out", [NL, D, D], F32, kind="ExternalInput").ap()
        self.tabA_d = dt("tabA", [8, 128, TA_W], F32, kind="ExternalInput").ap()
        self.tabB_d = dt("tabB", [24, 128, 256], F32, kind="ExternalInput").ap()
        self.tabC_d = dt("tabC", [8, 128, 256], F32, kind="ExternalInput").ap()
        self.cb_d = dt("constb", [128, 384], BF16, kind="ExternalInput").ap()
        self.cf_d = dt("constf", [128, 256], F32, kind="ExternalInput").ap()
        self.sm_d = dt("smallf", [128, NL * 22 + 1], F32, kind="ExternalInput").ap()
        self.y_d = dt("y", [S_TOK, D], F32, kind="ExternalOutput").ap()
        if debug:
            self.dbg_d = dt("dbg", [128, 12 * 2048], BF16, kind="ExternalOutput").ap()

    def stop(self, name):
        if self.debug == name:
            raise _Stop()

    def mm(self, out, lhsT, rhs, start, stop, reads, writes):
        self.S.op("pe", lambda e: e.matmul(out, lhsT=lhsT, rhs=rhs, start=start, stop=stop),
                  reads=reads, writes=writes)

    def carve(self, off, nbytes, dtype):
        assert off % 4 == 0 and nbytes % 4 == 0 and off + nbytes <= self.WORK, (off, nbytes)
        ap = self.work[:, off // 2:(off + nbytes) // 2]
        if dtype == F32:
            ap = ap.bitcast(F32)
        return ap

    def sbank(self):
        i = self.sb_i
        self.sb_i = (i + 1) % 4
        return self.banks[i], self.r_bank[i]

    def build(self):
        nc, S, NL = self.nc, self.S, self.NL
        with ExitStack() as st:
            T = lambda name, shape, dtype: st.enter_context(nc.sbuf_tensor(name, shape, dtype))
            self.xres = T("xres", [128, NT, D], F32)
            self.hnT = T("hnT", [128, 8, S_TOK], BF16)
            self.yT = T("yT", [128, 12, S_TOK], BF16)
            self.cb = T("cb", [128, 384], BF16)
            self.cf = T("cf", [128, 256], F32)
            self.sm = T("sm", [128, NL * 22 + 1], F32)
            self.epsc = self.sm[:, NL * 22:NL * 22 + 1]
            self.WORK = 61 * 1024
            self.work = T("work", [128, self.WORK // 2], BF16)
            self.banks = [st.enter_context(nc.psum_tensor("bank%d" % i, [128, 512], F32)) for i in range(8)]
            self.r_bank = [Res("bank%d" % i) for i in range(8)]
            self.sb_i = 0
            self.ident = self.cb[:, 0:128]
            self.bdiag = self.cb[:, 128:256]
            self.r_x = [Res("x%d" % i) for i in range(NT)]
            self.r_hnT = [[Res("hnT%d_%d" % (tg, kc)) for kc in range(8)] for tg in range(4)]
            self.r_yT = [[Res("yT%d_%d" % (ch, c)) for c in range(4)] for ch in range(12)]
            self.r_const = Res("const")
            self.NSTG, self.NSLOT = 2, 5
            self.stg = [self.carve(i * 4096, 4096, F32).rearrange("p (k c) -> p k c", k=8) for i in range(self.NSTG)]
            self.r_stg = [Res("stg%d" % i) for i in range(self.NSTG)]
            self.stg_slot = [S.new_slot() for _ in range(self.NSTG)]
            base = self.NSTG * 4096
            self.wsl = [self.carve(base + i * 2048, 2048, BF16).rearrange("p (k c) -> p k c", k=8)
                        for i in range(self.NSLOT)]
            self.r_wsl = [Res("wsl%d" % i) for i in range(self.NSLOT)]
            self.w_n = 0
            self.W0 = base + self.NSLOT * 2048
            self.misc_slot = S.new_slot()
            self.tab_slots = [S.new_slot(), S.new_slot()]
            self.out_slot = S.new_slot()

            S.dma("sp", lambda e: e.dma_start(out=self.cb[:], in_=self.cb_d[:, :]), self.misc_slot, writes=[self.r_const])
            S.dma("sp", lambda e: e.dma_start(out=self.cf[:], in_=self.cf_d[:, :]), self.misc_slot, writes=[self.r_const])
            S.dma("sp", lambda e: e.dma_start(out=self.sm[:], in_=self.sm_d[:, :]), self.misc_slot, writes=[self.r_const])
            xs = S.new_slot()
            xv = self.x_d.rearrange("(t p) d -> p t d", p=128)
            for i in range(NT):
                S.dma("sp", lambda e, i=i: e.dma_start(out=self.xres[:, i, :], in_=xv[:, i, :]), xs, writes=[self.r_x[i]])
            for i in range(NT):
                self.r_x[i].w = (xs.key, xs.total)
            for l in range(NL):
                sl = self.sm[:, l * 22 + 14: l * 22 + 22]
                S.op("act", lambda e, sl=sl: e.activation(out=sl, in_=sl, func=AF.Exp),
                     reads=[self.r_const], writes=[self.r_const])

            for l in range(NL):
                self.phase0(l)
                S.barrier()
                if self.debug == "p0":
                    break
                try:
                    self.phase1(l)
                except _Stop:
                    pass
                S.barrier()
                if self.debug and l == 0:
                    break
                self.phase2(l)
                S.barrier()

            if self.debug == "p0":
                ds = S.new_slot()
                for ch in range(8):
                    S.dma("sp", lambda e, ch=ch: e.dma_start(out=self.dbg_d[:, ch * 2048:(ch + 1) * 2048],
                                                             in_=self.hnT[:, ch, :]), ds,
                          reads=[self.r_hnT[tg][ch] for tg in range(4)])
            elif self.debug:
                ds = S.new_slot()
                nd = {"A1": 1, "A": 4, "B": 8, "yT": 12}.get(self.debug, 0)
                for ch in range(nd):
                    S.dma("sp", lambda e, ch=ch: e.dma_start(out=self.dbg_d[:, ch * 2048:(ch + 1) * 2048],
                                                             in_=self.yT[:, ch, :]), ds, reads=self.r_yT[ch])
            yv = self.y_d.rearrange("(t p) d -> p t d", p=128)
            for i in range(NT):
                S.dma("sp", lambda e, i=i: e.dma_start(out=yv[:, i, :], in_=self.xres[:, i, :]), self.out_slot,
                      reads=[self.r_x[i]])
            S.barrier()
            S.emit(nc, st)
        return nc

    def wunit(self, srcs, nk=8):
        S = self.S
        i = self.w_n
        self.w_n += 1
        sg, sl = i % self.NSTG, i % self.NSLOT
        stg, slot = self.stg[sg], self.wsl[sl]
        for (src, c0) in srcs:
            nc_ = src.shape[-1]
            S.dma("sp", lambda e, src=src, c0=c0, nc_=nc_: e.dma_start(out=stg[:, 0:nk, c0:c0 + nc_], in_=src),
                  self.stg_slot[sg], writes=[self.r_stg[sg]])
        S.op("pool", lambda e: e.tensor_copy(out=slot[:, 0:nk, :], in_=stg[:, 0:nk, :]),
             reads=[self.r_stg[sg]], writes=[self.r_wsl[sl]])
        return slot, self.r_wsl[sl]

    def win_unit(self, l, col0, ncols=128, dup=False):
        src = self.w_in_d[l, :, col0:col0 + ncols].rearrange("(kc p) c -> p kc c", p=128)
        if dup:
            return self.wunit([(src, 0), (src, ncols)])
        return self.wunit([(src, 0)])

    def phase0(self, l):
        S = self.S
        W0 = self.W0
        junk = self.carve(W0, 2048, BF16)
        hnb = self.carve(W0 + 2048, 8192, BF16).rearrange("p (t d) -> p t d", t=4)
        ss = self.carve(W0 + 10240, 64, F32)
        rstd = self.carve(W0 + 10304, 64, F32)
        r_junk, r_ss, r_rstd = Res(), [Res() for _ in range(NT)], [Res() for _ in range(NT)]
        r_hnb = [Res() for _ in range(4)]
        lng = self.sm[:, l * 22: l * 22 + 8]
        for tg in range(4):
            for tt in range(4):
                i = tg * 4 + tt
                S.op("act", lambda e, i=i: e.activation(out=junk, in_=self.xres[:, i, :], func=AF.Square,
                                                        accum_out=ss[:, i:i + 1]),
                     reads=[self.r_x[i]], writes=[r_junk, r_ss[i]])
                S.op("act", lambda e, i=i: e.activation(out=rstd[:, i:i + 1], in_=ss[:, i:i + 1], func=AF.Sqrt,
                                                        scale=1.0 / D, bias=self.epsc),
                     reads=[r_ss[i], self.r_const], writes=[r_rstd[i]])
                S.op("dve", lambda e, i=i: e.reciprocal(out=rstd[:, i:i + 1], in_=rstd[:, i:i + 1]),
                     reads=[r_rstd[i]], writes=[r_rstd[i]])
                S.op("dve", lambda e, i=i, tt=tt: e.tensor_scalar(out=hnb[:, tt, :], in0=self.xres[:, i, :],
                                                                  scalar1=rstd[:, i:i + 1], scalar2=None,
                                                                  op0=ALU.mult),
                     reads=[self.r_x[i], r_rstd[i]], writes=[r_hnb[tt]])
            for kc in range(8):
                bank, rb = self.sbank()
                pb = bank[:].bitcast(BF16)
                for tt in range(4):
                    S.op("pe", lambda e, pb=pb, tt=tt, kc=kc: e.transpose(pb[:, tt * 128:(tt + 1) * 128],
                                                                         hnb[:, tt, kc * 128:(kc + 1) * 128],
                                                                         self.ident),
                         reads=[r_hnb[tt], self.r_const], writes=[rb])
                dst = self.hnT[:, kc, tg * 512:(tg + 1) * 512]
                if kc % 2 == 0:
                    S.op("act", lambda e, pb=pb, dst=dst, kc=kc: e.activation(out=dst, in_=pb[:, 0:512], func=AF.Copy,
                                                                             scale=lng[:, kc:kc + 1]),
                         reads=[rb, self.r_const], writes=[self.r_hnT[tg][kc]])
                else:
                    S.op("dve", lambda e, pb=pb, dst=dst, kc=kc: e.tensor_scalar(out=dst, in0=pb[:, 0:512],
                                                                                scalar1=lng[:, kc:kc + 1], scalar2=None,
                                                                                op0=ALU.mult),
                         reads=[rb, self.r_const], writes=[self.r_hnT[tg][kc]])

    def p1_layout(self):
        o = self.W0
        L = {}

        def take(name, nbytes):
            nonlocal o
            L[name] = o
            o += nbytes
        take("QT", 4096)
        take("KT", 4096)
        take("G", 4096)
        take("Vp", 6144)
        take("P", 3 * 1024)
        take("sq", 2 * 1024)
        take("rs", 2 * 2048)
        take("tstage", 2 * 2176)
        take("tabBC", 2 * 1024)
        take("rr", 2048)
        take("tt", 2048)
        take("rb16", 2048)
        take("th", 2048)
        take("km", 64)
        take("rank", 1024)
        take("pen", 512)
        assert o <= self.WORK, o
        return L

    def phase1(self, l):
        S = self.S
        L = self.p1_layout()
        c = self.carve
        self.QT = c(L["QT"], 4096, BF16)
        self.KT = c(L["KT"], 4096, BF16)
        self.G = c(L["G"], 4096, BF16)
        self.Vp = c(L["Vp"], 6144, BF16).rearrange("p (t c) -> p t c", t=16)
        self.Pb = [c(L["P"] + i * 1024, 1024, BF16) for i in range(3)]
        self.sqb = [c(L["sq"] + i * 1024, 1024, BF16) for i in range(2)]
        self.rsb = [c(L["rs"] + i * 2048, 2048, F32) for i in range(2)]
        self.tstage = [c(L["tstage"] + i * 2176, 2176, F32) for i in range(2)]
        self.tabBC = [c(L["tabBC"] + i * 1024, 1024, BF16).rearrange("p (h c) -> p h c", h=2) for i in range(2)]
        self.rr = c(L["rr"], 2048, F32)
        self.tt = c(L["tt"], 2048, F32)
        self.rb16 = c(L["rb16"], 2048, BF16).rearrange("p (a n) -> p a n", a=2)
        self.th = c(L["th"], 2048, F32)
        self.km = c(L["km"], 64, BF16)[:, 0:16]
        self.rank = c(L["rank"], 1024, F32)
        self.pen = c(L["pen"], 512, BF16)
        yc = self.yT[:, 8:12, :].rearrange("p a t -> p (a t)")
        self.TA = [yc[:, i * TA_W:(i + 1) * TA_W] for i in range(2)]
        o = 2 * TA_W
        self.penT = yc[:, o:o + 2048]
        o += 2048
        self.gm = yc[:, o:o + 512].bitcast(F32)
        o += 512
        self.cmpb = yc[:, o:o + 1024]
        o += 1024
        self.acc0 = yc[:, 0:4096].bitcast(F32)
        self.acc1 = yc[:, 4096:8192].bitcast(F32)
        self.r_QT = [Res() for _ in range(4)]
        self.r_KT = [Res() for _ in range(4)]
        self.r_G = [Res() for _ in range(4)]
        self.r_Vp = [Res() for _ in range(4)]
        self.r_P = [Res() for _ in range(3)]
        self.r_sq = [Res() for _ in range(2)]
        self.r_rs = [Res() for _ in range(2)]
        self.r_tstage = [Res() for _ in range(2)]
        self.r_tabBC = [Res() for _ in range(2)]
        self.r_TA = [Res() for _ in range(2)]
        self.r_rr, self.r_tt, self.r_th = Res(), Res(), Res()
        self.r_rb16 = Res()
        self.r_misc = Res()
        self.r_acc = Res()
        self.p_i = 0
        self.q_i = 0
        self.ts_i = 0
        self.tb_i = 0
        self.o_i = 0
        S.op("pool", lambda e: e.memset(self.rb16, 0.0), writes=[self.r_rb16])
        S.op("pool", lambda e: e.memset(self.Vp[:, :, 64:128], 0.0), writes=self.r_Vp)
        S.op("pool", lambda e: e.memset(self.Vp[:, :, 64:65], 1.0), writes=self.r_Vp)
        gq = lambda j: self.sm[:, l * 22 + 8 + j: l * 22 + 9 + j]
        for hp in range(4):
            self.load_tabA(hp)
            self.proj_qk(l, A_Q + hp * 128, self.QT, self.r_QT, gq(0), 1)
            self.proj_qk(l, A_K + hp * 128, self.KT, self.r_KT, gq(1), 1)
            self.proj_v(l, A_V + hp * 128, 1)
            self.proj_g(l, A_G + hp * 128)
            if self.debug == "A1a":
                raise _Stop()
            self.moba_pair(l, hp)
            if self.debug == "A1":
                raise _Stop()
        S.barrier()
        if self.debug == "A":
            return
        for hp in range(4):
            for g, d in enumerate(DIL):
                self.load_tabBC(self.tabB_d, g * 8 + 2 * hp)
                self.stop("t1")
                self.proj_qk(l, B_Q + g * 512 + hp * 128, self.QT, self.r_QT, gq(2), d)
                self.proj_qk(l, B_K + g * 512 + hp * 128, self.KT, self.r_KT, gq(3), d)
                self.proj_v(l, B_V + g * 512 + hp * 128, d)
                if g == 0:
                    self.proj_g(l, B_G + hp * 128)
                self.stop("t2")
                self.window_pair(d, first=(g == 0), mode="B")
                self.stop("B%d" % (g + 1))
            self.finish_B(hp)
            self.stop("B4")
        S.barrier()
        if self.debug == "B":
            return
        for kv in range(2):
            self.proj_qk(l, C_K + kv * 64, self.KT, self.r_KT, gq(5), 1, ncols=64, dup=True)
            self.proj_v(l, C_V + kv * 64, 1, ncols=64, dup=True)
            for hq in range(2):
                hp = kv * 2 + hq
                self.load_tabBC(self.tabC_d, 2 * hp)
                self.proj_qk(l, C_Q + hp * 128, self.QT, self.r_QT, gq(4), 1)
                self.proj_g(l, C_G + hp * 128)
                self.window_pair(1, first=True, mode="C", hp=hp, l=l)

    def load_tabA(self, hp):
        S = self.S
        for hh in range(2):
            for pc in range(4):
                i = self.ts_i
                self.ts_i = (i + 1) % 2
                stg = self.tstage[i]
                src = self.tabA_d[2 * hp + hh, :, pc * 544:(pc + 1) * 544]
                S.dma("sp", lambda e, stg=stg, src=src: e.dma_start(out=stg, in_=src), self.tab_slots[i],
                      writes=[self.r_tstage[i]])
                dst = self.TA[hh][:, pc * 544:(pc + 1) * 544]
                S.op("act", lambda e, stg=stg, dst=dst: e.activation(out=dst, in_=stg, func=AF.Exp),
                     reads=[self.r_tstage[i]], writes=[self.r_TA[hh]])

    def load_tabBC(self, tab_d, idx0):
        S = self.S
        b = self.tb_i
        self.tb_i = (b + 1) % 2
        i = self.ts_i
        self.ts_i = (i + 1) % 2
        stg = self.tstage[i][:, 0:512].rearrange("p (h c) -> p h c", h=2)
        src = tab_d[idx0:idx0 + 2, :, :].rearrange("h p c -> p h c")
        S.dma("sp", lambda e: e.dma_start(out=stg, in_=src), self.tab_slots[i], writes=[self.r_tstage[i]])
        S.op("act", lambda e: e.activation(out=self.tabBC[b], in_=stg, func=AF.Exp),
             reads=[self.r_tstage[i]], writes=[self.r_tabBC[b]])
        self.cur_tab = (self.tabBC[b], self.r_tabBC[b])

    def proj_qk(self, l, col0, dst, r_dst, gcol, d, ncols=128, dup=False):
        S = self.S
        W, rW = self.win_unit(l, col0, ncols, dup)
        jobs = []
        for cch in range(4):
            def s0(cch=cch):
                bank, rb = self.sbank()
                for kc in range(8):
                    self.mm(bank[:], W[:, kc, :], self.hnT[:, kc, cch * 512:(cch + 1) * 512], kc == 0, kc == 7,
                            [rW, self.r_hnT[cch][kc]], [rb])
                i = self.q_i
                self.q_i = (i + 1) % 2
                S.op("act", lambda e: e.activation(out=self.sqb[i], in_=bank[:], func=AF.Square),
                     reads=[rb], writes=[self.r_sq[i]])
                return bank, rb, i

            def s1(state, cch=cch):
                bank, rb, i = state
                bank2, rb2 = self.sbank()
                self.mm(bank2[:], self.bdiag, self.sqb[i], True, True, [self.r_sq[i], self.r_const], [rb2])
                S.op("act", lambda e: e.activation(out=self.rsb[i], in_=bank2[:], func=AF.Sqrt, bias=self.epsc),
                     reads=[rb2, self.r_const], writes=[self.r_rs[i]])
                S.op("dve", lambda e: e.reciprocal(out=self.rsb[i], in_=self.rsb[i]),
                     reads=[self.r_rs[i]], writes=[self.r_rs[i]])
                if d == 1:
                    o_ap = dst[:, cch * 512:(cch + 1) * 512]
                    i0, i1 = bank[:], self.rsb[i]
                    wr = [r_dst[cch]]
                else:
                    na = 512 // d
                    o_ap = dst.rearrange("p (r m) -> p r m", r=d)[:, :, cch * na:(cch + 1) * na]
                    i0 = bank[:].rearrange("p (a r) -> p r a", r=d)
                    i1 = self.rsb[i].rearrange("p (a r) -> p r a", r=d)
                    wr = r_dst
                S.op("dve", lambda e: e.scalar_tensor_tensor(out=o_ap, in0=i0, scalar=gcol, in1=i1,
                                                             op0=ALU.mult, op1=ALU.mult),
                     reads=[rb, self.r_rs[i], self.r_const], writes=wr)
            jobs.append((s0, s1))
        self.pipeline(jobs, 1)

    def pipeline(self, jobs, lag):
        st = {}
        n = len(jobs)
        for t in range(n + lag):
            if t < n:
                st[t] = jobs[t][0]()
            if t - lag >= 0:
                jobs[t - lag][1](st.pop(t - lag))

    def proj_g(self, l, col0):
        S = self.S
        W, rW = self.win_unit(l, col0)
        for cch in range(4):
            bank, rb = self.sbank()
            for kc in range(8):
                self.mm(bank[:], W[:, kc, :], self.hnT[:, kc, cch * 512:(cch + 1) * 512], kc == 0, kc == 7,
                        [rW, self.r_hnT[cch][kc]], [rb])
            S.op("act", lambda e, bank=bank: e.activation(out=self.th, in_=bank[:], func=AF.Tanh, scale=0.5),
                 reads=[rb], writes=[self.r_th])
            dst = self.G[:, cch * 512:(cch + 1) * 512]
            S.op("dve", lambda e, bank=bank, dst=dst: e.scalar_tensor_tensor(out=dst, in0=self.th, scalar=1.0, in1=bank[:],
                                                                             op0=ALU.add, op1=ALU.mult),
                 reads=[rb, self.r_th], writes=[self.r_G[cch]])

    def proj_v(self, l, col0, d, ncols=128, dup=False):
        S = self.S
        W, rW = self.win_unit(l, col0, ncols, dup)
        Ls = S_TOK // d
        for tq in range(4):
            bank, rb = self.sbank()
            for tt in range(4):
                j = tq * 4 + tt
                u0 = j * 128
                r, m0 = u0 // Ls, u0 % Ls
                start = m0 * d + r
                for kc in range(8):
                    lhsT = self.hnT[:, kc, start: start + 127 * d + 1: d]
                    reads = [rW] + [self.r_hnT[cc][kc] for cc in range(start // 512, (start + 127 * d) // 512 + 1)]
                    self.mm(bank[:, tt * 128:(tt + 1) * 128], lhsT, W[:, kc, :], kc == 0, kc == 7, reads, [rb])
            bv = bank[:].rearrange("p (t c) -> p t c", t=4)
            S.op("act", lambda e, bv=bv, tq=tq: e.activation(out=self.Vp[:, tq * 4:(tq + 1) * 4, 0:64], in_=bv[:, :, 0:64],
                                                             func=AF.Copy),
                 reads=[rb], writes=[self.r_Vp[tq]])
            S.op("dve", lambda e, bv=bv, tq=tq: e.tensor_copy(out=self.Vp[:, tq * 4:(tq + 1) * 4, 128:192],
                                                              in_=bv[:, :, 64:128]),
                 reads=[rb], writes=[self.r_Vp[tq]])

    def normalize(self, o0, o1, r_o0, r_o1, N, gcols, ych, ycols, r_y, sink=None, from_sbuf=False):
        S = self.S
        rr, tt = self.rr, self.tt
        if sink is not None:
            e0, e1 = sink
            S.op("dve", lambda e: e.tensor_scalar(out=rr[64:65, 0:N], in0=o0[64:65, 0:N], scalar1=e0, scalar2=None,
                                                  op0=ALU.add), reads=[r_o0, self.r_const], writes=[self.r_rr])
            S.op("dve", lambda e: e.reciprocal(out=rr[64:65, 0:N], in_=rr[64:65, 0:N]), reads=[self.r_rr], writes=[self.r_rr])
            S.op("dve", lambda e: e.tensor_scalar(out=rr[0:1, 0:N], in0=o1[0:1, 0:N], scalar1=e1, scalar2=None,
                                                  op0=ALU.add), reads=[r_o1, self.r_const], writes=[self.r_rr])
            S.op("dve", lambda e: e.reciprocal(out=rr[0:1, 0:N], in_=rr[0:1, 0:N]), reads=[self.r_rr], writes=[self.r_rr])
        else:
            S.op("dve", lambda e: e.reciprocal(out=rr[64:65, 0:N], in_=o0[64:65, 0:N]), reads=[r_o0], writes=[self.r_rr])
            S.op("dve", lambda e: e.reciprocal(out=rr[0:1, 0:N], in_=o1[0:1, 0:N]), reads=[r_o1], writes=[self.r_rr])
        self.stop("h4")
        rb16 = self.rb16
        for row in (64, 0):
            S.op("dve", lambda e, row=row: e.tensor_copy(out=rb16[row:row + 1, 0, 0:N], in_=rr[row:row + 1, 0:N]),
                 reads=[self.r_rr], writes=[self.r_rb16])
            S.op("dve", lambda e, row=row: e.tensor_tensor(out=rb16[row:row + 1, 1, 0:N], in0=rr[row:row + 1, 0:N],
                                                           in1=rb16[row:row + 1, 0, 0:N], op=ALU.subtract),
                 reads=[self.r_rr, self.r_rb16], writes=[self.r_rb16])
        self.stop("h5")
        bank, rb = self.sbank()
        for a in range(2):
            self.mm(bank[:, 0:N], self.cb[:, 256:384], rb16[:, a, 0:N], a == 0, a == 1,
                    [self.r_rb16, self.r_const], [rb])
        self.stop("h6")
        S.op("dve", lambda e: e.tensor_tensor(out=tt[:, 0:N], in0=gcols, in1=bank[:, 0:N], op=ALU.mult),
             reads=[rb] + self.r_G, writes=[self.r_tt])
        self.stop("h7")
        S.op("dve", lambda e: e.tensor_tensor(out=self.yT[0:64, ych, ycols], in0=tt[0:64, 0:N], in1=o0[0:64, 0:N],
                                              op=ALU.mult), reads=[self.r_tt, r_o0], writes=r_y)
        S.op("dve", lambda e: e.tensor_tensor(out=self.yT[64:128, ych, ycols], in0=tt[64:128, 0:N], in1=o1[64:128, 0:N],
                                              op=ALU.mult), reads=[self.r_tt, r_o1], writes=r_y)

    def obanks(self):
        i = self.o_i
        self.o_i = (i + 1) % 2
        return (self.banks[4 + 2 * i], self.r_bank[4 + 2 * i], self.banks[5 + 2 * i], self.r_bank[5 + 2 * i])

    def moba_pair(self, l, hp):
        S = self.S
        QT, KT, Vp = self.QT, self.KT, self.Vp
        def km_fn(e):
            with self.nc.allow_low_precision("block key sums only rank MoBA blocks; bf16 matmul operand"):
                return e.tensor_reduce(out=self.km[:, 0:8], in_=KT.rearrange("p (j k) -> p j k", j=8),
                                       axis=AX.X, op=ALU.add)
        S.op("dve", km_fn, reads=self.r_KT, writes=[self.r_misc])
        self.stop("g1")
        gm = self.gm.rearrange("p (t h j) -> p t h j", t=16, h=2)
        cm = self.cf[:, 128:256].rearrange("p (t j) -> p t j", t=16)
        for hh in range(2):
            gbank, rgb = self.sbank()
            gv = gbank[:, 0:128].rearrange("p (t j) -> p t j", t=16)
            for t in range(16):
                self.mm(gv[:, t, :], QT[hh * 64:(hh + 1) * 64, t * 128:(t + 1) * 128],
                        self.km[hh * 64:(hh + 1) * 64, 0:8], True, True,
                        [self.r_QT[t // 4], self.r_misc], [rgb])
            S.op("dve", lambda e, gv=gv, hh=hh: e.tensor_tensor(out=gm[:, :, hh, :], in0=gv, in1=cm, op=ALU.add),
                 reads=[rgb, self.r_const], writes=[self.r_misc])
        self.stop("g3")
        g3 = self.gm.rearrange("p (a j) -> p a j", j=8)
        rank = self.rank.rearrange("p (a j) -> p a j", j=8)
        pen = self.pen.rearrange("p (a j) -> p a j", j=8)
        for half in range(2):
            a0 = half * 16
            cmpv = self.cmpb.rearrange("p (a j k) -> p a j k", a=16, j=8)
            in0 = g3[:, a0:a0 + 16, :].unsqueeze(2).broadcast_to([128, 16, 8, 8])
            in1 = g3[:, a0:a0 + 16, :].unsqueeze(3).broadcast_to([128, 16, 8, 8])
            S.op("dve", lambda e, in0=in0, in1=in1, cmpv=cmpv: e.tensor_tensor(out=cmpv, in0=in0, in1=in1, op=ALU.is_gt),
                 reads=[self.r_misc], writes=[self.r_misc])
            S.op("dve", lambda e, cmpv=cmpv, a0=a0: e.tensor_reduce(out=rank[:, a0:a0 + 16, :], in_=cmpv, axis=AX.X, op=ALU.add),
                 reads=[self.r_misc], writes=[self.r_misc])
        self.stop("g4")
        S.op("dve", lambda e: e.tensor_scalar(out=self.pen, in0=self.rank, scalar1=2.5, scalar2=-240000.0,
                                              op0=ALU.is_gt, op1=ALU.mult), reads=[self.r_misc], writes=[self.r_misc])
        self.stop("g5")
        pent = self.pen.rearrange("p (t c) -> p t c", t=16)
        r_penT = Res()
        S.op("pool", lambda e: e.memset(self.penT, 0.0), writes=[r_penT])
        for half in range(2):
            bank, rb = self.sbank()
            pb = bank[:].bitcast(BF16)
            for tt_ in range(8):
                t = half * 8 + tt_
                S.op("pe", lambda e, pb=pb, tt_=tt_, t=t: e.transpose(pb[0:16, tt_ * 128:(tt_ + 1) * 128], pent[:, t, :],
                                                                       self.ident),
                     reads=[self.r_misc, self.r_const], writes=[rb])
            if self.debug == "g6":
                continue
            S.op("dve", lambda e, pb=pb, half=half: e.tensor_copy(out=self.penT[0:16, half * 1024:(half + 1) * 1024],
                                                                  in_=pb[0:16, 0:1024]),
                 reads=[rb], writes=[r_penT])
        if self.debug in ("A1b", "g6"):
            raise _Stop()
        for n in range(8):
            if self.debug == "A1c" and n == 1:
                raise _Stop()
            O0b, rO0, O1b, rO1 = self.obanks()
            q0 = 256 * n
            for hh in range(2):
                ps = slice(hh * 64, (hh + 1) * 64)
                Ob, rO = (O0b, rO0) if hh == 0 else (O1b, rO1)
                Oview = Ob[0:65, 0:256] if hh == 0 else Ob[0:128, 0:256]
                lc = (lambda kt: Vp[:, kt, 0:65]) if hh == 0 else (lambda kt: Vp[:, kt, 64:192])
                npairs = n + 1
                jobs = []
                for jp in range(npairs):
                    a = 2 * jp
                    past = jp < n

                    def s0(a=a, past=past, jp=jp, ps=ps, hh=hh):
                        bank, rb = self.sbank()
                        rhs = QT[ps, q0:q0 + 256]
                        rq = [self.r_QT[q0 // 512]]
                        for hf, kt in enumerate((a + 1, a)):
                            o = bank[:, hf * 256:(hf + 1) * 256]
                            if past:
                                cidx = hh * 8 + jp
                                lp = self.cb[:, cidx:cidx + 1].broadcast_to([128, 128])
                                self.mm(o, lp, self.penT[:, q0:q0 + 256], True, False, [r_penT, self.r_const], [rb])
                            self.mm(o, KT[ps, kt * 128:(kt + 1) * 128], rhs, not past, True,
                                    rq + [self.r_KT[kt // 4]], [rb])
                        i = self.p_i
                        self.p_i = (i + 1) % 3
                        P = self.Pb[i]
                        S.op("act", lambda e: e.activation(out=P, in_=bank[:], func=AF.Exp, scale=0.125),
                             reads=[rb], writes=[self.r_P[i]])
                        self.stop("h1")
                        c0 = 256 * n - 128 * a + 128
                        base = self.TA[hh][:, c0 - 128:c0 - 127]
                        tv = bass.AP(base.tensor, base.offset, [list(base.ap[0]), [128, 2], [1, 256]])
                        Pv = P.rearrange("p (h c) -> p h c", h=2)
                        S.op("dve", lambda e: e.tensor_tensor(out=Pv, in0=Pv, in1=tv, op=ALU.mult),
                             reads=[self.r_TA[hh], self.r_P[i]], writes=[self.r_P[i]])
                        self.stop("h2")
                        return i

                    def s1(i, a=a, jp=jp, npairs=npairs, Oview=Oview, rO=rO, lc=lc):
                        P = self.Pb[i]
                        for hf, kt in enumerate((a + 1, a)):
                            self.mm(Oview, lc(kt), P[:, hf * 256:(hf + 1) * 256], jp == 0 and hf == 0,
                                    jp == npairs - 1 and hf == 1, [self.r_P[i], self.r_Vp[kt // 4]], [rO])
                    jobs.append((s0, s1))
                self.pipeline(jobs, 2)
                self.stop("h3")
            self.normalize(O0b, O1b, rO0, rO1, 256, self.G[:, q0:q0 + 256], hp, slice(q0, q0 + 256),
                           [self.r_yT[hp][q0 // 512]])

    def window_pair(self, d, first, mode, hp=None, l=None):
        S = self.S
        QT, KT, Vp = self.QT, self.KT, self.Vp
        tab, r_tab = self.cur_tab
        tps = NT // d
        osets = [(self.banks[4], self.r_bank[4], self.banks[5], self.r_bank[5]),
                 (self.banks[6], self.r_bank[6], self.banks[7], self.r_bank[7])]

        def pv(i, kt, qi, start, stop):
            qt = kt + qi
            O0b, rO0, O1b, rO1 = osets[(qt // 4) % 2]
            cs = slice((qt % 4) * 128, (qt % 4) * 128 + 128)
            P0 = self.Pb[i][:, qi * 128: qi * 128 + 128]
            P1 = self.Pb[i][:, 256 + qi * 128: 256 + qi * 128 + 128]
            self.mm(O0b[0:65, cs], Vp[:, kt, 0:65], P0, start, stop, [self.r_P[i], self.r_Vp[kt // 4]], [rO0])
            self.mm(O1b[0:128, cs], Vp[:, kt, 64:192], P1, start, stop, [self.r_P[i], self.r_Vp[kt // 4]], [rO1])

        jobs = []
        for kt in range(NT):
            first_in_sub = (kt % tps) == 0
            last_in_sub = (kt % tps) == tps - 1
            nq = 1 if last_in_sub else 2
            N = 128 * nq

            def s0(kt=kt, N=N):
                rq = [self.r_QT[kt // 4], self.r_QT[(kt * 128 + N - 1) // 512]]
                i = self.p_i
                self.p_i = (i + 1) % 3
                Pv = self.Pb[i].rearrange("p (h c) -> p h c", h=2)[:, :, 0:N]
                for hh in range(2):
                    bank, rb = self.sbank()
                    ps = slice(hh * 64, (hh + 1) * 64)
                    self.mm(bank[:, 0:N], KT[ps, kt * 128:(kt + 1) * 128],
                            QT[ps, kt * 128: kt * 128 + N], True, True, [self.r_KT[kt // 4]] + rq, [rb])
                    S.op("act", lambda e, bank=bank, hh=hh: e.activation(out=self.Pb[i][:, hh * 256: hh * 256 + N],
                                                                         in_=bank[:, 0:N], func=AF.Exp, scale=0.125),
                         reads=[rb], writes=[self.r_P[i]])
                tv = tab[:, :, 0:N]
                S.op("dve", lambda e: e.tensor_tensor(out=Pv, in0=Pv, in1=tv, op=ALU.mult),
                     reads=[r_tab, self.r_P[i]], writes=[self.r_P[i]])
                self.stop("w1")
                return i

            def s1(i, kt=kt, nq=nq, first_in_sub=first_in_sub):
                pv(i, kt, 0, first_in_sub, True)
                if nq == 2:
                    pv(i, kt, 1, True, False)
                self.stop("w2")
                if kt % 4 == 3:
                    if self.debug == "w3":
                        raise _Stop()
                    self.consume_group(kt // 4, d, first, mode, osets[(kt // 4) % 2], hp, l)
                    self.stop("w4")
            jobs.append((s0, s1))
        self.pipeline(jobs, 1)

    def consume_group(self, tq, d, first, mode, ob, hp, l):
        S = self.S
        O0b, rO0, O1b, rO1 = ob
        if mode == "C":
            q0 = tq * 512
            e0 = self.sm[64:65, l * 22 + 14 + 2 * hp: l * 22 + 15 + 2 * hp]
            e1 = self.sm[0:1, l * 22 + 15 + 2 * hp: l * 22 + 16 + 2 * hp]
            self.normalize(O0b, O1b, rO0, rO1, 512, self.G[:, q0:q0 + 512], 8 + hp, slice(q0, q0 + 512),
                           [self.r_yT[8 + hp][tq]], sink=(e0, e1))
            return
        Ls = S_TOK // d
        if d == 1:
            dst0 = self.acc0[0:65, tq * 512:(tq + 1) * 512]
            dst1 = self.acc1[:, tq * 512:(tq + 1) * 512]
            src0, src1 = O0b[0:65, :], O1b[:, :]
        elif d == 4:
            dst0 = self.acc0[0:65, tq:tq + 4 * 511 + 1:4]
            dst1 = self.acc1[:, tq:tq + 4 * 511 + 1:4]
            src0, src1 = O0b[0:65, :], O1b[:, :]
        else:
            dst0 = self.acc0[0:65, :].rearrange("p (m r) -> p r m", r=16)[:, 4 * tq:4 * tq + 4, :]
            dst1 = self.acc1[:, :].rearrange("p (m r) -> p r m", r=16)[:, 4 * tq:4 * tq + 4, :]
            src0 = O0b[0:65, :].rearrange("p (r m) -> p r m", r=4)
            src1 = O1b[:, :].rearrange("p (r m) -> p r m", r=4)
        if first:
            S.op("act", lambda e: e.activation(out=dst0, in_=src0, func=AF.Copy), reads=[rO0], writes=[self.r_acc])
            S.op("dve", lambda e: e.tensor_copy(out=dst1, in_=src1), reads=[rO1], writes=[self.r_acc])
        else:
            S.op("dve", lambda e: e.tensor_tensor(out=dst0, in0=dst0, in1=src0, op=ALU.add), reads=[rO0, self.r_acc],
                 writes=[self.r_acc])
            S.op("dve", lambda e: e.tensor_tensor(out=dst1, in0=dst1, in1=src1, op=ALU.add), reads=[rO1, self.r_acc],
                 writes=[self.r_acc])

    def finish_B(self, hp):
        for cch in range(4):
            q0 = cch * 512
            self.normalize(self.acc0[:, q0:q0 + 512], self.acc1[:, q0:q0 + 512], self.r_acc, self.r_acc, 512,
                           self.G[:, q0:q0 + 512], 4 + hp, slice(q0, q0 + 512), [self.r_yT[4 + hp][cch]])

    def phase2(self, l):
        S = self.S
        o = self.W0
        mT = self.carve(o, 16384, BF16).rearrange("p (k t) -> p k t", k=8)
        o += 16384
        acc = self.carve(o, 4096, F32)
        o += 4096
        th = [self.carve(o + i * 2048, 2048, F32) for i in range(2)]
        o += 4096
        wo = self.carve(o, 8192, BF16).rearrange("p (k e) -> p k e", k=8)
        o += 8192
        assert o <= self.WORK
        r_mT = [[Res() for _ in range(2)] for _ in range(8)]
        r_acc = [Res() for _ in range(2)]
        r_th = [Res() for _ in range(2)]
        r_wo = Res()
        th_i = 0
        for half in range(2):
            t0 = half * 1024
            for dc in range(8):
                for i in range(3):
                    Wg, rWg = self.win_unit(l, GATE0 + i * 1024 + dc * 128)
                    srcb = self.w_br_d[l, i, :, dc * 128:(dc + 1) * 128].rearrange("(wc p) c -> p wc c", p=128)
                    Wb, rWb = self.wunit([(srcb, 0)], nk=4)
                    for cc in range(2):
                        tok = slice(t0 + cc * 512, t0 + (cc + 1) * 512)
                        cg = (t0 + cc * 512) // 512
                        bg, rbg = self.sbank()
                        for kc in range(8):
                            self.mm(bg[:], Wg[:, kc, :], self.hnT[:, kc, tok], kc == 0, kc == 7,
                                    [rWg, self.r_hnT[cg][kc]], [rbg])
                        bb, rbb = self.sbank()
                        for wc in range(4):
                            self.mm(bb[:], Wb[:, wc, :], self.yT[:, i * 4 + wc, tok], wc == 0, wc == 3,
                                    [rWb, self.r_yT[i * 4 + wc][cg]], [rbb])
                        k = th_i
                        th_i = (k + 1) % 2
                        S.op("act", lambda e, bg=bg, k=k: e.activation(out=th[k], in_=bg[:], func=AF.Tanh, scale=0.5),
                             reads=[rbg], writes=[r_th[k]])
                        av = acc[:, cc * 512:(cc + 1) * 512]
                        if i == 0:
                            S.op("dve", lambda e, bb=bb, k=k, av=av: e.scalar_tensor_tensor(
                                out=av, in0=th[k], scalar=1.0, in1=bb[:], op0=ALU.add, op1=ALU.mult),
                                reads=[rbb, r_th[k]], writes=[r_acc[cc]])
                        else:
                            S.op("dve", lambda e, bb=bb, k=k: e.scalar_tensor_tensor(
                                out=th[k], in0=th[k], scalar=1.0, in1=bb[:], op0=ALU.add, op1=ALU.mult),
                                reads=[rbb, r_th[k]], writes=[r_th[k]])
                            if i == 1:
                                S.op("dve", lambda e, k=k, av=av: e.tensor_tensor(out=av, in0=av, in1=th[k], op=ALU.add),
                                     reads=[r_th[k], r_acc[cc]], writes=[r_acc[cc]])
                            else:
                                mv = mT[:, dc, cc * 512:(cc + 1) * 512]
                                S.op("dve", lambda e, k=k, av=av, mv=mv: e.tensor_tensor(out=mv, in0=av, in1=th[k], op=ALU.add),
                                     reads=[r_th[k], r_acc[cc]], writes=[r_mT[dc][cc]])
            for eh in range(2):
                for ec in range(4):
                    src = self.w_out_d[l, :, eh * 512 + ec * 128: eh * 512 + (ec + 1) * 128].rearrange("(kc p) c -> p kc c", p=128)
                    Wu, rWu = self.wunit([(src, 0)])
                    S.op("pool", lambda e, Wu=Wu, ec=ec: e.tensor_copy(out=wo[:, :, ec * 128:(ec + 1) * 128], in_=Wu[:, :, :]),
                         reads=[rWu], writes=[r_wo])
                for tt in range(8):
                    ti = half * 8 + tt
                    bank, rb = self.sbank()
                    for dc in range(8):
                        self.mm(bank[:], mT[:, dc, tt * 128:(tt + 1) * 128], wo[:, dc, :], dc == 0, dc == 7,
                                [r_mT[dc][tt // 4], r_wo], [rb])
                    xv = self.xres[:, ti, eh * 512:(eh + 1) * 512]
                    S.op("dve", lambda e, bank=bank, xv=xv: e.scalar_tensor_tensor(out=xv, in0=bank[:], scalar=0.5, in1=xv,
                                                                                  op0=ALU.mult, op1=ALU.add),
                         reads=[rb, self.r_x[ti]], writes=[self.r_x[ti]])


_PROG_CACHE = {}


def _get_prog(NL, debug=None):
    key = (NL, debug)
    if key not in _PROG_CACHE:
        p = Prog(NL, debug)
        p.build()
        _PROG_CACHE[key] = p
    return _PROG_CACHE[key]


FUSED = True


def kernel(x, ln_g, w_in, qk_g, sinks, w_branch, w_out, rel_bias):
    x = np.ascontiguousarray(np.asarray(x, np.float32))
    ln_g = np.asarray(ln_g, np.float32)
    w_in = np.ascontiguousarray(np.asarray(w_in, np.float32))
    qk_g = np.asarray(qk_g, np.float32)
    sinks = np.asarray(sinks, np.float32)
    w_branch = np.ascontiguousarray(np.asarray(w_branch, np.float32))
    w_out = np.ascontiguousarray(np.asarray(w_out, np.float32))
    tabA, tabB, tabC = _tables(rel_bias)
    cb, cf = _consts()
    depth = w_in.shape[0]
    ncores = x.shape[0]
    groups = [list(range(depth))] if FUSED else [[l] for l in range(depth)]
    cur = [x[b] for b in range(ncores)]
    for ls in groups:
        prog = _get_prog(len(ls))
        sl = slice(ls[0], ls[-1] + 1)
        common = {"w_in": w_in[sl], "w_branch": w_branch[sl], "w_out": w_out[sl], "tabA": tabA, "tabB": tabB,
                  "tabC": tabC, "constb": cb, "constf": cf, "smallf": _smallf(ln_g[sl], qk_g[sl], sinks[sl])}
        in_maps = [dict(common, x=cur[b]) for b in range(ncores)]
        res = run_bass_kernel_spmd(prog.nc, in_maps, core_ids=list(range(ncores)))
        cur = [np.asarray(res.results[b]["y"], np.float32) for b in range(ncores)]
    return np.stack(cur, axis=0)
```

```python
import math
from contextlib import ExitStack

import numpy as np
import ml_dtypes
import concourse.bass as bass
import concourse.mybir as mybir
from concourse.bass_utils import run_bass_kernel_spmd

F32 = mybir.dt.float32
BF16 = mybir.dt.bfloat16
AF = mybir.ActivationFunctionType
ALU = mybir.AluOpType
AX = mybir.AxisListType

S_TOK = 2048
D = 1024
NT = 16
C_IN = 11520
NEG = -30000.0
EPS = 1e-6
A_Q, A_K, A_V, A_G = 0, 512, 1024, 1536
B_Q, B_K, B_V, B_G = 2048, 3584, 5120, 6656
C_Q, C_K, C_V, C_G = 7168, 7680, 7808, 7936
GATE0 = 8448
DIL = (1, 4, 16)
TA_W = 2176


class Res:
    __slots__ = ("name", "w", "r")

    def __init__(self, name=""):
        self.name = name
        self.w = None
        self.r = {}


class DmaSlot:
    __slots__ = ("key", "total")

    def __init__(self, key):
        self.key = key
        self.total = 0


class Sched:
    ENGS = ("pe", "act", "dve", "pool", "sp")

    def __init__(self):
        self.prog = {e: [] for e in self.ENGS}
        self.count = {e: 0 for e in self.ENGS}
        self.seen = {e: {} for e in self.ENGS}
        self.slots = []

    def new_slot(self):
        s = DmaSlot("d%03d" % len(self.slots))
        self.slots.append(s)
        return s

    def _deps(self, eng, reads, writes):
        waits = {}
        seen = self.seen[eng]

        def add(tok):
            key, val = tok
            if key == eng and eng in ("pe", "sp"):
                return
            if seen.get(key, 0) >= val:
                return
            if waits.get(key, 0) < val:
                waits[key] = val

        for r in reads:
            if r.w is not None:
                add(r.w)
        for w in writes:
            if w.w is not None:
                add(w.w)
            for k, v in w.r.items():
                add((k, v))
        for k, v in waits.items():
            seen[k] = v
        return list(waits.items())

    def _mark(self, tok, reads, writes):
        for r in reads:
            if r.r.get(tok[0], 0) < tok[1]:
                r.r[tok[0]] = tok[1]
        for w in writes:
            w.w = tok
            w.r = {}

    def op(self, eng, fn, reads=(), writes=()):
        waits = self._deps(eng, reads, writes)
        self.count[eng] += 1
        tok = (eng, self.count[eng])
        self.prog[eng].append((fn, waits, (eng, 1)))
        self._mark(tok, reads, writes)
        return tok

    def dma(self, q, fn, slot, reads=(), writes=()):
        waits = self._deps(q, reads, writes)
        slot.total += 16
        tok = (slot.key, slot.total)
        self.prog[q].append((fn, waits, (slot.key, 16)))
        self._mark(tok, reads, writes)
        return tok

    def barrier(self):
        toks = [(e, self.count[e]) for e in self.ENGS if self.count[e] > 0]
        toks += [(s.key, s.total) for s in self.slots if s.total > 0]
        for e in self.ENGS:
            waits = {}
            for key, val in toks:
                if key == e and e in ("pe", "sp"):
                    continue
                if self.seen[e].get(key, 0) >= val:
                    continue
                waits[key] = val
                self.seen[e][key] = val
            if waits:
                self.prog[e].append((None, list(waits.items()), None))

    def emit(self, nc, stack):
        keys = set()
        for e in self.ENGS:
            for (_, waits, inc) in self.prog[e]:
                for k, _v in waits:
                    keys.add(k)
                if inc is not None:
                    keys.add(inc[0])
        sems = {}
        for k in sorted(keys):
            sems[k] = stack.enter_context(nc.semaphore("s_" + k))
        block = stack.enter_context(nc.Block())
        prog = self.prog

        def run(name, eng):
            for (fn, waits, inc) in prog[name]:
                for (k, v) in waits:
                    eng.wait_ge(sems[k], v)
                if fn is not None:
                    ins = fn(eng)
                    if inc is not None:
                        ins.then_inc(sems[inc[0]], inc[1])

        @block.tensor
        def _(e):
            run("pe", e)

        @block.scalar
        def _(e):
            run("act", e)

        @block.vector
        def _(e):
            run("dve", e)

        @block.gpsimd
        def _(e):
            run("pool", e)

        @block.sync
        def _(e):
            run("sp", e)


def _bucket(dist):
    dist = np.maximum(dist, 0)
    me = 16
    lr = np.log(np.maximum(dist, 1).astype(np.float32) / np.float32(me)) / np.float32(math.log(2048 / me))
    large = me + (lr.astype(np.float32) * np.float32(16)).astype(np.int32)
    large = np.minimum(large, 31)
    return np.where(dist < me, dist, large)


def _tables(rel_bias):
    rel_bias = np.asarray(rel_bias, np.float32)
    p = np.arange(128)[:, None]
    c = np.arange(TA_W)[None, :]
    dist = c - 128 - p
    bk = _bucket(dist)
    tabA = np.where(dist[None] >= 0, rel_bias[0:8][:, bk], np.float32(NEG)).astype(np.float32)
    c = np.arange(256)[None, :]
    dm = c - p
    tabB = np.empty((24, 128, 256), np.float32)
    for g, d in enumerate(DIL):
        bk = _bucket(d * dm)
        ok = (dm >= 0) & (dm <= 128)
        tabB[g * 8:(g + 1) * 8] = np.where(ok[None], rel_bias[8 + g * 8: 16 + g * 8][:, bk], np.float32(NEG))
    bk = _bucket(dm)
    ok = (dm >= 0) & (dm <= 127)
    tabC = np.where(ok[None], rel_bias[32:40][:, bk], np.float32(NEG)).astype(np.float32)
    return tabA, tabB, tabC


def _consts():
    cb = np.zeros((128, 384), np.float32)
    cb[64, 256:320] = 1.0
    cb[0, 320:384] = 1.0
    cb[:, 0:128] = np.eye(128, dtype=np.float32)
    cb[0:64, 128:192] = 1.0 / 64
    cb[64:128, 192:256] = 1.0 / 64
    cf = np.zeros((128, 256), np.float32)
    cf[64, 0:64] = 0.5
    cf[0, 64:128] = 0.5
    cm = np.zeros((16, 8), np.float32)
    for t in range(16):
        for j in range(8):
            cm[t, j] = 0.0 if j < (t // 2) else -1e30
    cf[:, 128:256] = cm.reshape(1, 128)
    return cb.astype(ml_dtypes.bfloat16), cf


def _smallf(ln_g, qk_g, sinks):
    nl = ln_g.shape[0]
    out = np.zeros((128, nl * 22 + 2), np.float32)
    out[:, nl * 22] = EPS
    out[:, nl * 22 + 1] = 1.0
    for l in range(nl):
        out[:, l * 22: l * 22 + 8] = np.asarray(ln_g[l], np.float32).reshape(8, 128).T
        for j in range(6):
            out[:, l * 22 + 8 + j] = np.tile(np.asarray(qk_g[l, j], np.float32), 2)
        out[:, l * 22 + 14: l * 22 + 22] = np.asarray(sinks[l], np.float32)[None, :]
    return out


class _Stop(Exception):
    pass


class Prog:
    def __init__(self, NL, debug=None):
        self.NL = NL
        self.debug = debug
        nc = self.nc = bass.Bass("TRN2", target_bir_lowering=False)
        self.S = Sched()
        dt = nc.dram_tensor
        self.x_d = dt("x", [S_TOK, D], F32, kind="ExternalInput").ap()
        self.w_in_d = dt("w_in", [NL, D, C_IN], F32, kind="ExternalInput").ap()
        self.w_br_d = dt("w_branch", [NL, 3, 512, D], F32, kind="ExternalInput").ap()
        self.w_out_d = dt("w_out", [NL, D, D], F32, kind="ExternalInput").ap()
        self.tabA_d = dt("tabA", [8, 128, TA_W], F32, kind="ExternalInput").ap()
        self.tabB_d = dt("tabB", [24, 128, 256], F32, kind="ExternalInput").ap()
        self.tabC_d = dt("tabC", [8, 128, 256], F32, kind="ExternalInput").ap()
        self.cb_d = dt("constb", [128, 384], BF16, kind="ExternalInput").ap()
        self.cf_d = dt("constf", [128, 256], F32, kind="ExternalInput").ap()
        self.sm_d = dt("smallf", [128, NL * 22 + 2], F32, kind="ExternalInput").ap()
        self.y_d = dt("y", [S_TOK, D], F32, kind="ExternalOutput").ap()
        if debug:
            self.dbg_d = dt("dbg", [128, 12 * 2048], BF16, kind="ExternalOutput").ap()

    def stop(self, name):
        if self.debug == name:
            raise _Stop()

    def mm(self, out, lhsT, rhs, start, stop, reads, writes):
        self.S.op("pe", lambda e: e.matmul(out, lhsT=lhsT, rhs=rhs, start=start, stop=stop),
                  reads=reads, writes=writes)

    def carve(self, off, nbytes, dtype):
        assert off % 4 == 0 and nbytes % 4 == 0 and off + nbytes <= self.WORK, (off, nbytes)
        ap = self.work[:, off // 2:(off + nbytes) // 2]
        if dtype == F32:
            ap = ap.bitcast(F32)
        return ap

    def sbank(self):
        i = self.sb_i
        self.sb_i = (i + 1) % 4
        return self.banks[i], self.r_bank[i]

    def build(self):
        nc, S, NL = self.nc, self.S, self.NL
        with ExitStack() as st:
            T = lambda name, shape, dtype: st.enter_context(nc.sbuf_tensor(name, shape, dtype))
            self.xres = T("xres", [128, NT, D], F32)
            self.hnT = T("hnT", [128, 8, S_TOK], BF16)
            self.yT = T("yT", [128, 12, S_TOK], BF16)
            self.cb = T("cb", [128, 384], BF16)
            self.cf = T("cf", [128, 256], F32)
            self.sm = T("sm", [128, NL * 22 + 2], F32)
            self.epsc = self.sm[:, NL * 22:NL * 22 + 1]
            self.onec = self.sm[:, NL * 22 + 1:NL * 22 + 2]
            self.WORK = 61 * 1024
            self.work = T("work", [128, self.WORK // 2], BF16)
            self.banks = [st.enter_context(nc.psum_tensor("bank%d" % i, [128, 512], F32)) for i in range(8)]
            self.r_bank = [Res("bank%d" % i) for i in range(8)]
            self.sb_i = 0
            self.ident = self.cb[:, 0:128]
            self.bdiag = self.cb[:, 128:256]
            self.r_x = [Res("x%d" % i) for i in range(NT)]
            self.r_hnT = [[Res("hnT%d_%d" % (tg, kc)) for kc in range(8)] for tg in range(4)]
            self.r_yT = [[Res("yT%d_%d" % (ch, c)) for c in range(4)] for ch in range(12)]
            self.r_const = Res("const")
            self.NSTG, self.NSLOT = 2, 5
            self.stg = [self.carve(i * 4096, 4096, F32).rearrange("p (k c) -> p k c", k=8) for i in range(self.NSTG)]
            self.r_stg = [Res("stg%d" % i) for i in range(self.NSTG)]
            self.stg_slot = [S.new_slot() for _ in range(self.NSTG)]
            base = self.NSTG * 4096
            self.wsl = [self.carve(base + i * 2048, 2048, BF16).rearrange("p (k c) -> p k c", k=8)
                        for i in range(self.NSLOT)]
            self.r_wsl = [Res("wsl%d" % i) for i in range(self.NSLOT)]
            self.w_n = 0
            self.W0 = base + self.NSLOT * 2048
            self.misc_slot = S.new_slot()
            self.tab_slots = [S.new_slot(), S.new_slot()]
            self.out_slot = S.new_slot()

            S.dma("sp", lambda e: e.dma_start(out=self.cb[:], in_=self.cb_d[:, :]), self.misc_slot, writes=[self.r_const])
            S.dma("sp", lambda e: e.dma_start(out=self.cf[:], in_=self.cf_d[:, :]), self.misc_slot, writes=[self.r_const])
            S.dma("sp", lambda e: e.dma_start(out=self.sm[:], in_=self.sm_d[:, :]), self.misc_slot, writes=[self.r_const])
            xs = S.new_slot()
            xv = self.x_d.rearrange("(t p) d -> p t d", p=128)
            for i in range(NT):
                S.dma("sp", lambda e, i=i: e.dma_start(out=self.xres[:, i, :], in_=xv[:, i, :]), xs, writes=[self.r_x[i]])
            for i in range(NT):
                self.r_x[i].w = (xs.key, xs.total)
            for l in range(NL):
                sl = self.sm[:, l * 22 + 14: l * 22 + 22]
                S.op("act", lambda e, sl=sl: e.activation(out=sl, in_=sl, func=AF.Exp),
                     reads=[self.r_const], writes=[self.r_const])

            for l in range(NL):
                self.phase0(l)
                S.barrier()
                if self.debug == "p0":
                    break
                try:
                    self.phase1(l)
                except _Stop:
                    pass
                S.barrier()
                if self.debug and l == 0:
                    break
                self.phase2(l)
                S.barrier()

            if self.debug == "p0":
                ds = S.new_slot()
                for ch in range(8):
                    S.dma("sp", lambda e, ch=ch: e.dma_start(out=self.dbg_d[:, ch * 2048:(ch + 1) * 2048],
                                                             in_=self.hnT[:, ch, :]), ds,
                          reads=[self.r_hnT[tg][ch] for tg in range(4)])
            elif self.debug:
                ds = S.new_slot()
                nd = {"A1": 1, "A": 4, "B": 8, "yT": 12}.get(self.debug, 0)
                for ch in range(nd):
                    S.dma("sp", lambda e, ch=ch: e.dma_start(out=self.dbg_d[:, ch * 2048:(ch + 1) * 2048],
                                                             in_=self.yT[:, ch, :]), ds, reads=self.r_yT[ch])
            yv = self.y_d.rearrange("(t p) d -> p t d", p=128)
            for i in range(NT):
                S.dma("sp", lambda e, i=i: e.dma_start(out=yv[:, i, :], in_=self.xres[:, i, :]), self.out_slot,
                      reads=[self.r_x[i]])
            S.barrier()
            S.emit(nc, st)
        return nc

    def wunit(self, srcs, nk=8):
        S = self.S
        i = self.w_n
        self.w_n += 1
        sg, sl = i % self.NSTG, i % self.NSLOT
        stg, slot = self.stg[sg], self.wsl[sl]
        for (src, c0) in srcs:
            nc_ = src.shape[-1]
            S.dma("sp", lambda e, src=src, c0=c0, nc_=nc_: e.dma_start(out=stg[:, 0:nk, c0:c0 + nc_], in_=src),
                  self.stg_slot[sg], writes=[self.r_stg[sg]])
        S.op("pool", lambda e: e.tensor_copy(out=slot[:, 0:nk, :], in_=stg[:, 0:nk, :]),
             reads=[self.r_stg[sg]], writes=[self.r_wsl[sl]])
        return slot, self.r_wsl[sl]

    def win_unit(self, l, col0, ncols=128, dup=False):
        src = self.w_in_d[l, :, col0:col0 + ncols].rearrange("(kc p) c -> p kc c", p=128)
        if dup:
            return self.wunit([(src, 0), (src, ncols)])
        return self.wunit([(src, 0)])

    def phase0(self, l):
        S = self.S
        W0 = self.W0
        junk = self.carve(W0, 2048, BF16)
        hnb = self.carve(W0 + 2048, 8192, BF16).rearrange("p (t d) -> p t d", t=4)
        ss = self.carve(W0 + 10240, 64, F32)
        rstd = self.carve(W0 + 10304, 64, F32)
        r_junk, r_ss, r_rstd = Res(), [Res() for _ in range(NT)], [Res() for _ in range(NT)]
        r_hnb = [Res() for _ in range(4)]
        lng = self.sm[:, l * 22: l * 22 + 8]
        for tg in range(4):
            for tt in range(4):
                i = tg * 4 + tt
                S.op("act", lambda e, i=i: e.activation(out=junk, in_=self.xres[:, i, :], func=AF.Square,
                                                        accum_out=ss[:, i:i + 1]),
                     reads=[self.r_x[i]], writes=[r_junk, r_ss[i]])
                S.op("act", lambda e, i=i: e.activation(out=rstd[:, i:i + 1], in_=ss[:, i:i + 1], func=AF.Ln,
                                                        scale=1.0 / D, bias=self.epsc),
                     reads=[r_ss[i], self.r_const], writes=[r_rstd[i]])
                S.op("act", lambda e, i=i: e.activation(out=rstd[:, i:i + 1], in_=rstd[:, i:i + 1], func=AF.Exp, scale=-0.5),
                     reads=[r_rstd[i]], writes=[r_rstd[i]])
                S.op("dve", lambda e, i=i, tt=tt: e.tensor_scalar(out=hnb[:, tt, :], in0=self.xres[:, i, :],
                                                                  scalar1=rstd[:, i:i + 1], scalar2=None,
                                                                  op0=ALU.mult),
                     reads=[self.r_x[i], r_rstd[i]], writes=[r_hnb[tt]])
            for kc in range(8):
                bank, rb = self.sbank()
                pb = bank[:].bitcast(BF16)
                for tt in range(4):
                    S.op("pe", lambda e, pb=pb, tt=tt, kc=kc: e.transpose(pb[:, tt * 128:(tt + 1) * 128],
                                                                         hnb[:, tt, kc * 128:(kc + 1) * 128],
                                                                         self.ident),
                         reads=[r_hnb[tt], self.r_const], writes=[rb])
                dst = self.hnT[:, kc, tg * 512:(tg + 1) * 512]
                if kc % 2 == 0:
                    S.op("act", lambda e, pb=pb, dst=dst, kc=kc: e.activation(out=dst, in_=pb[:, 0:512], func=AF.Copy,
                                                                             scale=lng[:, kc:kc + 1]),
                         reads=[rb, self.r_const], writes=[self.r_hnT[tg][kc]])
                else:
                    S.op("dve", lambda e, pb=pb, dst=dst, kc=kc: e.tensor_scalar(out=dst, in0=pb[:, 0:512],
                                                                                scalar1=lng[:, kc:kc + 1], scalar2=None,
                                                                                op0=ALU.mult),
                         reads=[rb, self.r_const], writes=[self.r_hnT[tg][kc]])

    def p1_layout(self):
        o = self.W0
        L = {}

        def take(name, nbytes):
            nonlocal o
            L[name] = o
            o += nbytes
        take("QT", 4096)
        take("KT", 4096)
        take("G", 4096)
        take("Vp", 6144)
        take("P", 3 * 1024)
        take("sq", 2 * 1024)
        take("rs", 2 * 2048)
        take("tstage", 2 * 2176)
        take("tabBC", 2 * 1024)
        take("rr", 2048)
        take("tt", 2048)
        take("rb16", 2048)
        take("th", 2048)
        take("km", 64)
        take("rank", 1024)
        take("pen", 512)
        assert o <= self.WORK, o
        return L

    def phase1(self, l):
        S = self.S
        L = self.p1_layout()
        c = self.carve
        self.QT = c(L["QT"], 4096, BF16)
        self.KT = c(L["KT"], 4096, BF16)
        self.G = c(L["G"], 4096, BF16)
        self.Vp = c(L["Vp"], 6144, BF16).rearrange("p (t c) -> p t c", t=16)
        self.Pb = [c(L["P"] + i * 1024, 1024, BF16) for i in range(3)]
        self.sqb = [c(L["sq"] + i * 1024, 1024, BF16) for i in range(2)]
        self.rsb = [c(L["rs"] + i * 2048, 2048, F32) for i in range(2)]
        self.tstage = [c(L["tstage"] + i * 2176, 2176, F32) for i in range(2)]
        self.tabBC = [c(L["tabBC"] + i * 1024, 1024, BF16).rearrange("p (h c) -> p h c", h=2) for i in range(2)]
        self.rr = c(L["rr"], 2048, F32)
        self.tt = c(L["tt"], 2048, F32)
        self.rb16 = c(L["rb16"], 2048, BF16).rearrange("p (a n) -> p a n", a=2)
        self.th = c(L["th"], 2048, F32)
        self.km = c(L["km"], 64, BF16)[:, 0:16]
        self.rank = c(L["rank"], 1024, F32)
        self.pen = c(L["pen"], 512, BF16)
        yc = self.yT[:, 8:12, :].rearrange("p a t -> p (a t)")
        self.TA = [yc[:, i * TA_W:(i + 1) * TA_W] for i in range(2)]
        o = 2 * TA_W
        self.penT = yc[:, o:o + 2048]
        o += 2048
        self.gm = yc[:, o:o + 512].bitcast(F32)
        o += 512
        self.cmpb = yc[:, o:o + 1024]
        o += 1024
        self.acc0 = yc[:, 0:4096].bitcast(F32)
        self.acc1 = yc[:, 4096:8192].bitcast(F32)
        self.r_QT = [Res() for _ in range(4)]
        self.r_KT = [Res() for _ in range(4)]
        self.r_G = [Res() for _ in range(4)]
        self.r_Vp = [Res() for _ in range(4)]
        self.r_P = [Res() for _ in range(3)]
        self.r_sq = [Res() for _ in range(2)]
        self.r_rs = [Res() for _ in range(2)]
        self.r_tstage = [Res() for _ in range(2)]
        self.r_tabBC = [Res() for _ in range(2)]
        self.r_TA = [Res() for _ in range(2)]
        self.r_rr, self.r_tt, self.r_th = Res(), Res(), Res()
        self.r_rb16 = Res()
        self.deferred = []
        self.r_misc = Res()
        self.r_acc = Res()
        self.p_i = 0
        self.q_i = 0
        self.ts_i = 0
        self.tb_i = 0
        self.o_i = 0
        S.op("pool", lambda e: e.memset(self.rb16, 0.0), writes=[self.r_rb16])
        S.op("pool", lambda e: e.memset(self.Vp[:, :, 64:128], 0.0), writes=self.r_Vp)
        S.op("pool", lambda e: e.memset(self.Vp[:, :, 64:65], 1.0), writes=self.r_Vp)
        gq = lambda j: self.sm[:, l * 22 + 8 + j: l * 22 + 9 + j]
        for hp in range(4):
            self.load_tabA(hp)
            self.proj_qk(l, A_Q + hp * 128, self.QT, self.r_QT, gq(0), 1)
            self.proj_qk(l, A_K + hp * 128, self.KT, self.r_KT, gq(1), 1)
            self.proj_v(l, A_V + hp * 128, 1)
            self.proj_g(l, A_G + hp * 128)
            if self.debug == "A1a":
                raise _Stop()
            self.moba_pair(l, hp)
            if self.debug == "A1":
                raise _Stop()
        S.barrier()
        if self.debug == "A":
            return
        for hp in range(4):
            for g, d in enumerate(DIL):
                self.load_tabBC(self.tabB_d, g * 8 + 2 * hp)
                self.stop("t1")
                self.proj_qk(l, B_Q + g * 512 + hp * 128, self.QT, self.r_QT, gq(2), d)
                self.proj_qk(l, B_K + g * 512 + hp * 128, self.KT, self.r_KT, gq(3), d)
                self.proj_v(l, B_V + g * 512 + hp * 128, d)
                if g == 0:
                    self.proj_g(l, B_G + hp * 128)
                self.stop("t2")
                self.window_pair(d, first=(g == 0), mode="B")
                self.stop("B%d" % (g + 1))
            self.finish_B(hp)
            self.stop("B4")
        S.barrier()
        if self.debug == "B":
            return
        for kv in range(2):
            self.proj_qk(l, C_K + kv * 64, self.KT, self.r_KT, gq(5), 1, ncols=64, dup=True)
            self.proj_v(l, C_V + kv * 64, 1, ncols=64, dup=True)
            for hq in range(2):
                hp = kv * 2 + hq
                self.load_tabBC(self.tabC_d, 2 * hp)
                self.proj_qk(l, C_Q + hp * 128, self.QT, self.r_QT, gq(4), 1)
                self.proj_g(l, C_G + hp * 128)
                self.window_pair(1, first=True, mode="C", hp=hp, l=l)

    def load_tabA(self, hp):
        S = self.S
        for hh in range(2):
            for pc in range(4):
                i = self.ts_i
                self.ts_i = (i + 1) % 2
                stg = self.tstage[i]
                src = self.tabA_d[2 * hp + hh, :, pc * 544:(pc + 1) * 544]
                S.dma("sp", lambda e, stg=stg, src=src: e.dma_start(out=stg, in_=src), self.tab_slots[i],
                      writes=[self.r_tstage[i]])
                dst = self.TA[hh][:, pc * 544:(pc + 1) * 544]
                S.op("act", lambda e, stg=stg, dst=dst: e.activation(out=dst, in_=stg, func=AF.Exp),
                     reads=[self.r_tstage[i]], writes=[self.r_TA[hh]])

    def load_tabBC(self, tab_d, idx0):
        S = self.S
        b = self.tb_i
        self.tb_i = (b + 1) % 2
        i = self.ts_i
        self.ts_i = (i + 1) % 2
        stg = self.tstage[i][:, 0:512].rearrange("p (h c) -> p h c", h=2)
        src = tab_d[idx0:idx0 + 2, :, :].rearrange("h p c -> p h c")
        S.dma("sp", lambda e: e.dma_start(out=stg, in_=src), self.tab_slots[i], writes=[self.r_tstage[i]])
        S.op("act", lambda e: e.activation(out=self.tabBC[b], in_=stg, func=AF.Exp),
             reads=[self.r_tstage[i]], writes=[self.r_tabBC[b]])
        self.cur_tab = (self.tabBC[b], self.r_tabBC[b])

    def proj_qk(self, l, col0, dst, r_dst, gcol, d, ncols=128, dup=False):
        S = self.S
        W, rW = self.win_unit(l, col0, ncols, dup)
        jobs = []
        for cch in range(4):
            def s0(cch=cch):
                bank, rb = self.sbank()
                for kc in range(8):
                    self.mm(bank[:], W[:, kc, :], self.hnT[:, kc, cch * 512:(cch + 1) * 512], kc == 0, kc == 7,
                            [rW, self.r_hnT[cch][kc]], [rb])
                i = self.q_i
                self.q_i = (i + 1) % 2
                S.op("act", lambda e: e.activation(out=self.sqb[i], in_=bank[:], func=AF.Square),
                     reads=[rb], writes=[self.r_sq[i]])
                return bank, rb, i

            def s1(state, cch=cch):
                bank, rb, i = state
                bank2, rb2 = self.sbank()
                self.mm(bank2[:], self.bdiag, self.sqb[i], True, True, [self.r_sq[i], self.r_const], [rb2])
                S.op("act", lambda e: e.activation(out=self.rsb[i], in_=bank2[:], func=AF.Ln, bias=self.epsc),
                     reads=[rb2, self.r_const], writes=[self.r_rs[i]])
                S.op("act", lambda e: e.activation(out=self.rsb[i], in_=self.rsb[i], func=AF.Exp, scale=-0.5),
                     reads=[self.r_rs[i]], writes=[self.r_rs[i]])
                if d == 1:
                    o_ap = dst[:, cch * 512:(cch + 1) * 512]
                    i0, i1 = bank[:], self.rsb[i]
                    wr = [r_dst[cch]]
                else:
                    na = 512 // d
                    o_ap = dst.rearrange("p (r m) -> p r m", r=d)[:, :, cch * na:(cch + 1) * na]
                    i0 = bank[:].rearrange("p (a r) -> p r a", r=d)
                    i1 = self.rsb[i].rearrange("p (a r) -> p r a", r=d)
                    wr = r_dst
                S.op("dve", lambda e: e.scalar_tensor_tensor(out=o_ap, in0=i0, scalar=gcol, in1=i1,
                                                             op0=ALU.mult, op1=ALU.mult),
                     reads=[rb, self.r_rs[i], self.r_const], writes=wr)
            jobs.append((s0, s1))
        self.pipeline(jobs, 1)

    def pipeline(self, jobs, lag):
        st = {}
        n = len(jobs)
        for t in range(n + lag):
            if t < n:
                st[t] = jobs[t][0]()
            if t - lag >= 0:
                jobs[t - lag][1](st.pop(t - lag))

    def sigmoid(self, dst, src, r_src, r_dst):
        S = self.S
        S.op("act", lambda e: e.activation(out=dst, in_=src, func=AF.Exp, scale=-1.0), reads=r_src, writes=[r_dst])
        S.op("act", lambda e: e.activation(out=dst, in_=dst, func=AF.Ln, bias=self.onec), reads=[r_dst, self.r_const],
             writes=[r_dst])
        S.op("act", lambda e: e.activation(out=dst, in_=dst, func=AF.Exp, scale=-1.0), reads=[r_dst], writes=[r_dst])

    def proj_g(self, l, col0):
        S = self.S
        W, rW = self.win_unit(l, col0)
        for cch in range(4):
            bank, rb = self.sbank()
            for kc in range(8):
                self.mm(bank[:], W[:, kc, :], self.hnT[:, kc, cch * 512:(cch + 1) * 512], kc == 0, kc == 7,
                        [rW, self.r_hnT[cch][kc]], [rb])
            self.sigmoid(self.th, bank[:], [rb], self.r_th)
            dst = self.G[:, cch * 512:(cch + 1) * 512]
            S.op("dve", lambda e, bank=bank, dst=dst: e.tensor_tensor(out=dst, in0=self.th, in1=bank[:], op=ALU.mult),
                 reads=[rb, self.r_th], writes=[self.r_G[cch]])

    def proj_v(self, l, col0, d, ncols=128, dup=False):
        S = self.S
        W, rW = self.win_unit(l, col0, ncols, dup)
        Ls = S_TOK // d
        for tq in range(4):
            bank, rb = self.sbank()
            for tt in range(4):
                j = tq * 4 + tt
                u0 = j * 128
                r, m0 = u0 // Ls, u0 % Ls
                start = m0 * d + r
                for kc in range(8):
                    lhsT = self.hnT[:, kc, start: start + 127 * d + 1: d]
                    reads = [rW] + [self.r_hnT[cc][kc] for cc in range(start // 512, (start + 127 * d) // 512 + 1)]
                    self.mm(bank[:, tt * 128:(tt + 1) * 128], lhsT, W[:, kc, :], kc == 0, kc == 7, reads, [rb])
            bv = bank[:].rearrange("p (t c) -> p t c", t=4)
            S.op("act", lambda e, bv=bv, tq=tq: e.activation(out=self.Vp[:, tq * 4:(tq + 1) * 4, 0:64], in_=bv[:, :, 0:64],
                                                             func=AF.Copy),
                 reads=[rb], writes=[self.r_Vp[tq]])
            S.op("dve", lambda e, bv=bv, tq=tq: e.tensor_copy(out=self.Vp[:, tq * 4:(tq + 1) * 4, 128:192],
                                                              in_=bv[:, :, 64:128]),
                 reads=[rb], writes=[self.r_Vp[tq]])

    def normalize(self, o0, o1, r_o0, r_o1, N, gcols, ych, ycols, r_y, sink=None, defer=False):
        S = self.S
        rr, tt = self.rr, self.tt
        for (row, o_, r_o_, sk) in ((64, o0, r_o0, sink[0] if sink else None), (0, o1, r_o1, sink[1] if sink else None)):
            if sk is not None:
                S.op("act", lambda e, row=row, o_=o_, sk=sk: e.activation(out=rr[row:row + 1, 0:N], in_=o_[row:row + 1, 0:N],
                                                                          func=AF.Ln, bias=sk),
                     reads=[r_o_, self.r_const], writes=[self.r_rr])
            else:
                S.op("act", lambda e, row=row, o_=o_: e.activation(out=rr[row:row + 1, 0:N], in_=o_[row:row + 1, 0:N],
                                                                   func=AF.Ln),
                     reads=[r_o_], writes=[self.r_rr])
            S.op("act", lambda e, row=row: e.activation(out=rr[row:row + 1, 0:N], in_=rr[row:row + 1, 0:N], func=AF.Exp,
                                                        scale=-1.0), reads=[self.r_rr], writes=[self.r_rr])
        self.stop("h4")
        rb16 = self.rb16
        for row in (64, 0):
            S.op("dve", lambda e, row=row: e.tensor_copy(out=rb16[row:row + 1, 0, 0:N], in_=rr[row:row + 1, 0:N]),
                 reads=[self.r_rr], writes=[self.r_rb16])
            S.op("dve", lambda e, row=row: e.tensor_tensor(out=rb16[row:row + 1, 1, 0:N], in0=rr[row:row + 1, 0:N],
                                                           in1=rb16[row:row + 1, 0, 0:N], op=ALU.subtract),
                 reads=[self.r_rr, self.r_rb16], writes=[self.r_rb16])
        self.stop("h5")
        if defer:
            self.deferred.append(lambda: self.normalize_b(o0, o1, r_o0, r_o1, N, gcols, ych, ycols, r_y))
        else:
            self.normalize_b(o0, o1, r_o0, r_o1, N, gcols, ych, ycols, r_y)

    def flush_deferred(self):
        d, self.deferred = self.deferred, []
        for f in d:
            f()

    def normalize_b(self, o0, o1, r_o0, r_o1, N, gcols, ych, ycols, r_y):
        S = self.S
        rb16, tt = self.rb16, self.tt
        bank, rb = self.sbank()
        for a in range(2):
            self.mm(bank[:, 0:N], self.cb[:, 256:384], rb16[:, a, 0:N], a == 0, a == 1,
                    [self.r_rb16, self.r_const], [rb])
        self.stop("h6")
        S.op("dve", lambda e: e.tensor_tensor(out=tt[:, 0:N], in0=gcols, in1=bank[:, 0:N], op=ALU.mult),
             reads=[rb] + self.r_G, writes=[self.r_tt])
        self.stop("h7")
        S.op("dve", lambda e: e.tensor_tensor(out=self.yT[0:64, ych, ycols], in0=tt[0:64, 0:N], in1=o0[0:64, 0:N],
                                              op=ALU.mult), reads=[self.r_tt, r_o0], writes=r_y)
        S.op("dve", lambda e: e.tensor_tensor(out=self.yT[64:128, ych, ycols], in0=tt[64:128, 0:N], in1=o1[64:128, 0:N],
                                              op=ALU.mult), reads=[self.r_tt, r_o1], writes=r_y)

    def obanks(self):
        i = self.o_i
        self.o_i = (i + 1) % 2
        return (self.banks[4 + 2 * i], self.r_bank[4 + 2 * i], self.banks[5 + 2 * i], self.r_bank[5 + 2 * i])

    def moba_pair(self, l, hp):
        S = self.S
        QT, KT, Vp = self.QT, self.KT, self.Vp
        def km_fn(e):
            with self.nc.allow_low_precision("block key sums only rank MoBA blocks; bf16 matmul operand"):
                return e.tensor_reduce(out=self.km[:, 0:8], in_=KT.rearrange("p (j k) -> p j k", j=8),
                                       axis=AX.X, op=ALU.add)
        S.op("dve", km_fn, reads=self.r_KT, writes=[self.r_misc])
        self.stop("g1")
        gm = self.gm.rearrange("p (t h j) -> p t h j", t=16, h=2)
        cm = self.cf[:, 128:256].rearrange("p (t j) -> p t j", t=16)
        for hh in range(2):
            gbank, rgb = self.sbank()
            gv = gbank[:, 0:128].rearrange("p (t j) -> p t j", t=16)
            for t in range(16):
                self.mm(gv[:, t, :], QT[hh * 64:(hh + 1) * 64, t * 128:(t + 1) * 128],
                        self.km[hh * 64:(hh + 1) * 64, 0:8], True, True,
                        [self.r_QT[t // 4], self.r_misc], [rgb])
            S.op("dve", lambda e, gv=gv, hh=hh: e.tensor_tensor(out=gm[:, :, hh, :], in0=gv, in1=cm, op=ALU.add),
                 reads=[rgb, self.r_const], writes=[self.r_misc])
        self.stop("g3")
        g3 = self.gm.rearrange("p (a j) -> p a j", j=8)
        rank = self.rank.rearrange("p (a j) -> p a j", j=8)
        pen = self.pen.rearrange("p (a j) -> p a j", j=8)
        for half in range(2):
            a0 = half * 16
            cmpv = self.cmpb.rearrange("p (a j k) -> p a j k", a=16, j=8)
            in0 = g3[:, a0:a0 + 16, :].unsqueeze(2).broadcast_to([128, 16, 8, 8])
            in1 = g3[:, a0:a0 + 16, :].unsqueeze(3).broadcast_to([128, 16, 8, 8])
            S.op("dve", lambda e, in0=in0, in1=in1, cmpv=cmpv: e.tensor_tensor(out=cmpv, in0=in0, in1=in1, op=ALU.is_gt),
                 reads=[self.r_misc], writes=[self.r_misc])
            S.op("dve", lambda e, cmpv=cmpv, a0=a0: e.tensor_reduce(out=rank[:, a0:a0 + 16, :], in_=cmpv, axis=AX.X, op=ALU.add),
                 reads=[self.r_misc], writes=[self.r_misc])
        self.stop("g4")
        S.op("dve", lambda e: e.tensor_scalar(out=self.pen, in0=self.rank, scalar1=2.5, scalar2=-240000.0,
                                              op0=ALU.is_gt, op1=ALU.mult), reads=[self.r_misc], writes=[self.r_misc])
        self.stop("g5")
        pent = self.pen.rearrange("p (t c) -> p t c", t=16)
        r_penT = Res()
        S.op("pool", lambda e: e.memset(self.penT, 0.0), writes=[r_penT])
        for half in range(2):
            bank, rb = self.sbank()
            pb = bank[:].bitcast(BF16)
            for tt_ in range(8):
                t = half * 8 + tt_
                S.op("pe", lambda e, pb=pb, tt_=tt_, t=t: e.transpose(pb[0:16, tt_ * 128:(tt_ + 1) * 128], pent[:, t, :],
                                                                       self.ident),
                     reads=[self.r_misc, self.r_const], writes=[rb])
            if self.debug == "g6":
                continue
            S.op("dve", lambda e, pb=pb, half=half: e.tensor_copy(out=self.penT[0:16, half * 1024:(half + 1) * 1024],
                                                                  in_=pb[0:16, 0:1024]),
                 reads=[rb], writes=[r_penT])
        if self.debug in ("A1b", "g6"):
            raise _Stop()
        for n in range(8):
            if self.debug == "A1c" and n == 1:
                raise _Stop()
            O0b, rO0, O1b, rO1 = self.obanks()
            q0 = 256 * n
            for hh in range(2):
                ps = slice(hh * 64, (hh + 1) * 64)
                Ob, rO = (O0b, rO0) if hh == 0 else (O1b, rO1)
                Oview = Ob[0:65, 0:256] if hh == 0 else Ob[0:128, 0:256]
                lc = (lambda kt: Vp[:, kt, 0:65]) if hh == 0 else (lambda kt: Vp[:, kt, 64:192])
                npairs = n + 1
                jobs = []
                for jp in range(npairs):
                    a = 2 * jp
                    past = jp < n

                    def s0(a=a, past=past, jp=jp, ps=ps, hh=hh):
                        bank, rb = self.sbank()
                        rhs = QT[ps, q0:q0 + 256]
                        rq = [self.r_QT[q0 // 512]]
                        for hf, kt in enumerate((a + 1, a)):
                            o = bank[:, hf * 256:(hf + 1) * 256]
                            if past:
                                cidx = hh * 8 + jp
                                lp = self.cb[:, cidx:cidx + 1].broadcast_to([128, 128])
                                self.mm(o, lp, self.penT[:, q0:q0 + 256], True, False, [r_penT, self.r_const], [rb])
                            self.mm(o, KT[ps, kt * 128:(kt + 1) * 128], rhs, not past, True,
                                    rq + [self.r_KT[kt // 4]], [rb])
                        i = self.p_i
                        self.p_i = (i + 1) % 3
                        P = self.Pb[i]
                        S.op("act", lambda e: e.activation(out=P, in_=bank[:], func=AF.Exp, scale=0.125),
                             reads=[rb], writes=[self.r_P[i]])
                        self.stop("h1")
                        c0 = 256 * n - 128 * a + 128
                        base = self.TA[hh][:, c0 - 128:c0 - 127]
                        tv = bass.AP(base.tensor, base.offset, [list(base.ap[0]), [128, 2], [1, 256]])
                        Pv = P.rearrange("p (h c) -> p h c", h=2)
                        S.op("dve", lambda e: e.tensor_tensor(out=Pv, in0=Pv, in1=tv, op=ALU.mult),
                             reads=[self.r_TA[hh], self.r_P[i]], writes=[self.r_P[i]])
                        self.stop("h2")
                        return i

                    def s1(i, a=a, jp=jp, npairs=npairs, Oview=Oview, rO=rO, lc=lc):
                        P = self.Pb[i]
                        for hf, kt in enumerate((a + 1, a)):
                            self.mm(Oview, lc(kt), P[:, hf * 256:(hf + 1) * 256], jp == 0 and hf == 0,
                                    jp == npairs - 1 and hf == 1, [self.r_P[i], self.r_Vp[kt // 4]], [rO])
                    jobs.append((s0, s1))
                self.pipeline(jobs, 2)
                self.stop("h3")
                if hh == 0:
                    self.flush_deferred()
            self.normalize(O0b, O1b, rO0, rO1, 256, self.G[:, q0:q0 + 256], hp, slice(q0, q0 + 256),
                           [self.r_yT[hp][q0 // 512]], defer=True)
        self.flush_deferred()

    def window_pair(self, d, first, mode, hp=None, l=None):
        S = self.S
        QT, KT, Vp = self.QT, self.KT, self.Vp
        tab, r_tab = self.cur_tab
        tps = NT // d
        osets = [(self.banks[4], self.r_bank[4], self.banks[5], self.r_bank[5]),
                 (self.banks[6], self.r_bank[6], self.banks[7], self.r_bank[7])]

        def pv(i, kt, qi, start, stop):
            qt = kt + qi
            O0b, rO0, O1b, rO1 = osets[(qt // 4) % 2]
            cs = slice((qt % 4) * 128, (qt % 4) * 128 + 128)
            P0 = self.Pb[i][:, qi * 128: qi * 128 + 128]
            P1 = self.Pb[i][:, 256 + qi * 128: 256 + qi * 128 + 128]
            self.mm(O0b[0:65, cs], Vp[:, kt, 0:65], P0, start, stop, [self.r_P[i], self.r_Vp[kt // 4]], [rO0])
            self.mm(O1b[0:128, cs], Vp[:, kt, 64:192], P1, start, stop, [self.r_P[i], self.r_Vp[kt // 4]], [rO1])

        jobs = []
        for kt in range(NT):
            first_in_sub = (kt % tps) == 0
            last_in_sub = (kt % tps) == tps - 1
            nq = 1 if last_in_sub else 2
            N = 128 * nq

            def s0(kt=kt, N=N):
                rq = [self.r_QT[kt // 4], self.r_QT[(kt * 128 + N - 1) // 512]]
                i = self.p_i
                self.p_i = (i + 1) % 3
                Pv = self.Pb[i].rearrange("p (h c) -> p h c", h=2)[:, :, 0:N]
                for hh in range(2):
                    bank, rb = self.sbank()
                    ps = slice(hh * 64, (hh + 1) * 64)
                    self.mm(bank[:, 0:N], KT[ps, kt * 128:(kt + 1) * 128],
                            QT[ps, kt * 128: kt * 128 + N], True, True, [self.r_KT[kt // 4]] + rq, [rb])
                    S.op("act", lambda e, bank=bank, hh=hh: e.activation(out=self.Pb[i][:, hh * 256: hh * 256 + N],
                                                                         in_=bank[:, 0:N], func=AF.Exp, scale=0.125),
                         reads=[rb], writes=[self.r_P[i]])
                tv = tab[:, :, 0:N]
                S.op("dve", lambda e: e.tensor_tensor(out=Pv, in0=Pv, in1=tv, op=ALU.mult),
                     reads=[r_tab, self.r_P[i]], writes=[self.r_P[i]])
                self.stop("w1")
                return i

            def s1(i, kt=kt, nq=nq, first_in_sub=first_in_sub):
                pv(i, kt, 0, first_in_sub, True)
                if nq == 2:
                    pv(i, kt, 1, True, False)
                self.stop("w2")
                if kt % 4 == 1:
                    self.flush_deferred()
                if kt % 4 == 3:
                    if self.debug == "w3":
                        raise _Stop()
                    self.consume_group(kt // 4, d, first, mode, osets[(kt // 4) % 2], hp, l)
                    self.stop("w4")
            jobs.append((s0, s1))
        self.pipeline(jobs, 1)
        self.flush_deferred()

    def consume_group(self, tq, d, first, mode, ob, hp, l):
        S = self.S
        O0b, rO0, O1b, rO1 = ob
        if mode == "C":
            q0 = tq * 512
            e0 = self.sm[64:65, l * 22 + 14 + 2 * hp: l * 22 + 15 + 2 * hp]
            e1 = self.sm[0:1, l * 22 + 15 + 2 * hp: l * 22 + 16 + 2 * hp]
            self.normalize(O0b, O1b, rO0, rO1, 512, self.G[:, q0:q0 + 512], 8 + hp, slice(q0, q0 + 512),
                           [self.r_yT[8 + hp][tq]], sink=(e0, e1), defer=True)
            return
        Ls = S_TOK // d
        if d == 1:
            dst0 = self.acc0[0:65, tq * 512:(tq + 1) * 512]
            dst1 = self.acc1[:, tq * 512:(tq + 1) * 512]
            src0, src1 = O0b[0:65, :], O1b[:, :]
        elif d == 4:
            dst0 = self.acc0[0:65, tq:tq + 4 * 511 + 1:4]
            dst1 = self.acc1[:, tq:tq + 4 * 511 + 1:4]
            src0, src1 = O0b[0:65, :], O1b[:, :]
        else:
            dst0 = self.acc0[0:65, :].rearrange("p (m r) -> p r m", r=16)[:, 4 * tq:4 * tq + 4, :]
            dst1 = self.acc1[:, :].rearrange("p (m r) -> p r m", r=16)[:, 4 * tq:4 * tq + 4, :]
            src0 = O0b[0:65, :].rearrange("p (r m) -> p r m", r=4)
            src1 = O1b[:, :].rearrange("p (r m) -> p r m", r=4)
        if first:
            S.op("act", lambda e: e.activation(out=dst0, in_=src0, func=AF.Copy), reads=[rO0], writes=[self.r_acc])
            S.op("dve", lambda e: e.tensor_copy(out=dst1, in_=src1), reads=[rO1], writes=[self.r_acc])
        else:
            S.op("dve", lambda e: e.tensor_tensor(out=dst0, in0=dst0, in1=src0, op=ALU.add), reads=[rO0, self.r_acc],
                 writes=[self.r_acc])
            S.op("dve", lambda e: e.tensor_tensor(out=dst1, in0=dst1, in1=src1, op=ALU.add), reads=[rO1, self.r_acc],
                 writes=[self.r_acc])

    def finish_B(self, hp):
        for cch in range(4):
            q0 = cch * 512
            self.normalize(self.acc0[:, q0:q0 + 512], self.acc1[:, q0:q0 + 512], self.r_acc, self.r_acc, 512,
                           self.G[:, q0:q0 + 512], 4 + hp, slice(q0, q0 + 512), [self.r_yT[4 + hp][cch]])

    def phase2(self, l):
        S = self.S
        o = self.W0
        mT = self.carve(o, 16384, BF16).rearrange("p (k t) -> p k t", k=8)
        o += 16384
        acc = self.carve(o, 4096, F32)
        o += 4096
        th = [self.carve(o + i * 2048, 2048, F32) for i in range(2)]
        o += 4096
        wo = self.carve(o, 8192, BF16).rearrange("p (k e) -> p k e", k=8)
        o += 8192
        assert o <= self.WORK
        r_mT = [[Res() for _ in range(2)] for _ in range(8)]
        r_acc = [Res() for _ in range(2)]
        r_th = [Res() for _ in range(2)]
        r_wo = Res()
        th_i = 0
        for half in range(2):
            t0 = half * 1024
            for dc in range(8):
                for i in range(3):
                    Wg, rWg = self.win_unit(l, GATE0 + i * 1024 + dc * 128)
                    srcb = self.w_br_d[l, i, :, dc * 128:(dc + 1) * 128].rearrange("(wc p) c -> p wc c", p=128)
                    Wb, rWb = self.wunit([(srcb, 0)], nk=4)
                    for cc in range(2):
                        tok = slice(t0 + cc * 512, t0 + (cc + 1) * 512)
                        cg = (t0 + cc * 512) // 512
                        bg, rbg = self.sbank()
                        for kc in range(8):
                            self.mm(bg[:], Wg[:, kc, :], self.hnT[:, kc, tok], kc == 0, kc == 7,
                                    [rWg, self.r_hnT[cg][kc]], [rbg])
                        bb, rbb = self.sbank()
                        for wc in range(4):
                            self.mm(bb[:], Wb[:, wc, :], self.yT[:, i * 4 + wc, tok], wc == 0, wc == 3,
                                    [rWb, self.r_yT[i * 4 + wc][cg]], [rbb])
                        k = th_i
                        th_i = (k + 1) % 2
                        self.sigmoid(th[k], bg[:], [rbg], r_th[k])
                        av = acc[:, cc * 512:(cc + 1) * 512]
                        if i == 0:
                            S.op("dve", lambda e, bb=bb, k=k, av=av: e.tensor_tensor(out=av, in0=th[k], in1=bb[:], op=ALU.mult),
                                 reads=[rbb, r_th[k]], writes=[r_acc[cc]])
                        else:
                            S.op("dve", lambda e, bb=bb, k=k: e.tensor_tensor(out=th[k], in0=th[k], in1=bb[:], op=ALU.mult),
                                 reads=[rbb, r_th[k]], writes=[r_th[k]])
                            if i == 1:
                                S.op("dve", lambda e, k=k, av=av: e.tensor_tensor(out=av, in0=av, in1=th[k], op=ALU.add),
                                     reads=[r_th[k], r_acc[cc]], writes=[r_acc[cc]])
                            else:
                                mv = mT[:, dc, cc * 512:(cc + 1) * 512]
                                S.op("dve", lambda e, k=k, av=av, mv=mv: e.tensor_tensor(out=mv, in0=av, in1=th[k], op=ALU.add),
                                     reads=[r_th[k], r_acc[cc]], writes=[r_mT[dc][cc]])
            for eh in range(2):
                for ec in range(4):
                    src = self.w_out_d[l, :, eh * 512 + ec * 128: eh * 512 + (ec + 1) * 128].rearrange("(kc p) c -> p kc c", p=128)
                    Wu, rWu = self.wunit([(src, 0)])
                    S.op("pool", lambda e, Wu=Wu, ec=ec: e.tensor_copy(out=wo[:, :, ec * 128:(ec + 1) * 128], in_=Wu[:, :, :]),
                         reads=[rWu], writes=[r_wo])
                for tt in range(8):
                    ti = half * 8 + tt
                    bank, rb = self.sbank()
                    for dc in range(8):
                        self.mm(bank[:], mT[:, dc, tt * 128:(tt + 1) * 128], wo[:, dc, :], dc == 0, dc == 7,
                                [r_mT[dc][tt // 4], r_wo], [rb])
                    xv = self.xres[:, ti, eh * 512:(eh + 1) * 512]
                    S.op("dve", lambda e, bank=bank, xv=xv: e.tensor_tensor(out=xv, in0=xv, in1=bank[:], op=ALU.add),
                         reads=[rb, self.r_x[ti]], writes=[self.r_x[ti]])


_PROG_CACHE = {}


def _get_prog(NL, debug=None):
    key = (NL, debug)
    if key not in _PROG_CACHE:
        p = Prog(NL, debug)
        p.build()
        _PROG_CACHE[key] = p
    return _PROG_CACHE[key]


FUSED = True


def kernel(x, ln_g, w_in, qk_g, sinks, w_branch, w_out, rel_bias):
    x = np.ascontiguousarray(np.asarray(x, np.float32))
    ln_g = np.asarray(ln_g, np.float32)
    w_in = np.ascontiguousarray(np.asarray(w_in, np.float32))
    qk_g = np.asarray(qk_g, np.float32)
    sinks = np.asarray(sinks, np.float32)
    w_branch = np.ascontiguousarray(np.asarray(w_branch, np.float32))
    w_out = np.ascontiguousarray(np.asarray(w_out, np.float32))
    tabA, tabB, tabC = _tables(rel_bias)
    cb, cf = _consts()
    depth = w_in.shape[0]
    ncores = x.shape[0]
    groups = [list(range(depth))] if FUSED else [[l] for l in range(depth)]
    cur = [x[b] for b in range(ncores)]
    for ls in groups:
        prog = _get_prog(len(ls))
        sl = slice(ls[0], ls[-1] + 1)
        common = {"w_in": w_in[sl], "w_branch": w_branch[sl], "w_out": w_out[sl], "tabA": tabA, "tabB": tabB,
                  "tabC": tabC, "constb": cb, "constf": cf, "smallf": _smallf(ln_g[sl], qk_g[sl], sinks[sl])}
        in_maps = [dict(common, x=cur[b]) for b in range(ncores)]
        res = run_bass_kernel_spmd(prog.nc, in_maps, core_ids=list(range(ncores)))
        cur = [np.asarray(res.results[b]["y"], np.float32) for b in range(ncores)]
    return np.stack(cur, axis=0)
```

```python
import math
from contextlib import ExitStack

import numpy as np
import ml_dtypes
import concourse.bass as bass
import concourse.mybir as mybir
from concourse.bass_utils import run_bass_kernel_spmd

F32 = mybir.dt.float32
BF16 = mybir.dt.bfloat16
AF = mybir.ActivationFunctionType
ALU = mybir.AluOpType
AX = mybir.AxisListType

S_TOK = 2048
D = 1024
NT = 16
C_IN = 11520
NEG = -30000.0
EPS = 1e-6
A_Q, A_K, A_V, A_G = 0, 512, 1024, 1536
B_Q, B_K, B_V, B_G = 2048, 3584, 5120, 6656
C_Q, C_K, C_V, C_G = 7168, 7680, 7808, 7936
GATE0 = 8448
DIL = (1, 4, 16)
TA_W = 2176


class Res:
    __slots__ = ("name", "w", "r")

    def __init__(self, name=""):
        self.name = name
        self.w = None
        self.r = {}


class DmaSlot:
    __slots__ = ("key", "total")

    def __init__(self, key):
        self.key = key
        self.total = 0


class Sched:
    ENGS = ("pe", "act", "dve", "pool", "sp")

    def __init__(self):
        self.prog = {e: [] for e in self.ENGS}
        self.count = {e: 0 for e in self.ENGS}
        self.seen = {e: {} for e in self.ENGS}
        self.slots = []

    def new_slot(self):
        s = DmaSlot("d%03d" % len(self.slots))
        self.slots.append(s)
        return s

    def _deps(self, eng, reads, writes):
        waits = {}
        seen = self.seen[eng]

        def add(tok):
            key, val = tok
            if key == eng and eng in ("pe", "sp"):
                return
            if seen.get(key, 0) >= val:
                return
            if waits.get(key, 0) < val:
                waits[key] = val

        for r in reads:
            if r.w is not None:
                add(r.w)
        for w in writes:
            if w.w is not None:
                add(w.w)
            for k, v in w.r.items():
                add((k, v))
        for k, v in waits.items():
            seen[k] = v
        return list(waits.items())

    def _mark(self, tok, reads, writes):
        for r in reads:
            if r.r.get(tok[0], 0) < tok[1]:
                r.r[tok[0]] = tok[1]
        for w in writes:
            w.w = tok
            w.r = {}

    def op(self, eng, fn, reads=(), writes=()):
        waits = self._deps(eng, reads, writes)
        self.count[eng] += 1
        tok = (eng, self.count[eng])
        self.prog[eng].append((fn, waits, (eng, 1)))
        self._mark(tok, reads, writes)
        return tok

    def dma(self, q, fn, slot, reads=(), writes=()):
        waits = self._deps(q, reads, writes)
        slot.total += 16
        tok = (slot.key, slot.total)
        self.prog[q].append((fn, waits, (slot.key, 16)))
        self._mark(tok, reads, writes)
        return tok

    def barrier(self):
        toks = [(e, self.count[e]) for e in self.ENGS if self.count[e] > 0]
        toks += [(s.key, s.total) for s in self.slots if s.total > 0]
        for e in self.ENGS:
            waits = {}
            for key, val in toks:
                if key == e and e in ("pe", "sp"):
                    continue
                if self.seen[e].get(key, 0) >= val:
                    continue
                waits[key] = val
                self.seen[e][key] = val
            if waits:
                self.prog[e].append((None, list(waits.items()), None))

    def emit(self, nc, stack):
        keys = set()
        for e in self.ENGS:
            for (_, waits, inc) in self.prog[e]:
                for k, _v in waits:
                    keys.add(k)
                if inc is not None:
                    keys.add(inc[0])
        sems = {}
        for k in sorted(keys):
            sems[k] = stack.enter_context(nc.semaphore("s_" + k))
        block = stack.enter_context(nc.Block())
        prog = self.prog

        def run(name, eng):
            for (fn, waits, inc) in prog[name]:
                for (k, v) in waits:
                    eng.wait_ge(sems[k], v)
                if fn is not None:
                    ins = fn(eng)
                    if inc is not None:
                        ins.then_inc(sems[inc[0]], inc[1])

        @block.tensor
        def _(e):
            run("pe", e)

        @block.scalar
        def _(e):
            run("act", e)

        @block.vector
        def _(e):
            run("dve", e)

        @block.gpsimd
        def _(e):
            run("pool", e)

        @block.sync
        def _(e):
            run("sp", e)


def _bucket(dist):
    dist = np.maximum(dist, 0)
    me = 16
    lr = np.log(np.maximum(dist, 1).astype(np.float32) / np.float32(me)) / np.float32(math.log(2048 / me))
    large = me + (lr.astype(np.float32) * np.float32(16)).astype(np.int32)
    large = np.minimum(large, 31)
    return np.where(dist < me, dist, large)


def _tables(rel_bias):
    rel_bias = np.asarray(rel_bias, np.float32)
    p = np.arange(128)[:, None]
    c = np.arange(TA_W)[None, :]
    dist = c - 128 - p
    bk = _bucket(dist)
    tabA = np.where(dist[None] >= 0, rel_bias[0:8][:, bk], np.float32(NEG)).astype(np.float32)
    c = np.arange(256)[None, :]
    dm = c - p
    tabB = np.empty((24, 128, 256), np.float32)
    for g, d in enumerate(DIL):
        bk = _bucket(d * dm)
        ok = (dm >= 0) & (dm <= 128)
        tabB[g * 8:(g + 1) * 8] = np.where(ok[None], rel_bias[8 + g * 8: 16 + g * 8][:, bk], np.float32(NEG))
    bk = _bucket(dm)
    ok = (dm >= 0) & (dm <= 127)
    tabC = np.where(ok[None], rel_bias[32:40][:, bk], np.float32(NEG)).astype(np.float32)
    return tabA, tabB, tabC


def _consts():
    cb = np.zeros((128, 384), np.float32)
    cb[64, 256:320] = 1.0
    cb[0, 320:384] = 1.0
    cb[:, 0:128] = np.eye(128, dtype=np.float32)
    cb[0:64, 128:192] = 1.0 / 64
    cb[64:128, 192:256] = 1.0 / 64
    cf = np.zeros((128, 256), np.float32)
    cf[64, 0:64] = 0.5
    cf[0, 64:128] = 0.5
    cm = np.zeros((16, 8), np.float32)
    for t in range(16):
        for j in range(8):
            cm[t, j] = 0.0 if j < (t // 2) else -1e30
    cf[:, 128:256] = cm.reshape(1, 128)
    return cb.astype(ml_dtypes.bfloat16), cf


def _smallf(ln_g, qk_g, sinks):
    nl = ln_g.shape[0]
    out = np.zeros((128, nl * 22 + 2), np.float32)
    out[:, nl * 22] = EPS
    out[:, nl * 22 + 1] = 1.0
    for l in range(nl):
        out[:, l * 22: l * 22 + 8] = np.asarray(ln_g[l], np.float32).reshape(8, 128).T
        for j in range(6):
            out[:, l * 22 + 8 + j] = np.tile(np.asarray(qk_g[l, j], np.float32), 2)
        out[:, l * 22 + 14: l * 22 + 22] = np.asarray(sinks[l], np.float32)[None, :]
    return out


class _Stop(Exception):
    pass


class Prog:
    def __init__(self, NL, debug=None):
        self.NL = NL
        self.debug = debug
        nc = self.nc = bass.Bass("TRN2", target_bir_lowering=False)
        self.S = Sched()
        dt = nc.dram_tensor
        self.x_d = dt("x", [S_TOK, D], F32, kind="ExternalInput").ap()
        self.w_in_d = dt("w_in", [NL, D, C_IN], F32, kind="ExternalInput").ap()
        self.w_br_d = dt("w_branch", [NL, 3, 512, D], F32, kind="ExternalInput").ap()
        self.w_out_d = dt("w_out", [NL, D, D], F32, kind="ExternalInput").ap()
        self.tabA_d = dt("tabA", [8, 128, TA_W], F32, kind="ExternalInput").ap()
        self.tabB_d = dt("tabB", [24, 128, 256], F32, kind="ExternalInput").ap()
        self.tabC_d = dt("tabC", [8, 128, 256], F32, kind="ExternalInput").ap()
        self.cb_d = dt("constb", [128, 384], BF16, kind="ExternalInput").ap()
        self.cf_d = dt("constf", [128, 256], F32, kind="ExternalInput").ap()
        self.sm_d = dt("smallf", [128, NL * 22 + 2], F32, kind="ExternalInput").ap()
        self.y_d = dt("y", [S_TOK, D], F32, kind="ExternalOutput").ap()
        if debug:
            self.dbg_d = dt("dbg", [128, 12 * 2048], BF16, kind="ExternalOutput").ap()

    def stop(self, name):
        if self.debug == name:
            raise _Stop()

    def mm(self, out, lhsT, rhs, start, stop, reads, writes):
        self.S.op("pe", lambda e: e.matmul(out, lhsT=lhsT, rhs=rhs, start=start, stop=stop),
                  reads=reads, writes=writes)

    def carve(self, off, nbytes, dtype):
        assert off % 4 == 0 and nbytes % 4 == 0 and off + nbytes <= self.WORK, (off, nbytes)
        ap = self.work[:, off // 2:(off + nbytes) // 2]
        if dtype == F32:
            ap = ap.bitcast(F32)
        return ap

    def sbank(self):
        i = self.sb_i
        self.sb_i = (i + 1) % 4
        return self.banks[i], self.r_bank[i]

    def build(self):
        nc, S, NL = self.nc, self.S, self.NL
        with ExitStack() as st:
            T = lambda name, shape, dtype: st.enter_context(nc.sbuf_tensor(name, shape, dtype))
            self.xres = T("xres", [128, NT, D], F32)
            self.hnT = T("hnT", [128, 8, S_TOK], BF16)
            self.yT = T("yT", [128, 12, S_TOK], BF16)
            self.cb = T("cb", [128, 384], BF16)
            self.cf = T("cf", [128, 256], F32)
            self.sm = T("sm", [128, NL * 22 + 2], F32)
            self.epsc = self.sm[:, NL * 22:NL * 22 + 1]
            self.onec = self.sm[:, NL * 22 + 1:NL * 22 + 2]
            self.WORK = 61 * 1024
            self.work = T("work", [128, self.WORK // 2], BF16)
            self.banks = [st.enter_context(nc.psum_tensor("bank%d" % i, [128, 512], F32)) for i in range(8)]
            self.r_bank = [Res("bank%d" % i) for i in range(8)]
            self.sb_i = 0
            self.ident = self.cb[:, 0:128]
            self.bdiag = self.cb[:, 128:256]
            self.r_x = [Res("x%d" % i) for i in range(NT)]
            self.r_hnT = [[Res("hnT%d_%d" % (tg, kc)) for kc in range(8)] for tg in range(4)]
            self.r_yT = [[Res("yT%d_%d" % (ch, c)) for c in range(4)] for ch in range(12)]
            self.r_const = Res("const")
            self.NSTG, self.NSLOT = 2, 5
            self.stg = [self.carve(i * 4096, 4096, F32).rearrange("p (k c) -> p k c", k=8) for i in range(self.NSTG)]
            self.r_stg = [Res("stg%d" % i) for i in range(self.NSTG)]
            self.stg_slot = [S.new_slot() for _ in range(self.NSTG)]
            base = self.NSTG * 4096
            self.wsl = [self.carve(base + i * 2048, 2048, BF16).rearrange("p (k c) -> p k c", k=8)
                        for i in range(self.NSLOT)]
            self.r_wsl = [Res("wsl%d" % i) for i in range(self.NSLOT)]
            self.w_n = 0
            self.W0 = base + self.NSLOT * 2048
            self.misc_slot = S.new_slot()
            self.tab_slots = [S.new_slot(), S.new_slot()]
            self.out_slot = S.new_slot()

            S.dma("sp", lambda e: e.dma_start(out=self.cb[:], in_=self.cb_d[:, :]), self.misc_slot, writes=[self.r_const])
            S.dma("sp", lambda e: e.dma_start(out=self.cf[:], in_=self.cf_d[:, :]), self.misc_slot, writes=[self.r_const])
            S.dma("sp", lambda e: e.dma_start(out=self.sm[:], in_=self.sm_d[:, :]), self.misc_slot, writes=[self.r_const])
            xs = S.new_slot()
            xv = self.x_d.rearrange("(t p) d -> p t d", p=128)
            for i in range(NT):
                S.dma("sp", lambda e, i=i: e.dma_start(out=self.xres[:, i, :], in_=xv[:, i, :]), xs, writes=[self.r_x[i]])
            for i in range(NT):
                self.r_x[i].w = (xs.key, xs.total)
            for l in range(NL):
                sl = self.sm[:, l * 22 + 14: l * 22 + 22]
                S.op("act", lambda e, sl=sl: e.activation(out=sl, in_=sl, func=AF.Exp),
                     reads=[self.r_const], writes=[self.r_const])

            for l in range(NL):
                self.phase0(l)
                S.barrier()
                if self.debug == "p0":
                    break
                try:
                    self.phase1(l)
                except _Stop:
                    pass
                S.barrier()
                if self.debug and l == 0:
                    break
                self.phase2(l)
                S.barrier()

            if self.debug == "p0":
                ds = S.new_slot()
                for ch in range(8):
                    S.dma("sp", lambda e, ch=ch: e.dma_start(out=self.dbg_d[:, ch * 2048:(ch + 1) * 2048],
                                                             in_=self.hnT[:, ch, :]), ds,
                          reads=[self.r_hnT[tg][ch] for tg in range(4)])
            elif self.debug:
                ds = S.new_slot()
                nd = {"A1": 1, "A": 4, "B": 8, "yT": 12}.get(self.debug, 0)
                for ch in range(nd):
                    S.dma("sp", lambda e, ch=ch: e.dma_start(out=self.dbg_d[:, ch * 2048:(ch + 1) * 2048],
                                                             in_=self.yT[:, ch, :]), ds, reads=self.r_yT[ch])
            yv = self.y_d.rearrange("(t p) d -> p t d", p=128)
            for i in range(NT):
                S.dma("sp", lambda e, i=i: e.dma_start(out=yv[:, i, :], in_=self.xres[:, i, :]), self.out_slot,
                      reads=[self.r_x[i]])
            S.barrier()
            S.emit(nc, st)
        return nc

    def wunit(self, srcs, nk=8):
        S = self.S
        i = self.w_n
        self.w_n += 1
        sg, sl = i % self.NSTG, i % self.NSLOT
        stg, slot = self.stg[sg], self.wsl[sl]
        for (src, c0) in srcs:
            nc_ = src.shape[-1]
            S.dma("sp", lambda e, src=src, c0=c0, nc_=nc_: e.dma_start(out=stg[:, 0:nk, c0:c0 + nc_], in_=src),
                  self.stg_slot[sg], writes=[self.r_stg[sg]])
        S.op("pool", lambda e: e.tensor_copy(out=slot[:, 0:nk, :], in_=stg[:, 0:nk, :]),
             reads=[self.r_stg[sg]], writes=[self.r_wsl[sl]])
        return slot, self.r_wsl[sl]

    def win_unit(self, l, col0, ncols=128, dup=False):
        src = self.w_in_d[l, :, col0:col0 + ncols].rearrange("(kc p) c -> p kc c", p=128)
        if dup:
            return self.wunit([(src, 0), (src, ncols)])
        return self.wunit([(src, 0)])

    def phase0(self, l):
        S = self.S
        W0 = self.W0
        junk = self.carve(W0, 2048, BF16)
        hnb = self.carve(W0 + 2048, 8192, BF16).rearrange("p (t d) -> p t d", t=4)
        ss = self.carve(W0 + 10240, 64, F32)
        rstd = self.carve(W0 + 10304, 64, F32)
        r_junk, r_ss, r_rstd = Res(), [Res() for _ in range(NT)], [Res() for _ in range(NT)]
        r_hnb = [Res() for _ in range(4)]
        lng = self.sm[:, l * 22: l * 22 + 8]
        for tg in range(4):
            for tt in range(4):
                i = tg * 4 + tt
                S.op("act", lambda e, i=i: e.activation(out=junk, in_=self.xres[:, i, :], func=AF.Square,
                                                        accum_out=ss[:, i:i + 1]),
                     reads=[self.r_x[i]], writes=[r_junk, r_ss[i]])
                S.op("act", lambda e, i=i: e.activation(out=rstd[:, i:i + 1], in_=ss[:, i:i + 1], func=AF.Ln,
                                                        scale=1.0 / D, bias=self.epsc),
                     reads=[r_ss[i], self.r_const], writes=[r_rstd[i]])
                S.op("act", lambda e, i=i: e.activation(out=rstd[:, i:i + 1], in_=rstd[:, i:i + 1], func=AF.Exp, scale=-0.5),
                     reads=[r_rstd[i]], writes=[r_rstd[i]])
                S.op("dve", lambda e, i=i, tt=tt: e.tensor_scalar(out=hnb[:, tt, :], in0=self.xres[:, i, :],
                                                                  scalar1=rstd[:, i:i + 1], scalar2=None,
                                                                  op0=ALU.mult),
                     reads=[self.r_x[i], r_rstd[i]], writes=[r_hnb[tt]])
            for kc in range(8):
                bank, rb = self.sbank()
                pb = bank[:].bitcast(BF16)
                for tt in range(4):
                    S.op("pe", lambda e, pb=pb, tt=tt, kc=kc: e.transpose(pb[:, tt * 128:(tt + 1) * 128],
                                                                         hnb[:, tt, kc * 128:(kc + 1) * 128],
                                                                         self.ident),
                         reads=[r_hnb[tt], self.r_const], writes=[rb])
                dst = self.hnT[:, kc, tg * 512:(tg + 1) * 512]
                if kc % 2 == 0:
                    S.op("act", lambda e, pb=pb, dst=dst, kc=kc: e.activation(out=dst, in_=pb[:, 0:512], func=AF.Copy,
                                                                             scale=lng[:, kc:kc + 1]),
                         reads=[rb, self.r_const], writes=[self.r_hnT[tg][kc]])
                else:
                    S.op("dve", lambda e, pb=pb, dst=dst, kc=kc: e.tensor_scalar(out=dst, in0=pb[:, 0:512],
                                                                                scalar1=lng[:, kc:kc + 1], scalar2=None,
                                                                                op0=ALU.mult),
                         reads=[rb, self.r_const], writes=[self.r_hnT[tg][kc]])

    def p1_layout(self):
        o = self.W0
        L = {}

        def take(name, nbytes):
            nonlocal o
            L[name] = o
            o += nbytes
        take("QT", 4096)
        take("KT", 4096)
        take("G", 4096)
        take("Vp", 6144)
        take("P", 3 * 1024)
        take("sq", 2 * 1024)
        take("rs", 2 * 2048)
        take("tstage", 2 * 2176)
        take("tabBC", 2 * 1024)
        take("rr", 2048)
        take("tt", 2048)
        take("rb16", 2048)
        take("th", 2048)
        take("km", 64)
        take("rank", 1024)
        take("pen", 512)
        assert o <= self.WORK, o
        return L

    def phase1(self, l):
        S = self.S
        L = self.p1_layout()
        c = self.carve
        self.QT = c(L["QT"], 4096, BF16)
        self.KT = c(L["KT"], 4096, BF16)
        self.G = c(L["G"], 4096, BF16)
        self.Vp = c(L["Vp"], 6144, BF16).rearrange("p (t c) -> p t c", t=16)
        self.Pb = [c(L["P"] + i * 1024, 1024, BF16) for i in range(3)]
        self.sqb = [c(L["sq"] + i * 1024, 1024, BF16) for i in range(2)]
        self.rsb = [c(L["rs"] + i * 2048, 2048, F32) for i in range(2)]
        self.tstage = [c(L["tstage"] + i * 2176, 2176, F32) for i in range(2)]
        self.tabBC = [c(L["tabBC"] + i * 1024, 1024, BF16).rearrange("p (h c) -> p h c", h=2) for i in range(2)]
        self.rr = c(L["rr"], 2048, F32)
        self.tt = c(L["tt"], 2048, F32)
        self.rb16 = c(L["rb16"], 2048, BF16).rearrange("p (a n) -> p a n", a=2)
        self.th = c(L["th"], 2048, F32)
        self.km = c(L["km"], 64, BF16)[:, 0:16]
        self.rank = c(L["rank"], 1024, F32)
        self.pen = c(L["pen"], 512, BF16)
        yc = self.yT[:, 8:12, :].rearrange("p a t -> p (a t)")
        self.TA = [yc[:, i * TA_W:(i + 1) * TA_W] for i in range(2)]
        o = 2 * TA_W
        self.penT = yc[:, o:o + 2048]
        o += 2048
        self.gm = yc[:, o:o + 512].bitcast(F32)
        o += 512
        self.cmpb = yc[:, o:o + 1024]
        o += 1024
        self.acc0 = yc[:, 0:4096].bitcast(F32)
        self.acc1 = yc[:, 4096:8192].bitcast(F32)
        self.r_QT = [Res() for _ in range(4)]
        self.r_KT = [Res() for _ in range(4)]
        self.r_G = [Res() for _ in range(4)]
        self.r_Vp = [Res() for _ in range(4)]
        self.r_P = [Res() for _ in range(3)]
        self.r_sq = [Res() for _ in range(2)]
        self.r_rs = [Res() for _ in range(2)]
        self.r_tstage = [Res() for _ in range(2)]
        self.r_tabBC = [Res() for _ in range(2)]
        self.r_TA = [Res() for _ in range(2)]
        self.r_rr, self.r_tt, self.r_th = Res(), Res(), Res()
        self.r_rb16 = Res()
        self.deferred = []
        self.r_misc = Res()
        self.r_acc = Res()
        self.p_i = 0
        self.q_i = 0
        self.ts_i = 0
        self.tb_i = 0
        self.o_i = 0
        S.op("pool", lambda e: e.memset(self.rb16, 0.0), writes=[self.r_rb16])
        S.op("pool", lambda e: e.memset(self.Vp[:, :, 64:128], 0.0), writes=self.r_Vp)
        S.op("pool", lambda e: e.memset(self.Vp[:, :, 64:65], 1.0), writes=self.r_Vp)
        gq = lambda j: self.sm[:, l * 22 + 8 + j: l * 22 + 9 + j]
        for hp in range(4):
            self.load_tabA(hp)
            self.proj_qk(l, A_Q + hp * 128, self.QT, self.r_QT, gq(0), 1)
            self.proj_qk(l, A_K + hp * 128, self.KT, self.r_KT, gq(1), 1)
            self.moba_gate(l, hp)
            self.proj_v(l, A_V + hp * 128, 1)
            self.proj_g(l, A_G + hp * 128)
            if self.debug == "A1a":
                raise _Stop()
            self.moba_attn(l, hp)
            if self.debug == "A1":
                raise _Stop()
        S.barrier()
        if self.debug == "A":
            return
        for hp in range(4):
            for g, d in enumerate(DIL):
                self.load_tabBC(self.tabB_d, g * 8 + 2 * hp)
                self.stop("t1")
                self.proj_qk(l, B_Q + g * 512 + hp * 128, self.QT, self.r_QT, gq(2), d)
                self.proj_qk(l, B_K + g * 512 + hp * 128, self.KT, self.r_KT, gq(3), d)
                self.proj_v(l, B_V + g * 512 + hp * 128, d)
                if g == 0:
                    self.proj_g(l, B_G + hp * 128)
                self.stop("t2")
                self.window_pair(d, first=(g == 0), mode="B")
                self.stop("B%d" % (g + 1))
            self.finish_B(hp)
            self.stop("B4")
        S.barrier()
        if self.debug == "B":
            return
        for kv in range(2):
            self.proj_qk(l, C_K + kv * 64, self.KT, self.r_KT, gq(5), 1, ncols=64, dup=True)
            self.proj_v(l, C_V + kv * 64, 1, ncols=64, dup=True)
            for hq in range(2):
                hp = kv * 2 + hq
                self.load_tabBC(self.tabC_d, 2 * hp)
                self.proj_qk(l, C_Q + hp * 128, self.QT, self.r_QT, gq(4), 1)
                self.proj_g(l, C_G + hp * 128)
                self.window_pair(1, first=True, mode="C", hp=hp, l=l)

    def load_tabA(self, hp):
        S = self.S
        for hh in range(2):
            for pc in range(4):
                i = self.ts_i
                self.ts_i = (i + 1) % 2
                stg = self.tstage[i]
                src = self.tabA_d[2 * hp + hh, :, pc * 544:(pc + 1) * 544]
                S.dma("sp", lambda e, stg=stg, src=src: e.dma_start(out=stg, in_=src), self.tab_slots[i],
                      writes=[self.r_tstage[i]])
                dst = self.TA[hh][:, pc * 544:(pc + 1) * 544]
                S.op("act", lambda e, stg=stg, dst=dst: e.activation(out=dst, in_=stg, func=AF.Exp),
                     reads=[self.r_tstage[i]], writes=[self.r_TA[hh]])

    def load_tabBC(self, tab_d, idx0):
        S = self.S
        b = self.tb_i
        self.tb_i = (b + 1) % 2
        i = self.ts_i
        self.ts_i = (i + 1) % 2
        stg = self.tstage[i][:, 0:512].rearrange("p (h c) -> p h c", h=2)
        src = tab_d[idx0:idx0 + 2, :, :].rearrange("h p c -> p h c")
        S.dma("sp", lambda e: e.dma_start(out=stg, in_=src), self.tab_slots[i], writes=[self.r_tstage[i]])
        S.op("act", lambda e: e.activation(out=self.tabBC[b], in_=stg, func=AF.Exp),
             reads=[self.r_tstage[i]], writes=[self.r_tabBC[b]])
        self.cur_tab = (self.tabBC[b], self.r_tabBC[b])

    def proj_qk(self, l, col0, dst, r_dst, gcol, d, ncols=128, dup=False):
        S = self.S
        W, rW = self.win_unit(l, col0, ncols, dup)
        jobs = []
        for cch in range(4):
            def s0(cch=cch):
                bank, rb = self.sbank()
                for kc in range(8):
                    self.mm(bank[:], W[:, kc, :], self.hnT[:, kc, cch * 512:(cch + 1) * 512], kc == 0, kc == 7,
                            [rW, self.r_hnT[cch][kc]], [rb])
                i = self.q_i
                self.q_i = (i + 1) % 2
                S.op("act", lambda e: e.activation(out=self.sqb[i], in_=bank[:], func=AF.Square),
                     reads=[rb], writes=[self.r_sq[i]])
                return bank, rb, i

            def s1(state, cch=cch):
                bank, rb, i = state
                bank2, rb2 = self.sbank()
                self.mm(bank2[:], self.bdiag, self.sqb[i], True, True, [self.r_sq[i], self.r_const], [rb2])
                S.op("act", lambda e: e.activation(out=self.rsb[i], in_=bank2[:], func=AF.Ln, bias=self.epsc),
                     reads=[rb2, self.r_const], writes=[self.r_rs[i]])
                S.op("act", lambda e: e.activation(out=self.rsb[i], in_=self.rsb[i], func=AF.Exp, scale=-0.5),
                     reads=[self.r_rs[i]], writes=[self.r_rs[i]])
                if d == 1:
                    o_ap = dst[:, cch * 512:(cch + 1) * 512]
                    i0, i1 = bank[:], self.rsb[i]
                    wr = [r_dst[cch]]
                else:
                    na = 512 // d
                    o_ap = dst.rearrange("p (r m) -> p r m", r=d)[:, :, cch * na:(cch + 1) * na]
                    i0 = bank[:].rearrange("p (a r) -> p r a", r=d)
                    i1 = self.rsb[i].rearrange("p (a r) -> p r a", r=d)
                    wr = r_dst
                S.op("dve", lambda e: e.scalar_tensor_tensor(out=o_ap, in0=i0, scalar=gcol, in1=i1,
                                                             op0=ALU.mult, op1=ALU.mult),
                     reads=[rb, self.r_rs[i], self.r_const], writes=wr)
            jobs.append((s0, s1))
        self.pipeline(jobs, 1)

    def pipeline(self, jobs, lag):
        st = {}
        n = len(jobs)
        for t in range(n + lag):
            if t < n:
                st[t] = jobs[t][0]()
            if t - lag >= 0:
                jobs[t - lag][1](st.pop(t - lag))

    def sigmoid(self, dst, src, r_src, r_dst):
        S = self.S
        S.op("act", lambda e: e.activation(out=dst, in_=src, func=AF.Exp, scale=-1.0), reads=r_src, writes=[r_dst])
        S.op("act", lambda e: e.activation(out=dst, in_=dst, func=AF.Ln, bias=self.onec), reads=[r_dst, self.r_const],
             writes=[r_dst])
        S.op("act", lambda e: e.activation(out=dst, in_=dst, func=AF.Exp, scale=-1.0), reads=[r_dst], writes=[r_dst])

    def proj_g(self, l, col0):
        S = self.S
        W, rW = self.win_unit(l, col0)
        for cch in range(4):
            bank, rb = self.sbank()
            for kc in range(8):
                self.mm(bank[:], W[:, kc, :], self.hnT[:, kc, cch * 512:(cch + 1) * 512], kc == 0, kc == 7,
                        [rW, self.r_hnT[cch][kc]], [rb])
            thb, r_thb = self.rsb[cch % 2], self.r_rs[cch % 2]
            self.sigmoid(thb, bank[:], [rb], r_thb)
            dst = self.G[:, cch * 512:(cch + 1) * 512]
            S.op("dve", lambda e, bank=bank, dst=dst, thb=thb: e.tensor_tensor(out=dst, in0=thb, in1=bank[:], op=ALU.mult),
                 reads=[rb, r_thb], writes=[self.r_G[cch]])

    def proj_v(self, l, col0, d, ncols=128, dup=False):
        S = self.S
        W, rW = self.win_unit(l, col0, ncols, dup)
        Ls = S_TOK // d
        for tq in range(4):
            bank, rb = self.sbank()
            for tt in range(4):
                j = tq * 4 + tt
                u0 = j * 128
                r, m0 = u0 // Ls, u0 % Ls
                start = m0 * d + r
                for kc in range(8):
                    lhsT = self.hnT[:, kc, start: start + 127 * d + 1: d]
                    reads = [rW] + [self.r_hnT[cc][kc] for cc in range(start // 512, (start + 127 * d) // 512 + 1)]
                    self.mm(bank[:, tt * 128:(tt + 1) * 128], lhsT, W[:, kc, :], kc == 0, kc == 7, reads, [rb])
            bv = bank[:].rearrange("p (t c) -> p t c", t=4)
            S.op("act", lambda e, bv=bv, tq=tq: e.activation(out=self.Vp[:, tq * 4:(tq + 1) * 4, 0:64], in_=bv[:, :, 0:64],
                                                             func=AF.Copy),
                 reads=[rb], writes=[self.r_Vp[tq]])
            S.op("dve", lambda e, bv=bv, tq=tq: e.tensor_copy(out=self.Vp[:, tq * 4:(tq + 1) * 4, 128:192],
                                                              in_=bv[:, :, 64:128]),
                 reads=[rb], writes=[self.r_Vp[tq]])

    def normalize(self, o0, o1, r_o0, r_o1, N, gcols, ych, ycols, r_y, sink=None, defer=False):
        S = self.S
        rr, tt = self.rr, self.tt
        for (row, o_, r_o_, sk) in ((64, o0, r_o0, sink[0] if sink else None), (0, o1, r_o1, sink[1] if sink else None)):
            if sk is not None:
                S.op("act", lambda e, row=row, o_=o_, sk=sk: e.activation(out=rr[row:row + 1, 0:N], in_=o_[row:row + 1, 0:N],
                                                                          func=AF.Ln, bias=sk),
                     reads=[r_o_, self.r_const], writes=[self.r_rr])
            else:
                S.op("act", lambda e, row=row, o_=o_: e.activation(out=rr[row:row + 1, 0:N], in_=o_[row:row + 1, 0:N],
                                                                   func=AF.Ln),
                     reads=[r_o_], writes=[self.r_rr])
            S.op("act", lambda e, row=row: e.activation(out=rr[row:row + 1, 0:N], in_=rr[row:row + 1, 0:N], func=AF.Exp,
                                                        scale=-1.0), reads=[self.r_rr], writes=[self.r_rr])
        self.stop("h4")
        rb16 = self.rb16
        for row in (64, 0):
            S.op("dve", lambda e, row=row: e.tensor_copy(out=rb16[row:row + 1, 0, 0:N], in_=rr[row:row + 1, 0:N]),
                 reads=[self.r_rr], writes=[self.r_rb16])
            S.op("dve", lambda e, row=row: e.tensor_tensor(out=rb16[row:row + 1, 1, 0:N], in0=rr[row:row + 1, 0:N],
                                                           in1=rb16[row:row + 1, 0, 0:N], op=ALU.subtract),
                 reads=[self.r_rr, self.r_rb16], writes=[self.r_rb16])
        self.stop("h5")
        if defer:
            self.deferred.append(lambda: self.normalize_b(o0, o1, r_o0, r_o1, N, gcols, ych, ycols, r_y))
        else:
            self.normalize_b(o0, o1, r_o0, r_o1, N, gcols, ych, ycols, r_y)

    def flush_deferred(self):
        d, self.deferred = self.deferred, []
        for f in d:
            f()

    def normalize_b(self, o0, o1, r_o0, r_o1, N, gcols, ych, ycols, r_y):
        S = self.S
        rb16, tt = self.rb16, self.tt
        bank, rb = self.sbank()
        for a in range(2):
            self.mm(bank[:, 0:N], self.cb[:, 256:384], rb16[:, a, 0:N], a == 0, a == 1,
                    [self.r_rb16, self.r_const], [rb])
        self.stop("h6")
        S.op("dve", lambda e: e.tensor_tensor(out=tt[:, 0:N], in0=gcols, in1=bank[:, 0:N], op=ALU.mult),
             reads=[rb] + self.r_G, writes=[self.r_tt])
        self.stop("h7")
        S.op("dve", lambda e: e.tensor_tensor(out=self.yT[0:64, ych, ycols], in0=tt[0:64, 0:N], in1=o0[0:64, 0:N],
                                              op=ALU.mult), reads=[self.r_tt, r_o0], writes=r_y)
        S.op("dve", lambda e: e.tensor_tensor(out=self.yT[64:128, ych, ycols], in0=tt[64:128, 0:N], in1=o1[64:128, 0:N],
                                              op=ALU.mult), reads=[self.r_tt, r_o1], writes=r_y)

    def obanks(self):
        i = self.o_i
        self.o_i = (i + 1) % 2
        return (self.banks[4 + 2 * i], self.r_bank[4 + 2 * i], self.banks[5 + 2 * i], self.r_bank[5 + 2 * i])

    def moba_gate(self, l, hp):
        S = self.S
        QT, KT, Vp = self.QT, self.KT, self.Vp
        def km_fn(e):
            with self.nc.allow_low_precision("block key sums only rank MoBA blocks; bf16 matmul operand"):
                return e.tensor_reduce(out=self.km[:, 0:8], in_=KT.rearrange("p (j k) -> p j k", j=8),
                                       axis=AX.X, op=ALU.add)
        S.op("dve", km_fn, reads=self.r_KT, writes=[self.r_misc])
        self.stop("g1")
        gm = self.gm.rearrange("p (t h j) -> p t h j", t=16, h=2)
        cm = self.cf[:, 128:256].rearrange("p (t j) -> p t j", t=16)
        for hh in range(2):
            gbank, rgb = self.sbank()
            gv = gbank[:, 0:128].rearrange("p (t j) -> p t j", t=16)
            for t in range(16):
                self.mm(gv[:, t, :], QT[hh * 64:(hh + 1) * 64, t * 128:(t + 1) * 128],
                        self.km[hh * 64:(hh + 1) * 64, 0:8], True, True,
                        [self.r_QT[t // 4], self.r_misc], [rgb])
            S.op("dve", lambda e, gv=gv, hh=hh: e.tensor_tensor(out=gm[:, :, hh, :], in0=gv, in1=cm, op=ALU.add),
                 reads=[rgb, self.r_const], writes=[self.r_misc])
        self.stop("g3")
        g3 = self.gm.rearrange("p (a j) -> p a j", j=8)
        rank = self.rank.rearrange("p (a j) -> p a j", j=8)
        pen = self.pen.rearrange("p (a j) -> p a j", j=8)
        for half in range(2):
            a0 = half * 16
            cmpv = self.cmpb.rearrange("p (a j k) -> p a j k", a=16, j=8)
            in0 = g3[:, a0:a0 + 16, :].unsqueeze(2).broadcast_to([128, 16, 8, 8])
            in1 = g3[:, a0:a0 + 16, :].unsqueeze(3).broadcast_to([128, 16, 8, 8])
            S.op("dve", lambda e, in0=in0, in1=in1, cmpv=cmpv: e.tensor_tensor(out=cmpv, in0=in0, in1=in1, op=ALU.is_gt),
                 reads=[self.r_misc], writes=[self.r_misc])
            S.op("dve", lambda e, cmpv=cmpv, a0=a0: e.tensor_reduce(out=rank[:, a0:a0 + 16, :], in_=cmpv, axis=AX.X, op=ALU.add),
                 reads=[self.r_misc], writes=[self.r_misc])
        self.stop("g4")
        S.op("dve", lambda e: e.tensor_scalar(out=self.pen, in0=self.rank, scalar1=2.5, scalar2=-240000.0,
                                              op0=ALU.is_gt, op1=ALU.mult), reads=[self.r_misc], writes=[self.r_misc])
        self.stop("g5")
    def moba_attn(self, l, hp):
        S = self.S
        QT, KT, Vp = self.QT, self.KT, self.Vp
        pent = self.pen.rearrange("p (t c) -> p t c", t=16)
        r_penT = Res()
        S.op("pool", lambda e: e.memset(self.penT, 0.0), writes=[r_penT])
        for half in range(2):
            bank, rb = self.sbank()
            pb = bank[:].bitcast(BF16)
            for tt_ in range(8):
                t = half * 8 + tt_
                S.op("pe", lambda e, pb=pb, tt_=tt_, t=t: e.transpose(pb[0:16, tt_ * 128:(tt_ + 1) * 128], pent[:, t, :],
                                                                       self.ident),
                     reads=[self.r_misc, self.r_const], writes=[rb])
            if self.debug == "g6":
                continue
            S.op("dve", lambda e, pb=pb, half=half: e.tensor_copy(out=self.penT[0:16, half * 1024:(half + 1) * 1024],
                                                                  in_=pb[0:16, 0:1024]),
                 reads=[rb], writes=[r_penT])
        if self.debug in ("A1b", "g6"):
            raise _Stop()
        jobs = []
        for n in range(8):
            O0b, rO0, O1b, rO1 = self.obanks()
            q0 = 256 * n
            for hh in range(2):
                ps = slice(hh * 64, (hh + 1) * 64)
                Ob, rO = (O0b, rO0) if hh == 0 else (O1b, rO1)
                Oview = Ob[0:65, 0:256] if hh == 0 else Ob[0:128, 0:256]
                lc = (lambda kt: Vp[:, kt, 0:65]) if hh == 0 else (lambda kt: Vp[:, kt, 64:192])
                npairs = n + 1
                for jp in range(npairs):
                    a = 2 * jp
                    past = jp < n

                    def s0(a=a, past=past, jp=jp, ps=ps, hh=hh, n=n, q0=q0):
                        bank, rb = self.sbank()
                        rhs = QT[ps, q0:q0 + 256]
                        rq = [self.r_QT[q0 // 512]]
                        for hf, kt in enumerate((a + 1, a)):
                            o = bank[:, hf * 256:(hf + 1) * 256]
                            if past:
                                cidx = hh * 8 + jp
                                lp = self.cb[:, cidx:cidx + 1].broadcast_to([128, 128])
                                self.mm(o, lp, self.penT[:, q0:q0 + 256], True, False, [r_penT, self.r_const], [rb])
                            self.mm(o, KT[ps, kt * 128:(kt + 1) * 128], rhs, not past, True,
                                    rq + [self.r_KT[kt // 4]], [rb])
                        i = self.p_i
                        self.p_i = (i + 1) % 3
                        P = self.Pb[i]
                        S.op("act", lambda e: e.activation(out=P, in_=bank[:], func=AF.Exp, scale=0.125),
                             reads=[rb], writes=[self.r_P[i]])
                        c0 = 256 * n - 128 * a + 128
                        base = self.TA[hh][:, c0 - 128:c0 - 127]
                        tv = bass.AP(base.tensor, base.offset, [list(base.ap[0]), [128, 2], [1, 256]])
                        Pv = P.rearrange("p (h c) -> p h c", h=2)
                        S.op("dve", lambda e: e.tensor_tensor(out=Pv, in0=Pv, in1=tv, op=ALU.mult),
                             reads=[self.r_TA[hh], self.r_P[i]], writes=[self.r_P[i]])
                        return i

                    def s1(i, a=a, jp=jp, npairs=npairs, Oview=Oview, rO=rO, lc=lc):
                        P = self.Pb[i]
                        for hf, kt in enumerate((a + 1, a)):
                            self.mm(Oview, lc(kt), P[:, hf * 256:(hf + 1) * 256], jp == 0 and hf == 0,
                                    jp == npairs - 1 and hf == 1, [self.r_P[i], self.r_Vp[kt // 4]], [rO])
                    jobs.append((s0, s1))
                if hh == 0:
                    jobs.append((lambda: self.flush_deferred(), lambda st: None))

            def fin(st, O0b=O0b, O1b=O1b, rO0=rO0, rO1=rO1, q0=q0):
                self.normalize(O0b, O1b, rO0, rO1, 256, self.G[:, q0:q0 + 256], hp, slice(q0, q0 + 256),
                               [self.r_yT[hp][q0 // 512]], defer=True)
            jobs.append((lambda: None, fin))
        self.pipeline(jobs, 2)
        self.flush_deferred()

    def window_pair(self, d, first, mode, hp=None, l=None):
        S = self.S
        QT, KT, Vp = self.QT, self.KT, self.Vp
        tab, r_tab = self.cur_tab
        tps = NT // d
        osets = [(self.banks[4], self.r_bank[4], self.banks[5], self.r_bank[5]),
                 (self.banks[6], self.r_bank[6], self.banks[7], self.r_bank[7])]

        def pv(i, kt, qi, start, stop):
            qt = kt + qi
            O0b, rO0, O1b, rO1 = osets[(qt // 4) % 2]
            cs = slice((qt % 4) * 128, (qt % 4) * 128 + 128)
            P0 = self.Pb[i][:, qi * 128: qi * 128 + 128]
            P1 = self.Pb[i][:, 256 + qi * 128: 256 + qi * 128 + 128]
            self.mm(O0b[0:65, cs], Vp[:, kt, 0:65], P0, start, stop, [self.r_P[i], self.r_Vp[kt // 4]], [rO0])
            self.mm(O1b[0:128, cs], Vp[:, kt, 64:192], P1, start, stop, [self.r_P[i], self.r_Vp[kt // 4]], [rO1])

        jobs = []
        for kt in range(NT):
            first_in_sub = (kt % tps) == 0
            last_in_sub = (kt % tps) == tps - 1
            nq = 1 if last_in_sub else 2
            N = 128 * nq

            def s0(kt=kt, N=N):
                rq = [self.r_QT[kt // 4], self.r_QT[(kt * 128 + N - 1) // 512]]
                i = self.p_i
                self.p_i = (i + 1) % 3
                Pv = self.Pb[i].rearrange("p (h c) -> p h c", h=2)[:, :, 0:N]
                for hh in range(2):
                    bank, rb = self.sbank()
                    ps = slice(hh * 64, (hh + 1) * 64)
                    self.mm(bank[:, 0:N], KT[ps, kt * 128:(kt + 1) * 128],
                            QT[ps, kt * 128: kt * 128 + N], True, True, [self.r_KT[kt // 4]] + rq, [rb])
                    S.op("act", lambda e, bank=bank, hh=hh: e.activation(out=self.Pb[i][:, hh * 256: hh * 256 + N],
                                                                         in_=bank[:, 0:N], func=AF.Exp, scale=0.125),
                         reads=[rb], writes=[self.r_P[i]])
                tv = tab[:, :, 0:N]
                S.op("dve", lambda e: e.tensor_tensor(out=Pv, in0=Pv, in1=tv, op=ALU.mult),
                     reads=[r_tab, self.r_P[i]], writes=[self.r_P[i]])
                self.stop("w1")
                return i

            def s1(i, kt=kt, nq=nq, first_in_sub=first_in_sub):
                pv(i, kt, 0, first_in_sub, True)
                if nq == 2:
                    pv(i, kt, 1, True, False)
                self.stop("w2")
                if kt % 4 == 1:
                    self.flush_deferred()
                if kt % 4 == 3:
                    if self.debug == "w3":
                        raise _Stop()
                    self.consume_group(kt // 4, d, first, mode, osets[(kt // 4) % 2], hp, l)
                    self.stop("w4")
            jobs.append((s0, s1))
        self.pipeline(jobs, 1)
        self.flush_deferred()

    def consume_group(self, tq, d, first, mode, ob, hp, l):
        S = self.S
        O0b, rO0, O1b, rO1 = ob
        if mode == "C":
            q0 = tq * 512
            e0 = self.sm[64:65, l * 22 + 14 + 2 * hp: l * 22 + 15 + 2 * hp]
            e1 = self.sm[0:1, l * 22 + 15 + 2 * hp: l * 22 + 16 + 2 * hp]
            self.normalize(O0b, O1b, rO0, rO1, 512, self.G[:, q0:q0 + 512], 8 + hp, slice(q0, q0 + 512),
                           [self.r_yT[8 + hp][tq]], sink=(e0, e1), defer=True)
            return
        Ls = S_TOK // d
        if d == 1:
            dst0 = self.acc0[0:65, tq * 512:(tq + 1) * 512]
            dst1 = self.acc1[:, tq * 512:(tq + 1) * 512]
            src0, src1 = O0b[0:65, :], O1b[:, :]
        elif d == 4:
            dst0 = self.acc0[0:65, tq:tq + 4 * 511 + 1:4]
            dst1 = self.acc1[:, tq:tq + 4 * 511 + 1:4]
            src0, src1 = O0b[0:65, :], O1b[:, :]
        else:
            dst0 = self.acc0[0:65, :].rearrange("p (m r) -> p r m", r=16)[:, 4 * tq:4 * tq + 4, :]
            dst1 = self.acc1[:, :].rearrange("p (m r) -> p r m", r=16)[:, 4 * tq:4 * tq + 4, :]
            src0 = O0b[0:65, :].rearrange("p (r m) -> p r m", r=4)
            src1 = O1b[:, :].rearrange("p (r m) -> p r m", r=4)
        if first:
            S.op("act", lambda e: e.activation(out=dst0, in_=src0, func=AF.Copy), reads=[rO0], writes=[self.r_acc])
            S.op("dve", lambda e: e.tensor_copy(out=dst1, in_=src1), reads=[rO1], writes=[self.r_acc])
        else:
            S.op("dve", lambda e: e.tensor_tensor(out=dst0, in0=dst0, in1=src0, op=ALU.add), reads=[rO0, self.r_acc],
                 writes=[self.r_acc])
            S.op("dve", lambda e: e.tensor_tensor(out=dst1, in0=dst1, in1=src1, op=ALU.add), reads=[rO1, self.r_acc],
                 writes=[self.r_acc])

    def finish_B(self, hp):
        for cch in range(4):
            q0 = cch * 512
            self.normalize(self.acc0[:, q0:q0 + 512], self.acc1[:, q0:q0 + 512], self.r_acc, self.r_acc, 512,
                           self.G[:, q0:q0 + 512], 4 + hp, slice(q0, q0 + 512), [self.r_yT[4 + hp][cch]])

    def phase2(self, l):
        S = self.S
        o = self.W0
        mT = self.carve(o, 16384, BF16).rearrange("p (k t) -> p k t", k=8)
        o += 16384
        acc = self.carve(o, 4096, F32)
        o += 4096
        th = [self.carve(o + i * 2048, 2048, F32) for i in range(2)]
        o += 4096
        wos = [self.carve(o + i * 8192, 8192, BF16).rearrange("p (k e) -> p k e", k=8) for i in range(2)]
        o += 16384
        assert o <= self.WORK
        r_mT = [[Res() for _ in range(2)] for _ in range(8)]
        r_acc = [Res() for _ in range(2)]
        r_th = [Res() for _ in range(2)]
        r_wos = [Res(), Res()]
        th_i = 0
        for half in range(2):
            t0 = half * 1024
            for dc in range(8):
                for i in range(3):
                    Wg, rWg = self.win_unit(l, GATE0 + i * 1024 + dc * 128)
                    srcb = self.w_br_d[l, i, :, dc * 128:(dc + 1) * 128].rearrange("(wc p) c -> p wc c", p=128)
                    Wb, rWb = self.wunit([(srcb, 0)], nk=4)
                    for cc in range(2):
                        tok = slice(t0 + cc * 512, t0 + (cc + 1) * 512)
                        cg = (t0 + cc * 512) // 512
                        bg, rbg = self.sbank()
                        for kc in range(8):
                            self.mm(bg[:], Wg[:, kc, :], self.hnT[:, kc, tok], kc == 0, kc == 7,
                                    [rWg, self.r_hnT[cg][kc]], [rbg])
                        bb, rbb = self.sbank()
                        for wc in range(4):
                            self.mm(bb[:], Wb[:, wc, :], self.yT[:, i * 4 + wc, tok], wc == 0, wc == 3,
                                    [rWb, self.r_yT[i * 4 + wc][cg]], [rbb])
                        k = th_i
                        th_i = (k + 1) % 2
                        self.sigmoid(th[k], bg[:], [rbg], r_th[k])
                        av = acc[:, cc * 512:(cc + 1) * 512]
                        if i == 0:
                            S.op("dve", lambda e, bb=bb, k=k, av=av: e.tensor_tensor(out=av, in0=th[k], in1=bb[:], op=ALU.mult),
                                 reads=[rbb, r_th[k]], writes=[r_acc[cc]])
                        else:
                            S.op("dve", lambda e, bb=bb, k=k: e.tensor_tensor(out=th[k], in0=th[k], in1=bb[:], op=ALU.mult),
                                 reads=[rbb, r_th[k]], writes=[r_th[k]])
                            if i == 1:
                                S.op("dve", lambda e, k=k, av=av: e.tensor_tensor(out=av, in0=av, in1=th[k], op=ALU.add),
                                     reads=[r_th[k], r_acc[cc]], writes=[r_acc[cc]])
                            else:
                                mv = mT[:, dc, cc * 512:(cc + 1) * 512]
                                S.op("dve", lambda e, k=k, av=av, mv=mv: e.tensor_tensor(out=mv, in0=av, in1=th[k], op=ALU.add),
                                     reads=[r_th[k], r_acc[cc]], writes=[r_mT[dc][cc]])
            for eh in range(2):
                wo, r_wo = wos[eh], r_wos[eh]
                for ec in range(4):
                    src = self.w_out_d[l, :, eh * 512 + ec * 128: eh * 512 + (ec + 1) * 128].rearrange("(kc p) c -> p kc c", p=128)
                    Wu, rWu = self.wunit([(src, 0)])
                    S.op("pool", lambda e, Wu=Wu, ec=ec, wo=wo: e.tensor_copy(out=wo[:, :, ec * 128:(ec + 1) * 128], in_=Wu[:, :, :]),
                         reads=[rWu], writes=[r_wo])
                for tt in range(8):
                    ti = half * 8 + tt
                    bank, rb = self.sbank()
                    for dc in range(8):
                        self.mm(bank[:], mT[:, dc, tt * 128:(tt + 1) * 128], wo[:, dc, :], dc == 0, dc == 7,
                                [r_mT[dc][tt // 4], r_wo], [rb])
                    xv = self.xres[:, ti, eh * 512:(eh + 1) * 512]
                    S.op("dve", lambda e, bank=bank, xv=xv: e.tensor_tensor(out=xv, in0=xv, in1=bank[:], op=ALU.add),
                         reads=[rb, self.r_x[ti]], writes=[self.r_x[ti]])


_PROG_CACHE = {}


def _get_prog(NL, debug=None):
    key = (NL, debug)
    if key not in _PROG_CACHE:
        p = Prog(NL, debug)
        p.build()
        _PROG_CACHE[key] = p
    return _PROG_CACHE[key]


FUSED = True


def kernel(x, ln_g, w_in, qk_g, sinks, w_branch, w_out, rel_bias):
    x = np.ascontiguousarray(np.asarray(x, np.float32))
    ln_g = np.asarray(ln_g, np.float32)
    w_in = np.ascontiguousarray(np.asarray(w_in, np.float32))
    qk_g = np.asarray(qk_g, np.float32)
    sinks = np.asarray(sinks, np.float32)
    w_branch = np.ascontiguousarray(np.asarray(w_branch, np.float32))
    w_out = np.ascontiguousarray(np.asarray(w_out, np.float32))
    tabA, tabB, tabC = _tables(rel_bias)
    cb, cf = _consts()
    depth = w_in.shape[0]
    ncores = x.shape[0]
    groups = [list(range(depth))] if FUSED else [[l] for l in range(depth)]
    cur = [x[b] for b in range(ncores)]
    for ls in groups:
        prog = _get_prog(len(ls))
        sl = slice(ls[0], ls[-1] + 1)
        common = {"w_in": w_in[sl], "w_branch": w_branch[sl], "w_out": w_out[sl], "tabA": tabA, "tabB": tabB,
                  "tabC": tabC, "constb": cb, "constf": cf, "smallf": _smallf(ln_g[sl], qk_g[sl], sinks[sl])}
        in_maps = [dict(common, x=cur[b]) for b in range(ncores)]
        res = run_bass_kernel_spmd(prog.nc, in_maps, core_ids=list(range(ncores)))
        cur = [np.asarray(res.results[b]["y"], np.float32) for b in range(ncores)]
    return np.stack(cur, axis=0)
```
